# Optimizing a Trainium2 kernel written in Bass

```python
import math, functools
import jax, jax.numpy as jnp
from jax import lax
import numpy as np

D_MODEL = 1024
BATCH = 8
SEQ = 4096
DEPTH = 2
DEC_BATCH = 32
DEC_SEQ = 4
PAST_LEN = 16384
PAGE_SIZE = 128

N_GROUPS = 4
GROUP_W = D_MODEL // N_GROUPS
MIX_W = N_GROUPS * GROUP_W
N_IN_BLOCKS = 12
IN_W = N_IN_BLOCKS * GROUP_W
SSM_CH = 16
SSM_GROUPS = GROUP_W // SSM_CH
SSM_STATE = 64
CONV_K = 3
HEAD_DIM = 64
SB_HEADS = GROUP_W // HEAD_DIM
SB_BIAS_INIT = -6.0
MEM_HEADS = 4
MEM_HEAD_DIM = GROUP_W // MEM_HEADS
N_MEM = 256
Q_BLOCK = 128
EPS = 1e-6
DT_MIN = 1e-3
DT_MAX = 1e-1

kernel_name = "hymba_s5_conv_stickbreak_mem_step"


def rmsnorm(x, g):
    xf = x.astype(jnp.float32)
    y = xf * lax.rsqrt(jnp.mean(xf * xf, axis=-1, keepdims=True) + EPS)
    return (y * g.astype(jnp.float32)).astype(x.dtype)


def _ssm_combine(e1, e2):
    a1, b1 = e1
    a2, b2 = e2
    return a1 * a2, a2 * b1 + b2


def ssm_branch(u, h0_re, h0_im, lam_re, lam_im, b_re, b_im, c_re, c_im, log_dt, d, w_glu):
    f32 = jnp.float32
    n, l, _ = u.shape
    uf = u.astype(f32)
    lam = lax.complex(lam_re.astype(f32), lam_im.astype(f32))
    dt = jnp.exp(log_dt.astype(f32))[:, None]
    lam_bar = jnp.exp(lam * dt)
    b = lax.complex(b_re.astype(f32), b_im.astype(f32))
    b_bar = ((lam_bar - 1.0) / lam)[..., None] * b
    c = lax.complex(c_re.astype(f32), c_im.astype(f32))
    ug = uf.reshape(n, l, SSM_GROUPS, SSM_CH).astype(jnp.complex64)
    bu = jnp.einsum("nlgc,gpc->nlgp", ug, b_bar)
    h0 = lax.complex(h0_re.astype(f32), h0_im.astype(f32))
    bu = bu.at[:, 0].add(lam_bar * h0)
    a = jnp.broadcast_to(lam_bar, bu.shape)
    _, h = lax.associative_scan(_ssm_combine, (a, bu), axis=1)
    y = jnp.einsum("nlgp,gcp->nlgc", h, c).real.reshape(n, l, GROUP_W) + d.astype(f32) * uf
    y = jax.nn.gelu(y)
    y = y * jax.nn.sigmoid(y @ w_glu.astype(f32))
    h_last = h[:, -1]
    return y.astype(u.dtype), jnp.real(h_last), jnp.imag(h_last)


def short_conv(b_gate, c_gate, x_in, buf, w):
    v = c_gate * x_in
    vp = jnp.concatenate([buf.astype(v.dtype), v], axis=1)
    l = v.shape[1]
    y = vp[:, 0:l] * w[0]
    for j in range(1, CONV_K):
        y = y + vp[:, j:j + l] * w[j]
    return b_gate * y, vp[:, -(CONV_K - 1):]


def stick_breaking(q, k, v, bias, q_pos, k_pos):
    f32 = jnp.float32
    z = jnp.einsum("nqhd,nkhd->nhqk", q.astype(f32), k.astype(f32)) * (HEAD_DIM ** -0.5)
    z = z + bias.astype(f32)[None, :, None, None]
    mask = k_pos[None, :] < q_pos[:, None]
    log_beta = jax.nn.log_sigmoid(z)
    log_1m = jnp.where(mask, jax.nn.log_sigmoid(-z), 0.0)
    rev = lax.cumsum(log_1m, axis=3, reverse=True)
    excl = jnp.concatenate([rev[..., 1:], jnp.zeros_like(rev[..., :1])], axis=-1)
    wts = jnp.where(mask, jnp.exp(log_beta + excl), 0.0)
    return jnp.einsum("nhqk,nkhd->nqhd", wts, v.astype(f32)).astype(q.dtype)


def sb_prompt(q, k, v, bias):
    n, l, h, dh = q.shape
    nb = l // Q_BLOCK
    qb = q.reshape(n, nb, Q_BLOCK, h, dh).transpose(1, 0, 2, 3, 4)
    k_pos = jnp.arange(l, dtype=jnp.int32)

    def block(args):
        qi, bi = args
        q_pos = bi * Q_BLOCK + jnp.arange(Q_BLOCK, dtype=jnp.int32)
        return stick_breaking(qi, k, v, bias, q_pos, k_pos)

    out = lax.map(block, (qb, jnp.arange(nb, dtype=jnp.int32)))
    return out.transpose(1, 0, 2, 3, 4).reshape(n, l, h, dh)


def sb_sample(q, k, v, bias, k_past, v_past):
    past = k_past.shape[1]
    k_all = jnp.concatenate([k_past.astype(k.dtype), k], axis=1)
    v_all = jnp.concatenate([v_past.astype(v.dtype), v], axis=1)
    q_pos = past + jnp.arange(q.shape[1], dtype=jnp.int32)
    k_pos = jnp.arange(k_all.shape[1], dtype=jnp.int32)
    return stick_breaking(q, k_all, v_all, bias, q_pos, k_pos)


def mem_kv(mem, w):
    n, m, _ = mem.shape
    mk, mv = jnp.split(mem @ w, 2, axis=-1)
    return (mk.reshape(n, m, MEM_HEADS, MEM_HEAD_DIM), mv.reshape(n, m, MEM_HEADS, MEM_HEAD_DIM))


def mem_attend(q, mk, mv):
    f32 = jnp.float32
    s = jnp.einsum("nlhd,nmhd->nhlm", q.astype(f32), mk.astype(f32)) * (MEM_HEAD_DIM ** -0.5)
    p = jax.nn.softmax(s, axis=-1)
    return jnp.einsum("nhlm,nmhd->nlhd", p, mv.astype(f32)).astype(q.dtype)


def mixer_layer(x, lp, h0_re, h0_im, conv_buf, attend_fn, mk, mv):
    n, l, _ = x.shape
    h = rmsnorm(x, lp["norm_g"]) @ lp["w_in"]
    (a_u, a_g, b_b, b_c, b_x, b_g, c_q, c_k, c_v, c_g, m_q, m_g) = jnp.split(h, N_IN_BLOCKS, axis=-1)
    y_a, h_re, h_im = ssm_branch(a_u, h0_re, h0_im, lp["lam_re"], lp["lam_im"], lp["b_re"], lp["b_im"],
                                 lp["c_re"], lp["c_im"], lp["log_dt"], lp["d"], lp["w_glu"])
    y_b, new_buf = short_conv(b_b, b_c, b_x, conv_buf, lp["conv_w"])
    q = c_q.reshape(n, l, SB_HEADS, HEAD_DIM)
    k = c_k.reshape(n, l, SB_HEADS, HEAD_DIM)
    v = c_v.reshape(n, l, SB_HEADS, HEAD_DIM)
    y_c = attend_fn(q, k, v, lp["sb_bias"]).reshape(n, l, GROUP_W)
    y_m = mem_attend(m_q.reshape(n, l, MEM_HEADS, MEM_HEAD_DIM), mk, mv).reshape(n, l, GROUP_W)
    gn = lp["group_norm_g"]
    merged = jnp.concatenate([
        rmsnorm(y_a, gn[0]) * jax.nn.silu(a_g),
        rmsnorm(y_b, gn[1]) * jax.nn.silu(b_g),
        rmsnorm(y_c, gn[2]) * jax.nn.silu(c_g),
        rmsnorm(y_m, gn[3]) * jax.nn.silu(m_g)], axis=-1)
    return x + merged @ lp["w_out"], h_re, h_im, new_buf, k, v


def setup_inputs(seed: int = 0) -> dict:
    key = jax.random.key(seed)
    ks = list(jax.random.split(key, 32))
    f32 = jnp.float32

    def nrm(i, shape, scale):
        return jax.random.normal(ks[i], shape, f32) * scale

    n_pages = PAST_LEN // PAGE_SIZE
    n_used = DEC_BATCH * n_pages
    n_phys = n_used + n_used // 4
    perm = jax.random.permutation(ks[0], n_phys)
    page_table = perm[:n_used].reshape(DEC_BATCH, n_pages).astype(jnp.int32)

    x_prompt = nrm(1, (BATCH, SEQ, D_MODEL), 1.0)
    x_sample = nrm(2, (DEC_BATCH, DEC_SEQ, D_MODEL), 1.0)
    cache_sb_k = nrm(3, (n_phys, DEPTH, PAGE_SIZE, SB_HEADS, HEAD_DIM), 1.0)
    cache_sb_v = nrm(4, (n_phys, DEPTH, PAGE_SIZE, SB_HEADS, HEAD_DIM), 1.0)
    state_ssm_re = nrm(5, (DEC_BATCH, DEPTH, SSM_GROUPS, SSM_STATE), 0.5)
    state_ssm_im = nrm(6, (DEC_BATCH, DEPTH, SSM_GROUPS, SSM_STATE), 0.5)
    state_conv = nrm(7, (DEC_BATCH, DEPTH, CONV_K - 1, GROUP_W), 1.0)
    cache_mem_k = nrm(8, (DEC_BATCH, DEPTH, N_MEM, MEM_HEADS, MEM_HEAD_DIM), 1.0)
    cache_mem_v = nrm(9, (DEC_BATCH, DEPTH, N_MEM, MEM_HEADS, MEM_HEAD_DIM), 1.0)
    mem_prompt = nrm(10, (BATCH, N_MEM, D_MODEL), 1.0)

    norm_g = 1.0 + nrm(11, (DEPTH, D_MODEL), 0.02)
    w_in = nrm(12, (DEPTH, D_MODEL, IN_W), D_MODEL ** -0.5)
    w_out = nrm(13, (DEPTH, MIX_W, D_MODEL), MIX_W ** -0.5)
    group_norm_g = 1.0 + nrm(14, (DEPTH, N_GROUPS, GROUP_W), 0.02)
    ssm_lambda_re = -0.5 + nrm(15, (DEPTH, SSM_GROUPS, SSM_STATE), 0.01)
    ssm_lambda_im = math.pi * jnp.arange(SSM_STATE, dtype=f32) + nrm(16, (DEPTH, SSM_GROUPS, SSM_STATE), 0.01)
    ssm_b_re = nrm(17, (DEPTH, SSM_GROUPS, SSM_STATE, SSM_CH), (2 * SSM_CH) ** -0.5)
    ssm_b_im = nrm(18, (DEPTH, SSM_GROUPS, SSM_STATE, SSM_CH), (2 * SSM_CH) ** -0.5)
    ssm_c_re = nrm(19, (DEPTH, SSM_GROUPS, SSM_CH, SSM_STATE), (2 * SSM_STATE) ** -0.5)
    ssm_c_im = nrm(20, (DEPTH, SSM_GROUPS, SSM_CH, SSM_STATE), (2 * SSM_STATE) ** -0.5)
    ssm_log_dt = jax.random.uniform(ks[21], (DEPTH, SSM_GROUPS), f32, math.log(DT_MIN), math.log(DT_MAX))
    ssm_d = nrm(22, (DEPTH, GROUP_W), 1.0)
    ssm_w_glu = nrm(23, (DEPTH, GROUP_W, GROUP_W), GROUP_W ** -0.5)
    conv_w = nrm(24, (DEPTH, CONV_K, GROUP_W), CONV_K ** -0.5)
    w_mem_kv = nrm(25, (DEPTH, D_MODEL, 2 * GROUP_W), D_MODEL ** -0.5)
    final_norm_g = 1.0 + nrm(26, (D_MODEL,), 0.02)
    sb_bias = SB_BIAS_INIT + nrm(27, (DEPTH, SB_HEADS), 0.1)
    return {
        "x_prompt": x_prompt, "x_sample": x_sample,
        "cache_sb_k": cache_sb_k, "cache_sb_v": cache_sb_v,
        "state_ssm_re": state_ssm_re, "state_ssm_im": state_ssm_im, "state_conv": state_conv,
        "cache_mem_k": cache_mem_k, "cache_mem_v": cache_mem_v,
        "page_table": page_table, "mem_prompt": mem_prompt,
        "norm_g": norm_g, "w_in": w_in, "w_out": w_out, "group_norm_g": group_norm_g,
        "ssm_lambda_re": ssm_lambda_re, "ssm_lambda_im": ssm_lambda_im,
        "ssm_b_re": ssm_b_re, "ssm_b_im": ssm_b_im, "ssm_c_re": ssm_c_re, "ssm_c_im": ssm_c_im,
        "ssm_log_dt": ssm_log_dt, "ssm_d": ssm_d, "ssm_w_glu": ssm_w_glu,
        "conv_w": conv_w, "sb_bias": sb_bias, "w_mem_kv": w_mem_kv, "final_norm_g": final_norm_g,
    }


def reference(x_prompt, x_sample, cache_sb_k, cache_sb_v, state_ssm_re, state_ssm_im, state_conv,
              cache_mem_k, cache_mem_v, page_table, mem_prompt, norm_g, w_in, w_out, group_norm_g,
              ssm_lambda_re, ssm_lambda_im, ssm_b_re, ssm_b_im, ssm_c_re, ssm_c_im, ssm_log_dt, ssm_d,
              ssm_w_glu, conv_w, sb_bias, w_mem_kv, final_norm_g):
    n_p = x_prompt.shape[0]
    n_s = x_sample.shape[0]
    past_len = page_table.shape[1] * cache_sb_k.shape[2]
    xp, xs = x_prompt, x_sample
    zeros_h = jnp.zeros((n_p, SSM_GROUPS, SSM_STATE), jnp.float32)
    zeros_buf = jnp.zeros((n_p, CONV_K - 1, GROUP_W), x_prompt.dtype)
    p_k, p_v, p_re, p_im, p_conv, p_mk, p_mv = [], [], [], [], [], [], []
    s_k, s_v, s_re, s_im, s_conv = [], [], [], [], []
    for i in range(DEPTH):
        lp = {"norm_g": norm_g[i], "w_in": w_in[i], "w_out": w_out[i], "group_norm_g": group_norm_g[i],
              "lam_re": ssm_lambda_re[i], "lam_im": ssm_lambda_im[i], "b_re": ssm_b_re[i],
              "b_im": ssm_b_im[i], "c_re": ssm_c_re[i], "c_im": ssm_c_im[i], "log_dt": ssm_log_dt[i],
              "d": ssm_d[i], "w_glu": ssm_w_glu[i], "conv_w": conv_w[i], "sb_bias": sb_bias[i]}
        mk, mv = mem_kv(mem_prompt, w_mem_kv[i])
        xp, h_re, h_im, buf, k, v = mixer_layer(xp, lp, zeros_h, zeros_h, zeros_buf, sb_prompt, mk, mv)
        p_k.append(k); p_v.append(v); p_re.append(h_re); p_im.append(h_im)
        p_conv.append(buf); p_mk.append(mk); p_mv.append(mv)
        k_past = cache_sb_k[page_table, i].reshape(n_s, past_len, SB_HEADS, HEAD_DIM)
        v_past = cache_sb_v[page_table, i].reshape(n_s, past_len, SB_HEADS, HEAD_DIM)
        attend = functools.partial(sb_sample, k_past=k_past, v_past=v_past)
        xs, h_re, h_im, buf, k, v = mixer_layer(xs, lp, state_ssm_re[:, i], state_ssm_im[:, i],
                                                state_conv[:, i], attend,
                                                cache_mem_k[:, i], cache_mem_v[:, i])
        s_k.append(k); s_v.append(v); s_re.append(h_re); s_im.append(h_im); s_conv.append(buf)
    y_prompt = rmsnorm(xp, final_norm_g)
    y_sample = rmsnorm(xs, final_norm_g)
    return (y_prompt, y_sample,
            jnp.stack(p_k, axis=1), jnp.stack(p_v, axis=1),
            jnp.stack(p_re, axis=1), jnp.stack(p_im, axis=1), jnp.stack(p_conv, axis=1),
            jnp.stack(p_mk, axis=1), jnp.stack(p_mv, axis=1),
            jnp.stack(s_k, axis=1), jnp.stack(s_v, axis=1),
            jnp.stack(s_re, axis=1), jnp.stack(s_im, axis=1), jnp.stack(s_conv, axis=1))
```

```python
from contextlib import ExitStack
import math
import os
import numpy as np
import concourse.bass as bass
import concourse.mybir as mybir
from concourse.bass_utils import run_bass_kernel_spmd

F32 = mybir.dt.float32
BF16 = mybir.dt.bfloat16
I32 = mybir.dt.int32
AF = mybir.ActivationFunctionType
ALU = mybir.AluOpType
AX = mybir.AxisListType

D = 1024
DEPTH = 2
TT = 128
NS = TT // 128
EPS = 1e-6
TWO_PI = 2.0 * math.pi


class Buf:
    __slots__ = ("lw", "rd")

    def __init__(self):
        self.lw = None
        self.rd = {}


class Prog:
    EPOCH = 12000

    def __init__(self, nc, stack):
        self.nc = nc
        self.stack = stack
        self.engs = {"pe": nc.tensor, "dve": nc.vector, "act": nc.scalar, "pool": nc.gpsimd, "sp": nc.sync}
        self.esem, self.ecnt = {}, {}
        self.seen = {e: {} for e in self.engs}
        self.nsem = 0
        for e in self.engs:
            self._new_esem(e)
        self.dsems, self.dcnt = {}, {}
        self.out_tks = []

    def _mksem(self, name):
        self.nsem += 1
        return self.stack.enter_context(self.nc.semaphore(f"{name}_{self.nsem}"))

    def _new_esem(self, e):
        self.esem[e] = self._mksem("e" + e)
        self.ecnt[e] = 0

    def sbuf(self, name, shape, dt):
        return self.stack.enter_context(self.nc.sbuf_tensor(name, shape, dt))

    def psum(self, name, shape, dt):
        return self.stack.enter_context(self.nc.psum_tensor(name, shape, dt))

    def _wait(self, eng, deps):
        E = self.engs[eng]
        seen = self.seen[eng]
        for (sem, val) in deps:
            k = id(sem)
            if seen.get(k, 0) < val:
                E.wait_ge(sem, val)
                seen[k] = val

    def _deps(self, eng, reads, writes, is_dma):
        deps = set()
        own = None if is_dma else id(self.esem[eng])
        for b in reads:
            if b.lw is not None and not (eng == "pe" and not is_dma and id(b.lw[0]) == own):
                deps.add(b.lw)
        for b in writes:
            if b.lw is not None and id(b.lw[0]) != own:
                deps.add(b.lw)
            for sem_id, tk in b.rd.items():
                if sem_id != own:
                    deps.add(tk)
        return deps

    def _record(self, tk, reads, writes):
        k = id(tk[0])
        for b in reads:
            if b.rd.get(k, (None, 0))[1] < tk[1]:
                b.rd[k] = tk
        for b in writes:
            b.lw = tk
            b.rd = {}

    def op(self, eng, fn, reads=(), writes=()):
        self._wait(eng, self._deps(eng, reads, writes, False))
        inst = fn(self.engs[eng])
        if self.ecnt[eng] >= self.EPOCH:
            self._new_esem(eng)
        self.ecnt[eng] += 1
        tk = (self.esem[eng], self.ecnt[eng])
        inst.then_inc(tk[0], 1)
        self._record(tk, reads, writes)
        return tk

    def dma(self, q, fn, reads=(), writes=(), nsem=8, is_out=False):
        self._wait(q, self._deps(q, reads, writes, True))
        if q not in self.dsems:
            self.dsems[q] = [self._mksem("d" + q) for _ in range(nsem)]
            self.dcnt[q] = 0
        i = self.dcnt[q]
        self.dcnt[q] += 1
        sems = self.dsems[q]
        sem = sems[i % len(sems)]
        tk = (sem, 16 * (i // len(sems) + 1))
        fn(self.engs[q]).then_inc(sem, 16)
        self._record(tk, reads, writes)
        if is_out:
            self.out_tks.append(tk)
        return tk


class T:
    def __init__(self, t, nb=1):
        self.t = t
        self.bs = [Buf() for _ in range(nb)]
        self.b = self.bs[0]


class _Stop(Exception):
    pass


def build(nc, L, NPG, NPH, STOP=999, DO_SAMPLE=True):
    def stage(n):
        if n >= STOP:
            raise _Stop()

    NT = L // TT
    NSUB = L // 128

    def din(name, shape, dt=F32):
        return nc.dram_tensor(name, shape, dt, kind="ExternalInput").ap()

    def dout(name, shape, dt=F32):
        return nc.dram_tensor(name, shape, dt, kind="ExternalOutput").ap()

    xp = din("xp", [L, D])
    memp = din("memp", [256, D])
    norm_g = din("norm_g", [DEPTH, D])
    w_in = din("w_in", [DEPTH, D, 3072])
    w_out = din("w_out", [DEPTH, D, D])
    gng = din("group_norm_g", [DEPTH, 4, 256])
    lam_re = din("ssm_lambda_re", [DEPTH, 16, 64])
    lam_im = din("ssm_lambda_im", [DEPTH, 16, 64])
    b_re = din("ssm_b_re", [DEPTH, 16, 64, 16])
    b_im = din("ssm_b_im", [DEPTH, 16, 64, 16])
    c_re = din("ssm_c_re", [DEPTH, 16, 16, 64])
    c_im = din("ssm_c_im", [DEPTH, 16, 16, 64])
    log_dt = din("ssm_log_dt", [DEPTH, 16])
    ssm_d = din("ssm_d", [DEPTH, 256])
    w_glu = din("ssm_w_glu", [DEPTH, 256, 256])
    conv_w = din("conv_w", [DEPTH, 3, 256])
    sb_bias = din("sb_bias", [DEPTH, 4])
    w_mem = din("w_mem_kv", [DEPTH, D, 512])
    fin_g = din("final_norm_g", [D])

    y_p = dout("y_p", [L, D])
    sbk_p = dout("sbk_p", [DEPTH, L, 256])
    sbv_p = dout("sbv_p", [DEPTH, L, 256])
    ssmre_p = dout("ssmre_p", [DEPTH, 16, 64])
    ssmim_p = dout("ssmim_p", [DEPTH, 16, 64])
    conv_p = dout("conv_p", [DEPTH, 2, 256])
    memk_p = dout("memk_p", [DEPTH, 256, 256])
    memv_p = dout("memv_p", [DEPTH, 256, 256])
    x1 = nc.dram_tensor("x1_scratch", [L, D], F32, kind="Internal").ap()
    xs = din("xs", [16, D])
    ck = din("ck", [NPH * 2 * 128, 256])
    cv = din("cv", [NPH * 2 * 128, 256])
    sre = din("sre", [4, DEPTH, 16, 64])
    sim = din("sim", [4, DEPTH, 16, 64])
    sconv = din("sconv", [4, DEPTH, 2, 256])
    cmk = din("cmk", [4, DEPTH, 256, 256])
    cmv = din("cmv", [4, DEPTH, 256, 256])
    ptab = din("ptab", [4 * NPG], I32)
    y_s = dout("y_s", [16, D])
    sbk_s = dout("sbk_s", [4, DEPTH, 4, 256])
    sbv_s = dout("sbv_s", [4, DEPTH, 4, 256])
    ssmre_s = dout("ssmre_s", [4, DEPTH, 16, 64])
    ssmim_s = dout("ssmim_s", [4, DEPTH, 16, 64])
    conv_s = dout("conv_s", [4, DEPTH, 2, 256])

    with ExitStack() as st:
        p = Prog(nc, st)

        def sb(name, shape, dt=F32, nb=1):
            return T(p.sbuf(name, shape, dt), nb)

        def ps(name, shape, dt=F32):
            return T(p.psum(name, shape, dt))

        x1bufs = [Buf() for _ in range(NT)]

        def act(out, in_, func, R, W, bias=None, scale=None):
            kw = {}
            if bias is not None:
                kw["bias"] = bias
            if scale is not None:
                kw["scale"] = scale
            return p.op("act", lambda e: e.activation(out=out, in_=in_, func=func, **kw), R, W)

        def tt(eng, out, a, b, op, R, W):
            return p.op(eng, lambda e: e.tensor_tensor(out=out, in0=a, in1=b, op=op), R, W)

        def ts(eng, out, a, s1, op0, R, W, s2=None, op1=None):
            if op1 is None:
                return p.op(eng, lambda e: e.tensor_scalar(out=out, in0=a, scalar1=s1, scalar2=None, op0=op0), R, W)
            return p.op(eng, lambda e: e.tensor_scalar(out=out, in0=a, scalar1=s1, scalar2=s2, op0=op0, op1=op1), R, W)

        def stt(out, a, s, b, op0, op1, R, W):
            return p.op("dve", lambda e: e.scalar_tensor_tensor(out=out, in0=a, scalar=s, in1=b, op0=op0, op1=op1), R, W)

        def cp(eng, out, in_, R, W):
            if eng == "act":
                return act(out, in_, AF.Copy, R, W)
            return p.op(eng, lambda e: e.tensor_copy(out=out, in_=in_), R, W)

        def mset(eng, ap, v, W):
            return p.op(eng, lambda e: e.memset(ap, v), (), W)

        def mm(out, lhsT, rhs, start, stop, R, W):
            return p.op("pe", lambda e: e.matmul(out, lhsT=lhsT, rhs=rhs, start=start, stop=stop), R, W)

        def tr(out, in_, ident, R, W):
            return p.op("pe", lambda e: e.transpose(out=out, in_=in_, identity=ident), R, W)

        def dma(q, out, in_, R, W, slow=False, is_out=False):
            if slow:
                return p.dma(q, lambda e: e.dma_start(out=out, in_=in_, allow_slow_non_contiguous=True), R, W, is_out=is_out)
            return p.dma(q, lambda e: e.dma_start(out=out, in_=in_), R, W, is_out=is_out)

        def scan(out, d0, d1, init, R, W):
            return p.op("dve", lambda e: e.tensor_tensor_scan(out=out, data0=d0, data1=d1, initial=init,
                                                               op0=ALU.mult, op1=ALU.add), R, W)

        identf = sb("identf", [128, 128])
        identb = sb("identb", [128, 128], BF16)
        onesf = sb("onesf", [128, 128])
        epsT = sb("epsT", [128, 1])
        onec = sb("onec", [128, 1])
        mset("pool", identf.t[:], 1.0, [identf.b])
        p.op("pool", lambda e: e.affine_select(out=identf.t[:], in_=identf.t[:], pattern=[[-1, 128]],
                                               compare_op=ALU.is_equal, fill=0.0, base=0, channel_multiplier=1),
             [identf.b], [identf.b])
        cp("dve", identb.t[:], identf.t[:], [identf.b], [identb.b])
        mset("dve", onesf.t[:], 1.0, [onesf.b])
        mset("dve", epsT.t[:], EPS, [epsT.b])
        mset("dve", onec.t[:], 1.0, [onec.b])

        win = sb("win", [128, 8, 3072], BF16, 48)

        def wb(kc, c0, c1):
            return [win.bs[kc * 6 + c] for c in range(c0 // 512, (c1 - 1) // 512 + 1)]

        wout = sb("wout", [128, 8, 1024], BF16, 16)
        wglu = sb("wglu", [128, 2, 256], BF16)
        gt = sb("gt", [128, 1024])
        gnAB = sb("gnAB", [128, 4])
        gnCM = sb("gnCM", [128, 2, 256])
        dvec = sb("dvec", [128, 2])
        cw = sb("cw", [128, 2, 3])
        sbb = sb("sbb", [128, 4])
        KcT = sb("KcT", [128, 2, L], BF16, NSUB)
        Vc = sb("Vc", [128, NSUB, 256], BF16, NSUB)
        cosT = sb("cosT", [128, 8, TT])
        sinT = sb("sinT", [128, 8, TT])
        rsp = sb("rsp", [128, 8])
        BtR = sb("BtR", [128, 8, 128], BF16)
        BtI = sb("BtI", [128, 8, 128], BF16)
        CtR = sb("CtR", [128, 8, 128], BF16)
        CtI = sb("CtI", [128, 8, 128], BF16)
        memT = sb("memT", [128, 8, 256], BF16)
        mkT = sb("mkT", [128, 2, 256], BF16)
        mvA = sb("mvA", [128, 2, 4, 65], BF16)
        hst = sb("hst", [128, 8, 2])
        xt = sb("xt", [128, NS, 1024])
        xnT = sb("xnT", [128, 8, TT], BF16)
        mergedT = sb("mergedT", [128, 8, TT], BF16, 8)
        au_f = sb("au_f", [128, 2, TT])
        au_b = sb("au_b", [128, 2, TT], BF16)
        sgA = sb("sgA", [128, 2, TT])
        sgB = sb("sgB", [128, 2, TT])
        bbx = sb("bbx", [128, 6, TT])
        qT = sb("qT", [128, 2, TT], BF16)
        mqT = sb("mqT", [128, 2, TT], BF16)
        sgC = sb("sgC", [128, NS, 256])
        sgM = sb("sgM", [128, NS, 256])
        vbuf = sb("vbuf", [128, 2, TT + 2])
        wk = [sb(f"wk{i}", [128, 1024]) for i in range(8)]
        wkb = [sb(f"wkb{i}", [128, 1024], BF16) for i in range(4)]
        sm = [sb(f"sm{i}", [128, 16]) for i in range(6)]
        pmm = [ps(f"pmm{i}", [128, 512]) for i in range(2)]
        ptr = T(p.psum("ptr", [128, 1024], BF16), 2)
        pss = [ps(f"pss{i}", [128, 512]) for i in range(2)]
        po = ps("po", [128, 512])
        po2 = ps("po2", [128, 512])
        py = ps("py", [128, 2, 256])

        def rstd_of(out, in_, n, R, W):
            act(out, in_, AF.Ln, R + [epsT.b], W, bias=epsT.t[:in_.shape[0], 0:1], scale=1.0 / n)
            act(out, out, AF.Exp, W, W, scale=-0.5)

        def sincos(ang, angb, N, sin_out, sin_b, cos_out, cos_b, tmpf, tmpi):
            for (o_, ob, shift) in ((sin_out, sin_b, 0.0), (cos_out, cos_b, 0.25)):
                for c0 in range(0, N, 512):
                    n_ = min(512, N - c0)
                    o = o_[:, c0:c0 + n_]
                    ts("dve", tmpf.t[:, 0:n_], ang[:, c0:c0 + n_], 1.0 / TWO_PI, ALU.mult, [angb], [tmpf.b], s2=shift, op1=ALU.add)
                    cp("dve", tmpi.t[:, 0:n_], tmpf.t[:, 0:n_], [tmpf.b], [tmpi.b])
                    cp("dve", o, tmpi.t[:, 0:n_], [tmpi.b], [ob])
                    tt("dve", tmpf.t[:, 0:n_], tmpf.t[:, 0:n_], o, ALU.subtract, [tmpf.b, ob], [tmpf.b])
                    act(o, tmpf.t[:, 0:n_], AF.Sin, [tmpf.b], [ob], scale=TWO_PI * (1.0 - 1e-6))

        tmpi = sb("tmpi", [128, 512], I32)

        def prep_mem():
            for mc in range(2):
                dma("sp", wk[0].t[:, :], memp[mc * 128:(mc + 1) * 128, :], [], [wk[0].b])
                cp("dve", wkb[0].t[:, :], wk[0].t[:, :], [wk[0].b], [wkb[0].b])
                for kc in range(8):
                    tr(ptr.t[:, kc * 128:(kc + 1) * 128], wkb[0].t[:, kc * 128:(kc + 1) * 128], identb.t[:],
                       [wkb[0].b, identb.b], ptr.bs)
                cp("act", memT.t[:, :, mc * 128:(mc + 1) * 128],
                   ptr.t[:, :].rearrange("p (k m) -> p k m", k=8), ptr.bs, [memT.b])


        NQ = 16
        xs_t = sb("xs_t", [NQ, 1024])
        idxT = sb("idxT", [128, 4 * NPG], I32)
        iot = sb("iot", [128, 1], I32)
        mask16 = sb("mask16", [NQ, 4])
        sbias16 = sb("sbias16", [NQ, 1])
        s_xnT = sb("s_xnT", [128, 8, NQ], BF16)
        s_auf = sb("s_auf", [128, 2, NQ])
        s_aub = sb("s_aub", [128, 2, NQ], BF16)
        s_sgA = sb("s_sgA", [128, 2, NQ])
        s_sgB = sb("s_sgB", [128, 2, NQ])
        s_bbx = sb("s_bbx", [128, 6, NQ])
        s_qT = sb("s_qT", [128, 2, NQ], BF16)
        s_kT = sb("s_kT", [128, 2, NQ], BF16)
        s_mqT = sb("s_mqT", [128, 2, NQ], BF16)
        s_sgC = sb("s_sgC", [NQ, 256])
        s_sgM = sb("s_sgM", [NQ, 256])
        s_vbn = sb("s_vbn", [4, 4, 256], BF16)
        s_vbuf = sb("s_vbuf", [128, 2, 4, 6])
        s_hst = sb("s_hst", [128, 8, 4, 2])
        s_mT = sb("s_mT", [128, 8, NQ], BF16)
        s_yc = sb("s_yc", [NQ, 256])
        s_ym = sb("s_ym", [NQ, 256])
        qbd = sb("qbd", [128, 2, NQ], BF16)
        s_ncar = sb("s_ncar", [NQ, 1])
        Kb = sb("Kb", [128, 2, 4, 256], BF16, 2)
        Vb = sb("Vb", [128, 1, 4, 256], BF16, 1)

        def colv(ap1d):
            return ap1d.rearrange("(p o) -> p o", o=1)

        def v4(ap):
            return ap.rearrange("p (n t) -> p n t", n=4)

        def sample_setup():
            dma("sp", xs_t.t[:, :], xs, [], [xs_t.b])
            dma("sp", idxT.t[:, :], ptab.partition_broadcast(128), [], [idxT.b])
            pi = sm[5]
            pI = tmpi
            p.op("pool", lambda e: e.iota(pI.t[:NQ, 0:1], pattern=[[0, 1]], base=0, channel_multiplier=1), [], [pI.b])
            p.op("dve", lambda e: e.tensor_single_scalar(out=pI.t[:NQ, 0:1], in_=pI.t[:NQ, 0:1], scalar=3, op=ALU.bitwise_and),
                 [pI.b], [pI.b])
            cp("dve", pi.t[:NQ, 0:1], pI.t[:NQ, 0:1], [pI.b], [pi.b])
            p.op("pool", lambda e: e.iota(pI.t[:NQ, 8:12], pattern=[[1, 4]], base=0, channel_multiplier=0), [pI.b], [pI.b])
            cp("dve", pi.t[:NQ, 4:8], pI.t[:NQ, 8:12], [pI.b], [pi.b])
            ts("dve", mask16.t[:, :], pi.t[:NQ, 4:8], pi.t[:NQ, 0:1], ALU.is_lt, [pi.b], [mask16.b])

        def s_merge_fm(br, y_, sg):
            sq = wk[7]
            pm = pmm[0]
            for hf in range(2):
                tt("dve", sq.t[:, hf * NQ:(hf + 1) * NQ], y_.t[:, hf * NQ:(hf + 1) * NQ], y_.t[:, hf * NQ:(hf + 1) * NQ], ALU.mult,
                   [y_.b], [sq.b])
            for hf in range(2):
                mm(pm.t[:, 0:NQ], onesf.t[:, :], sq.t[:, hf * NQ:(hf + 1) * NQ], hf == 0, hf == 1, [onesf.b, sq.b], [pm.b])
            rs = wk[6]
            rstd_of(rs.t[:, 0:NQ], pm.t[:, 0:NQ], 256.0, [pm.b], [rs.b])
            for hf in range(2):
                stt(sq.t[:, hf * NQ:(hf + 1) * NQ], y_.t[:, hf * NQ:(hf + 1) * NQ], gnAB.t[:, br * 2 + hf:br * 2 + hf + 1],
                    rs.t[:, 0:NQ], ALU.mult, ALU.mult, [y_.b, gnAB.b, rs.b], [sq.b])
                tt("dve", s_mT.t[:, br * 2 + hf, :], sq.t[:, hf * NQ:(hf + 1) * NQ], sg.t[:, hf, :], ALU.mult, [sq.b, sg.b], [s_mT.b])

        def s_merge_tm(idx, y_, sg, mtm):
            sq = wk[7]
            ssq = sm[1]
            tt("dve", sq.t[:NQ, 0:256], y_.t[:, :], y_.t[:, :], ALU.mult, [y_.b], [sq.b])
            p.op("dve", lambda e: e.reduce_sum(out=ssq.t[:NQ, 0:1], in_=sq.t[:NQ, 0:256], axis=AX.X), [sq.b], [ssq.b])
            rstd_of(ssq.t[:NQ, 0:1], ssq.t[:NQ, 0:1], 256.0, [ssq.b], [ssq.b])
            stt(sq.t[:NQ, 0:256], y_.t[:, :], ssq.t[:NQ, 0:1], gnCM.t[:NQ, idx, :], ALU.mult, ALU.mult, [y_.b, ssq.b, gnCM.b], [sq.b])
            tt("dve", mtm.t[:NQ, idx * 256:(idx + 1) * 256], sq.t[:NQ, 0:256], sg.t[:, :], ALU.mult, [sq.b, sg.b], [mtm.b])

        def sample_layer(l):
            if l == 0:
                p.op("pool", lambda e: e.iota(iot.t[:, 0:1], pattern=[[0, 1]], base=0, channel_multiplier=1), [], [iot.b])
                ts("dve", idxT.t[:, :], idxT.t[:, :], 256, ALU.mult, [idxT.b, iot.b], [idxT.b], s2=iot.t[:, 0:1], op1=ALU.add)
            else:
                ts("dve", idxT.t[:, :], idxT.t[:, :], 128, ALU.add, [idxT.b], [idxT.b])
            for h in range(4):
                dma("sp", sbias16.t[h * 4:(h + 1) * 4, 0:1], sb_bias[l, h:h + 1].partition_broadcast(4), [], [sbias16.b], slow=True)
            sq, ss = wk[0], sm[0]
            tt("dve", sq.t[:NQ, :], xs_t.t[:, :], xs_t.t[:, :], ALU.mult, [xs_t.b], [sq.b])
            p.op("dve", lambda e: e.reduce_sum(out=ss.t[:NQ, 0:1], in_=sq.t[:NQ, :], axis=AX.X), [sq.b], [ss.b])
            rstd_of(ss.t[:NQ, 0:1], ss.t[:NQ, 0:1], 1024.0, [ss.b], [ss.b])
            xnb = wkb[0]
            stt(xnb.t[:NQ, :], xs_t.t[:, :], ss.t[:NQ, 0:1], gt.t[:NQ, :], ALU.mult, ALU.mult, [xs_t.b, ss.b, gt.b], [xnb.b])
            for kc in range(8):
                tr(ptr.t[:, kc * NQ:(kc + 1) * NQ], xnb.t[:NQ, kc * 128:(kc + 1) * 128], identb.t[:NQ, :NQ], [xnb.b, identb.b], ptr.bs)
            cp("act", s_xnT.t[:, :, :], ptr.t[:, 0:8 * NQ].rearrange("p (k m) -> p k m", k=8), ptr.bs, [s_xnT.b])

            def sproj(j, evac):
                pm = pmm[j % 2]
                for kc in range(8):
                    mm(pm.t[:, 0:NQ], win.t[:, kc, j * 128:(j + 1) * 128], s_xnT.t[:, kc, :], kc == 0, kc == 7,
                       wb(kc, j * 128, (j + 1) * 128) + [s_xnT.b], [pm.b])
                evac(pm)

            for hf in range(2):
                def ev(pm, hf=hf):
                    cp("act", s_auf.t[:, hf, :], pm.t[:, 0:NQ], [pm.b], [s_auf.b])
                    cp("act", s_aub.t[:, hf, :], pm.t[:, 0:NQ], [pm.b], [s_aub.b])
                sproj(0 + hf, ev)
                sproj(2 + hf, lambda pm, hf=hf: act(s_sgA.t[:, hf, :], pm.t[:, 0:NQ], AF.Silu, [pm.b], [s_sgA.b]))
                for bi in range(3):
                    sproj(4 + 2 * bi + hf, lambda pm, hf=hf, bi=bi: cp("act", s_bbx.t[:, 2 * bi + hf, :], pm.t[:, 0:NQ], [pm.b], [s_bbx.b]))
                sproj(10 + hf, lambda pm, hf=hf: act(s_sgB.t[:, hf, :], pm.t[:, 0:NQ], AF.Silu, [pm.b], [s_sgB.b]))
                sproj(12 + hf, lambda pm, hf=hf: cp("act", s_qT.t[:, hf, :], pm.t[:, 0:NQ], [pm.b], [s_qT.b]))
                sproj(14 + hf, lambda pm, hf=hf: cp("act", s_kT.t[:, hf, :], pm.t[:, 0:NQ], [pm.b], [s_kT.b]))
                sproj(20 + hf, lambda pm, hf=hf: cp("act", s_mqT.t[:, hf, :], pm.t[:, 0:NQ], [pm.b], [s_mqT.b]))
            for (c0, dst) in ((2304, s_sgC), (2816, s_sgM)):
                pm = pmm[0]
                for kc in range(8):
                    mm(pm.t[:NQ, 0:256], s_xnT.t[:, kc, :], win.t[:, kc, c0:c0 + 256], kc == 0, kc == 7,
                       wb(kc, c0, c0 + 256) + [s_xnT.b], [pm.b])
                act(dst.t[:, :], pm.t[:NQ, 0:256], AF.Silu, [pm.b], [dst.b])
            for n in range(4):
                pm = pmm[n % 2]
                for kc in range(8):
                    mm(pm.t[:4, 0:512], s_xnT.t[:, kc, n * 4:(n + 1) * 4], win.t[:, kc, 1792:2304], kc == 0, kc == 7,
                       wb(kc, 1792, 2304) + [s_xnT.b], [pm.b])
                kvf = wk[1 + n % 2]
                cp("act", kvf.t[:4, 0:512], pm.t[:4, :], [pm.b], [kvf.b])
                dma("sp", sbk_s[n, l], kvf.t[:4, 0:256], [kvf.b], [Buf()], is_out=True)
                dma("sp", sbv_s[n, l], kvf.t[:4, 256:512], [kvf.b], [Buf()], is_out=True)
                cp("act", s_vbn.t[:, n, :], kvf.t[:4, 256:512], [kvf.b], [s_vbn.b])

            for hf in range(2):
                for n in range(4):
                    for j in range(2):
                        dma("sp", s_vbuf.t[:, hf, n, j:j + 1], colv(sconv[n, l, j, hf * 128:(hf + 1) * 128]), [], [s_vbuf.b], slow=True)
            yb = wk[3]
            for hf in range(2):
                tt("dve", s_vbuf.t[:, hf, :, 2:6], v4(s_bbx.t[:, 2 + hf, :]), v4(s_bbx.t[:, 4 + hf, :]), ALU.mult, [s_bbx.b], [s_vbuf.b])
                acc = yb.t[:, hf * NQ:(hf + 1) * NQ]
                ts("dve", v4(acc), s_vbuf.t[:, hf, :, 2:6], cw.t[:, hf, 2:3], ALU.mult, [s_vbuf.b, cw.b], [yb.b])
                stt(v4(acc), s_vbuf.t[:, hf, :, 1:5], cw.t[:, hf, 1:2], v4(acc), ALU.mult, ALU.add, [s_vbuf.b, cw.b, yb.b], [yb.b])
                stt(v4(acc), s_vbuf.t[:, hf, :, 0:4], cw.t[:, hf, 0:1], v4(acc), ALU.mult, ALU.add, [s_vbuf.b, cw.b, yb.b], [yb.b])
                tt("dve", acc, acc, s_bbx.t[:, 0 + hf, :], ALU.mult, [yb.b, s_bbx.b], [yb.b])
            for hf in range(2):
                for n in range(4):
                    for j in range(2):
                        dma("sp", colv(conv_s[n, l, j, hf * 128:(hf + 1) * 128]), s_vbuf.t[:, hf, n, 4 + j:5 + j], [s_vbuf.b], [Buf()],
                            slow=True, is_out=True)
            s_merge_fm(1, yb, s_sgB)

            for n in range(4):
                for gi in range(2):
                    dma("sp", s_hst.t[gi * 64:(gi + 1) * 64, :, n, 0], sre[n, l].rearrange("(gp gi) p -> gi p gp", gi=2)[gi], [], [s_hst.b], slow=True)
                    dma("sp", s_hst.t[gi * 64:(gi + 1) * 64, :, n, 1], sim[n, l].rearrange("(gp gi) p -> gi p gp", gi=2)[gi], [], [s_hst.b], slow=True)
            ya = wk[3]
            yab = wkb[2]
            for gp in range(8):
                hf = gp // 4
                pb = pss[gp % 2]
                mm(pb.t[:, 0:NQ], BtR.t[:, gp, :], s_aub.t[:, hf, :], True, True, [BtR.b, s_aub.b], [pb.b])
                mm(pb.t[:, NQ:2 * NQ], BtI.t[:, gp, :], s_aub.t[:, hf, :], True, True, [BtI.b, s_aub.b], [pb.b])
                c_ = cosT.t[:, gp, 0:4].unsqueeze(1).to_broadcast([128, 4, 4])
                s_ = sinT.t[:, gp, 0:4].unsqueeze(1).to_broadcast([128, 4, 4])
                w = wk[4]
                q1, q2, q3, q4 = (w.t[:, i * NQ:(i + 1) * NQ] for i in range(4))
                bre_, bim_ = v4(pb.t[:, 0:NQ]), v4(pb.t[:, NQ:2 * NQ])
                tt("dve", v4(q1), bre_, c_, ALU.mult, [pb.b, cosT.b], [w.b])
                tt("dve", v4(q2), bim_, s_, ALU.mult, [pb.b, sinT.b], [w.b])
                tt("dve", v4(q3), bim_, c_, ALU.mult, [pb.b, cosT.b], [w.b])
                tt("dve", v4(q4), bre_, s_, ALU.mult, [pb.b, sinT.b], [w.b])
                tt("dve", q1, q1, q2, ALU.add, [w.b], [w.b])
                tt("dve", q3, q3, q4, ALU.subtract, [w.b], [w.b])
                g_ = wk[5]
                gre, gim = g_.t[:, 0:NQ], g_.t[:, NQ:2 * NQ]
                rb = rsp.t[:, gp:gp + 1].to_broadcast([128, 4])
                for n in range(4):
                    scan(gre[:, n * 4:(n + 1) * 4], rb, q1[:, n * 4:(n + 1) * 4], s_hst.t[:, gp, n, 0:1], [w.b, rsp.b, s_hst.b], [g_.b])
                    scan(gim[:, n * 4:(n + 1) * 4], rb, q3[:, n * 4:(n + 1) * 4], s_hst.t[:, gp, n, 1:2], [w.b, rsp.b, s_hst.b], [g_.b])
                tt("dve", v4(q1), v4(gre), c_, ALU.mult, [g_.b, cosT.b], [w.b])
                tt("dve", v4(q2), v4(gim), s_, ALU.mult, [g_.b, sinT.b], [w.b])
                tt("dve", v4(q3), v4(gim), c_, ALU.mult, [g_.b, cosT.b], [w.b])
                tt("dve", v4(q4), v4(gre), s_, ALU.mult, [g_.b, sinT.b], [w.b])
                hb = wkb[3]
                hR, hI = hb.t[:, 0:NQ], hb.t[:, NQ:2 * NQ]
                tt("dve", hR, q1, q2, ALU.subtract, [w.b], [hb.b])
                tt("dve", hI, q3, q4, ALU.add, [w.b], [hb.b])
                tt("dve", s_hst.t[:, gp, :, 0], v4(q1)[:, :, 3], v4(q2)[:, :, 3], ALU.subtract, [w.b], [s_hst.b])
                tt("dve", s_hst.t[:, gp, :, 1], v4(q3)[:, :, 3], v4(q4)[:, :, 3], ALU.add, [w.b], [s_hst.b])
                mm(py.t[:, hf, 0:NQ], CtR.t[:, gp, :], hR, gp % 4 == 0, False, [CtR.b, hb.b], [py.b])
                mm(py.t[:, hf, 0:NQ], CtI.t[:, gp, :], hI, False, gp % 4 == 3, [CtI.b, hb.b], [py.b])
                if gp % 4 == 3:
                    yh = ya.t[:, hf * NQ:(hf + 1) * NQ]
                    stt(yh, s_auf.t[:, hf, :], dvec.t[:, hf:hf + 1], py.t[:, hf, 0:NQ], ALU.mult, ALU.add, [s_auf.b, dvec.b, py.b], [ya.b])
                    act(yh, yh, AF.Gelu, [ya.b], [ya.b])
                    cp("dve", yab.t[:, hf * NQ:(hf + 1) * NQ], yh, [ya.b], [yab.b])
            for n in range(4):
                for gi in range(2):
                    dma("sp", ssmre_s[n, l].rearrange("(gp gi) p -> gi p gp", gi=2)[gi], s_hst.t[gi * 64:(gi + 1) * 64, :, n, 0], [s_hst.b], [Buf()],
                        slow=True, is_out=True)
                    dma("sp", ssmim_s[n, l].rearrange("(gp gi) p -> gi p gp", gi=2)[gi], s_hst.t[gi * 64:(gi + 1) * 64, :, n, 1], [s_hst.b], [Buf()],
                        slow=True, is_out=True)
            for oc in range(2):
                pm = pmm[oc]
                for k2 in range(2):
                    mm(pm.t[:, 0:NQ], wglu.t[:, k2, oc * 128:(oc + 1) * 128], yab.t[:, k2 * NQ:(k2 + 1) * NQ], k2 == 0, k2 == 1,
                       [wglu.b, yab.b], [pm.b])
                sg_ = wk[5]
                act(sg_.t[:, 0:NQ], pm.t[:, 0:NQ], AF.Sigmoid, [pm.b], [sg_.b])
                tt("dve", ya.t[:, oc * NQ:(oc + 1) * NQ], ya.t[:, oc * NQ:(oc + 1) * NQ], sg_.t[:, 0:NQ], ALU.mult, [ya.b, sg_.b], [ya.b])
            s_merge_fm(0, ya, s_sgA)

            NB = NPG // 4
            ncar = s_ncar
            o16 = wk[6]

            def gatherK(n, kb, slot):
                for pg in range(4):
                    col_ = n * NPG + kb * 4 + pg
                    off = bass.IndirectOffsetOnAxis(ap=idxT.t[:, col_:col_ + 1], axis=0)
                    p.dma("pool", lambda e, off=off, pg=pg: e.indirect_dma_start(out=Kb.t[:, slot, pg, :], out_offset=None, in_=ck, in_offset=off),
                          [idxT.b], [Kb.bs[slot]])

            def gatherV(n, kb):
                for pg in range(4):
                    col_ = n * NPG + kb * 4 + pg
                    off = bass.IndirectOffsetOnAxis(ap=idxT.t[:, col_:col_ + 1], axis=0)
                    p.dma("pool", lambda e, off=off, pg=pg: e.indirect_dma_start(out=Vb.t[:, 0, pg, :], out_offset=None, in_=cv, in_offset=off),
                          [idxT.b], [Vb.bs[0]])

            def blk_tail(E, ncol, kw, nch, vfn, first, last):
                Lb, P_ = wk[2], wk[4]
                mset("pool", Lb.t[:NQ, 0:1], 0.0, [Lb.b])
                act(Lb.t[:NQ, 1:ncol + 1], E.t[:NQ, 0:ncol], AF.Ln, [E.b], [Lb.b], bias=1.0)
                scan(P_.t[:NQ, 0:ncol + 1], onec.t[:NQ, 0:1].to_broadcast([NQ, ncol + 1]), Lb.t[:NQ, 0:ncol + 1], 0.0,
                     [onec.b, Lb.b], [P_.b])
                tt("dve", ncar.t[:NQ, 0:1], ncar.t[:NQ, 0:1], P_.t[:NQ, ncol:ncol + 1], ALU.subtract, [ncar.b, P_.b], [ncar.b])
                act(P_.t[:NQ, 0:ncol], P_.t[:NQ, 0:ncol], AF.Exp, [P_.b, ncar.b], [P_.b], bias=ncar.t[:NQ, 0:1])
                Wb = wkb[3]
                tt("dve", Wb.t[:NQ, 0:ncol], E.t[:NQ, 0:ncol], P_.t[:NQ, 0:ncol], ALU.mult, [E.b, P_.b], [Wb.b])
                for c4 in range(nch):
                    tr(ptr.t[:kw, c4 * NQ:(c4 + 1) * NQ], Wb.t[:NQ, c4 * kw:(c4 + 1) * kw], identb.t[:NQ, :NQ], [Wb.b, identb.b], ptr.bs)
                WT = wkb[2]
                cp("act", WT.t[:kw, 0:nch * NQ], ptr.t[:kw, 0:nch * NQ], ptr.bs, [WT.b])
                for c4 in range(nch):
                    vap, vbufs = vfn(c4)
                    mm(py.t[:NQ, 0, :], WT.t[:kw, c4 * NQ:(c4 + 1) * NQ], vap, first and c4 == 0, last and c4 == nch - 1,
                       [WT.b] + vbufs, [py.b])

            for n in range(4):
                mset("pool", qbd.t[:, :, :].rearrange("p a b -> p (a b)"), 0.0, [qbd.b])
                for h in range(4):
                    pr, hc = (h % 2) * 64, h // 2
                    cp("pool", qbd.t[pr:pr + 64, hc, h * 4:(h + 1) * 4], s_qT.t[pr:pr + 64, hc, n * 4:(n + 1) * 4], [s_qT.b], [qbd.b])
                mset("dve", ncar.t[:NQ, 0:1], 0.0, [ncar.b])
                if NB > 0:
                    gatherK(n, NB - 1, (NB - 1) % 2)
                    gatherV(n, NB - 1)
                S = pss[0]
                for hc in range(2):
                    mm(S.t[:NQ, 0:4], qbd.t[:, hc, :], s_kT.t[:, hc, n * 4:(n + 1) * 4], hc == 0, hc == 1, [qbd.b, s_kT.b], [S.b])
                E = wk[0]
                act(E.t[:NQ, 0:4], S.t[:NQ, 0:4], AF.Exp, [S.b, sbias16.b], [E.b], bias=sbias16.t[:, 0:1], scale=0.125)
                tt("dve", E.t[:NQ, 0:4], E.t[:NQ, 0:4], mask16.t[:, :], ALU.mult, [E.b, mask16.b], [E.b])
                blk_tail(E, 4, 4, 1, lambda c4, n=n: (s_vbn.t[:, n, :], [s_vbn.b]), True, NB == 0)
                yield
                for kb in range(NB - 1, -1, -1):
                    slot = kb % 2
                    if kb > 0:
                        gatherK(n, kb - 1, (kb - 1) % 2)
                    for pg in range(4):
                        for hc in range(2):
                            tr(ptr.t[:, (hc * 4 + pg) * 128:(hc * 4 + pg + 1) * 128], Kb.t[:, slot, pg, hc * 128:(hc + 1) * 128], identb.t[:, :],
                               [Kb.bs[slot], identb.b], ptr.bs)
                    kT = wkb[1]
                    cp("act", kT.t[:, :], ptr.t[:, :], ptr.bs, [kT.b])
                    S = pss[kb % 2]
                    for hc in range(2):
                        mm(S.t[:NQ, :], qbd.t[:, hc, :], kT.t[:, hc * 512:(hc + 1) * 512], hc == 0, hc == 1, [qbd.b, kT.b], [S.b])
                    E = wk[kb % 2]
                    act(E.t[:NQ, 0:512], S.t[:NQ, :], AF.Exp, [S.b, sbias16.b], [E.b], bias=sbias16.t[:, 0:1], scale=0.125)
                    blk_tail(E, 512, 128, 4, lambda c4: (Vb.t[:, 0, c4, :], [Vb.bs[0]]), False, kb == 0)
                    if kb > 0:
                        gatherV(n, kb - 1)
                    yield
                cp("act", o16.t[:NQ, 0:256], py.t[:NQ, 0, :], [py.b], [o16.b])
                for h in range(4):
                    dma("sp", s_yc.t[n * 4:(n + 1) * 4, h * 64:(h + 1) * 64], o16.t[h * 4:(h + 1) * 4, h * 64:(h + 1) * 64], [o16.b], [s_yc.b])
            mtm = wkb[0]
            s_merge_tm(0, s_yc, s_sgC, mtm)

            for n in range(4):
                mkf, mvf = wk[0], wk[1]
                dma("sp", mkf.t[:, 0:512].rearrange("p (c d) -> p c d", c=2), cmk[n, l].rearrange("(c p) d -> p c d", p=128), [], [mkf.b])
                dma("sp", mvf.t[:, 0:512].rearrange("p (c d) -> p c d", c=2), cmv[n, l].rearrange("(c p) d -> p c d", p=128), [], [mvf.b])
                mkb = wkb[1]
                cp("dve", mkb.t[:, 0:512], mkf.t[:, 0:512], [mkf.b], [mkb.b])
                for mc in range(2):
                    for hc in range(2):
                        tr(ptr.t[:, (hc * 2 + mc) * 128:(hc * 2 + mc + 1) * 128], mkb.t[:, mc * 256 + hc * 128:mc * 256 + (hc + 1) * 128],
                           identb.t[:, :], [mkb.b, identb.b], ptr.bs)
                mkTs = wkb[2]
                cp("act", mkTs.t[:, 0:512], ptr.t[:, 0:512], ptr.bs, [mkTs.b])
                mvAs = wkb[3]
                mset("dve", mvAs.t[:, 0:520], 1.0, [mvAs.b])
                cp("dve", mvAs.t[:, 0:520].rearrange("p (c h e) -> p c h e", c=2, h=4)[:, :, :, 0:64],
                   mvf.t[:, 0:512].rearrange("p (c h d) -> p c h d", c=2, h=4), [mvf.b], [mvAs.b])
                for h in range(4):
                    pr, hc = (h % 2) * 64, h // 2
                    S = pss[h % 2]
                    mm(S.t[:4, 0:256], s_mqT.t[pr:pr + 64, hc, n * 4:(n + 1) * 4], mkTs.t[pr:pr + 64, hc * 256:(hc + 1) * 256], True, True,
                       [s_mqT.b, mkTs.b], [S.b])
                    mx = sm[3]
                    p.op("dve", lambda e: e.reduce_max(out=mx.t[:4, h:h + 1], in_=S.t[:4, 0:256], axis=AX.X), [S.b], [mx.b])
                    ts("dve", mx.t[:4, h:h + 1], mx.t[:4, h:h + 1], -0.125, ALU.mult, [mx.b], [mx.b])
                    Pm = wkb[0]
                    act(Pm.t[:4, 512:768], S.t[:4, 0:256], AF.Exp, [S.b, mx.b], [Pm.b], bias=mx.t[:4, h:h + 1], scale=0.125)
                    for mc in range(2):
                        tr(ptr.t[:, 512 + mc * 4:512 + (mc + 1) * 4], Pm.t[:4, 512 + mc * 128:512 + (mc + 1) * 128], identb.t[:4, :4],
                           [Pm.b, identb.b], ptr.bs)
                    PT = wkb[0]
                    cp("act", PT.t[:, 768:776], ptr.t[:, 512:520], ptr.bs, [PT.b])
                    for mc in range(2):
                        mm(po2.t[:4, h * 65:(h + 1) * 65], PT.t[:, 768 + mc * 4:768 + (mc + 1) * 4],
                           mvAs.t[:, 0:520].rearrange("p (c h e) -> p c h e", c=2, h=4)[:, mc, h, :], mc == 0, mc == 1,
                           [PT.b, mvAs.b], [po2.b])
                ym4 = wk[5]
                rd = sm[4]
                for h in range(4):
                    p.op("dve", lambda e: e.reciprocal(out=rd.t[:4, h:h + 1], in_=po2.t[:4, h * 65 + 64:h * 65 + 65]), [po2.b], [rd.b])
                    ts("dve", ym4.t[:4, h * 64:(h + 1) * 64], po2.t[:4, h * 65:h * 65 + 64], rd.t[:4, h:h + 1], ALU.mult, [po2.b, rd.b], [ym4.b])
                dma("sp", s_ym.t[n * 4:(n + 1) * 4, :], ym4.t[:4, 0:256], [ym4.b], [s_ym.b])
            s_merge_tm(1, s_ym, s_sgM, mtm)
            for c in range(4):
                tr(ptr.t[:, c * NQ:(c + 1) * NQ], mtm.t[:NQ, c * 128:(c + 1) * 128], identb.t[:NQ, :NQ], [mtm.b, identb.b], ptr.bs)
            cp("act", s_mT.t[:, 4:8, :], ptr.t[:, 0:4 * NQ].rearrange("p (k m) -> p k m", k=4), ptr.bs, [s_mT.b])
            for oc in range(2):
                pm = pmm[oc]
                for kc in range(8):
                    mm(pm.t[:NQ, :], s_mT.t[:, kc, :], wout.t[:, kc, oc * 512:(oc + 1) * 512], kc == 0, kc == 7,
                       [s_mT.b, wout.bs[kc * 2 + oc]], [pm.b])
                tt("dve", xs_t.t[:, oc * 512:(oc + 1) * 512], xs_t.t[:, oc * 512:(oc + 1) * 512], pm.t[:NQ, :], ALU.add, [xs_t.b, pm.b], [xs_t.b])
            if l == DEPTH - 1:
                fg = wk[5]
                dma("sp", fg.t[:, :], fin_g.partition_broadcast(128), [], [fg.b])
                tt("dve", sq.t[:NQ, :], xs_t.t[:, :], xs_t.t[:, :], ALU.mult, [xs_t.b], [sq.b])
                p.op("dve", lambda e: e.reduce_sum(out=ss.t[:NQ, 0:1], in_=sq.t[:NQ, :], axis=AX.X), [sq.b], [ss.b])
                rstd_of(ss.t[:NQ, 0:1], ss.t[:NQ, 0:1], 1024.0, [ss.b], [ss.b])
                yo = wk[1]
                stt(yo.t[:NQ, :], xs_t.t[:, :], ss.t[:NQ, 0:1], fg.t[:NQ, :], ALU.mult, ALU.mult, [xs_t.b, ss.b, fg.b], [yo.b])
                dma("sp", y_s, yo.t[:NQ, :], [yo.b], [Buf()], is_out=True)

        def body():
          prep_mem()
          sample_setup()
          stage(1)
          for l in range(DEPTH):
              for kc in range(8):
                  for c3 in range(6):
                      dma("pool", win.t[:, kc, c3 * 512:(c3 + 1) * 512], w_in[l, kc * 128:(kc + 1) * 128, c3 * 512:(c3 + 1) * 512],
                          [], [win.bs[kc * 6 + c3]])
              for kc in range(8):
                  for c2 in range(2):
                      dma("pool", wout.t[:, kc, c2 * 512:(c2 + 1) * 512], w_out[l, kc * 128:(kc + 1) * 128, c2 * 512:(c2 + 1) * 512],
                          [], [wout.bs[kc * 2 + c2]])
              for k2 in range(2):
                  dma("pool", wglu.t[:, k2, :], w_glu[l, k2 * 128:(k2 + 1) * 128, :], [], [wglu.b])
              wdeps = set(b.lw for b in win.bs + wout.bs + [wglu.b])
              p._wait("dve", wdeps)
              p._wait("act", wdeps)
              dma("sp", gt.t[:, :], norm_g[l].partition_broadcast(128), [], [gt.b])
              def col(ap1d):
                  return ap1d.rearrange("(p o) -> p o", o=1)
              for g_ in range(2):
                  for h_ in range(2):
                      dma("sp", gnAB.t[:, g_ * 2 + h_:g_ * 2 + h_ + 1], col(gng[l, g_, h_ * 128:(h_ + 1) * 128]), [], [gnAB.b], slow=True)
              dma("sp", gnCM.t[:, :, :], gng[l, 2:4, :].partition_broadcast(128), [], [gnCM.b])
              for h_ in range(2):
                  dma("sp", dvec.t[:, h_:h_ + 1], col(ssm_d[l, h_ * 128:(h_ + 1) * 128]), [], [dvec.b], slow=True)
                  for j_ in range(3):
                      dma("sp", cw.t[:, h_, j_:j_ + 1], col(conv_w[l, j_, h_ * 128:(h_ + 1) * 128]), [], [cw.b], slow=True)
              dma("sp", sbb.t[:, :], sb_bias[l].partition_broadcast(128), [], [sbb.b])

              stage(2)
              lr, li, a1, th, er, cs, sn, tf = wk[0], wk[1], wk[2], wk[3], wk[4], wk[5], wk[6], wk[7]
              ldt = sm[0]
              dma("sp", lr.t[:, :], lam_re[l].rearrange("g p -> (g p)").partition_broadcast(128), [], [lr.b])
              dma("sp", li.t[:, :], lam_im[l].rearrange("g p -> (g p)").partition_broadcast(128), [], [li.b])
              dma("sp", ldt.t[:, :], log_dt[l].partition_broadcast(128), [], [ldt.b])
              act(ldt.t[:, :], ldt.t[:, :], AF.Exp, [ldt.b], [ldt.b])
              dtb = ldt.t[:, :].unsqueeze(2).to_broadcast([128, 16, 64])

              def v3(x):
                  return x.t[:, :].rearrange("p (g s) -> p g s", g=16)

              tt("dve", v3(a1), v3(lr), dtb, ALU.mult, [lr.b, ldt.b], [a1.b])
              tt("dve", v3(th), v3(li), dtb, ALU.mult, [li.b, ldt.b], [th.b])
              act(er.t[:, :], a1.t[:, :], AF.Exp, [a1.b], [er.b])
              sincos(th.t[:, :], th.b, 1024, sn.t[:, :], sn.b, cs.t[:, :], cs.b, tf, tmpi)
              tt("dve", cs.t[:, :], cs.t[:, :], er.t[:, :], ALU.mult, [cs.b, er.b], [cs.b])
              ts("dve", cs.t[:, :], cs.t[:, :], -1.0, ALU.add, [cs.b], [cs.b])
              tt("dve", sn.t[:, :], sn.t[:, :], er.t[:, :], ALU.mult, [sn.b, er.b], [sn.b])
              tt("dve", er.t[:, :], lr.t[:, :], lr.t[:, :], ALU.mult, [lr.b], [er.b])
              tt("dve", a1.t[:, :], li.t[:, :], li.t[:, :], ALU.mult, [li.b], [a1.b])
              tt("dve", er.t[:, :], er.t[:, :], a1.t[:, :], ALU.add, [er.b, a1.b], [er.b])
              p.op("dve", lambda e: e.reciprocal(out=er.t[:, :], in_=er.t[:, :]), [er.b], [er.b])
              tt("dve", a1.t[:, :], cs.t[:, :], lr.t[:, :], ALU.mult, [cs.b, lr.b], [a1.b])
              tt("dve", tf.t[:, :], sn.t[:, :], li.t[:, :], ALU.mult, [sn.b, li.b], [tf.b])
              tt("dve", a1.t[:, :], a1.t[:, :], tf.t[:, :], ALU.add, [a1.b, tf.b], [a1.b])
              tt("dve", a1.t[:, :], a1.t[:, :], er.t[:, :], ALU.mult, [a1.b, er.b], [a1.b])
              tt("dve", th.t[:, :], sn.t[:, :], lr.t[:, :], ALU.mult, [sn.b, lr.b], [th.b])
              tt("dve", tf.t[:, :], cs.t[:, :], li.t[:, :], ALU.mult, [cs.b, li.b], [tf.b])
              tt("dve", th.t[:, :], th.t[:, :], tf.t[:, :], ALU.subtract, [th.b, tf.b], [th.b])
              tt("dve", th.t[:, :], th.t[:, :], er.t[:, :], ALU.mult, [th.b, er.b], [th.b])
              cre, cim = a1, th
              BfR, BfI = wk[0], wk[1]
              mset("dve", BfR.t[:, :], 0.0, [BfR.b])
              mset("dve", BfI.t[:, :], 0.0, [BfI.b])
              for g in range(16):
                  r0 = (g % 8) * 16
                  c0 = (g // 2) * 128 + (g % 2) * 64
                  dma("sp", BfR.t[r0:r0 + 16, c0:c0 + 64], b_re[l, g].rearrange("p c -> c p"), [], [BfR.b], slow=True)
                  dma("sp", BfI.t[r0:r0 + 16, c0:c0 + 64], b_im[l, g].rearrange("p c -> c p"), [], [BfI.b], slow=True)
              t1, t2 = wk[4], wk[5]
              tt("dve", t1.t[:, :], cre.t[:, :], BfR.t[:, :], ALU.mult, [cre.b, BfR.b], [t1.b])
              tt("dve", t2.t[:, :], cim.t[:, :], BfI.t[:, :], ALU.mult, [cim.b, BfI.b], [t2.b])
              tt("dve", BtR.t[:, :, :].rearrange("p g s -> p (g s)"), t1.t[:, :], t2.t[:, :], ALU.subtract,
                 [t1.b, t2.b], [BtR.b])
              tt("dve", t1.t[:, :], cre.t[:, :], BfI.t[:, :], ALU.mult, [cre.b, BfI.b], [t1.b])
              tt("dve", t2.t[:, :], cim.t[:, :], BfR.t[:, :], ALU.mult, [cim.b, BfR.b], [t2.b])
              tt("dve", BtI.t[:, :, :].rearrange("p g s -> p (g s)"), t1.t[:, :], t2.t[:, :], ALU.add,
                 [t1.b, t2.b], [BtI.b])
              CfR, CfI = wk[6], wk[7]
              mset("dve", CfR.t[:, :], 0.0, [CfR.b])
              mset("dve", CfI.t[:, :], 0.0, [CfI.b])
              for g in range(16):
                  gp, gi = g // 2, g % 2
                  c0 = gp * 128 + (gp % 4) * 32 + gi * 16
                  dma("sp", CfR.t[gi * 64:gi * 64 + 64, c0:c0 + 16], c_re[l, g].rearrange("c p -> p c"), [], [CfR.b], slow=True)
                  dma("sp", CfI.t[gi * 64:gi * 64 + 64, c0:c0 + 16], c_im[l, g].rearrange("c p -> p c"), [], [CfI.b], slow=True)
              cp("dve", CtR.t[:, :, :].rearrange("p g s -> p (g s)"), CfR.t[:, :], [CfR.b], [CtR.b])
              ts("dve", CtI.t[:, :, :].rearrange("p g s -> p (g s)"), CfI.t[:, :], -1.0, ALU.mult, [CfI.b], [CtI.b])
              lrs, lis, dts, ths = sm[1], sm[2], sm[3], sm[4]
              for gi in range(2):
                  dma("sp", lrs.t[gi * 64:(gi + 1) * 64, 0:8], lam_re[l].rearrange("(gp gi) p -> gi p gp", gi=2)[gi], [], [lrs.b], slow=True)
                  dma("sp", lis.t[gi * 64:(gi + 1) * 64, 0:8], lam_im[l].rearrange("(gp gi) p -> gi p gp", gi=2)[gi], [], [lis.b], slow=True)
              ldt2 = log_dt[l].rearrange("(gp gi) -> gi gp", gi=2)
              for gi in range(2):
                  dma("sp", dts.t[gi * 64:(gi + 1) * 64, 0:8], ldt2[gi].partition_broadcast(64), [], [dts.b], slow=True)
              act(dts.t[:, 0:8], dts.t[:, 0:8], AF.Exp, [dts.b], [dts.b])
              tt("dve", lrs.t[:, 0:8], lrs.t[:, 0:8], dts.t[:, 0:8], ALU.mult, [lrs.b, dts.b], [lrs.b])
              act(rsp.t[:, :], lrs.t[:, 0:8], AF.Exp, [lrs.b], [rsp.b])
              tt("dve", ths.t[:, 0:8], lis.t[:, 0:8], dts.t[:, 0:8], ALU.mult, [lis.b, dts.b], [ths.b])
              tpos_i = tmpi
              p.op("pool", lambda e: e.iota(tpos_i.t[:, 0:TT], pattern=[[1, TT]], base=1, channel_multiplier=0),
                   [], [tpos_i.b])
              tpos = wk[4]
              cp("dve", tpos.t[:, 0:TT], tpos_i.t[:, 0:TT], [tpos_i.b], [tpos.b])
              ang = wk[5]
              for half in range(2):
                  for g4 in range(4):
                      gp = half * 4 + g4
                      ts("dve", ang.t[:, g4 * TT:(g4 + 1) * TT], tpos.t[:, 0:TT], ths.t[:, gp:gp + 1], ALU.mult,
                         [tpos.b, ths.b], [ang.b])
                  sincos(ang.t[:, 0:4 * TT], ang.b, 4 * TT,
                         sinT.t[:, half * 4:(half + 1) * 4, :].rearrange("p g t -> p (g t)"), sinT.b,
                         cosT.t[:, half * 4:(half + 1) * 4, :].rearrange("p g t -> p (g t)"), cosT.b, wk[6], tmpi)

              stage(3)
              wmem = wkb[0]
              wm = [wkb[0], wkb[1], wkb[2], wkb[3]]
              for kc in range(8):
                  dma("pool", wm[kc // 2].t[:, (kc % 2) * 512:(kc % 2 + 1) * 512], w_mem[l, kc * 128:(kc + 1) * 128, :],
                      [], [wm[kc // 2].b])

              def wmk(kc, c0, c1):
                  return wm[kc // 2].t[:, (kc % 2) * 512 + c0:(kc % 2) * 512 + c1]

              mset("dve", mvA.t[:, :, :, :].rearrange("p a b c -> p (a b c)"), 1.0, [mvA.b])
              for mc in range(2):
                  pm = pmm[mc]
                  for kc in range(8):
                      mm(pm.t[:, :], memT.t[:, kc, mc * 128:(mc + 1) * 128], wmk(kc, 0, 512), kc == 0, kc == 7,
                         [memT.b, wm[kc // 2].b], [pm.b])
                  kvf = wk[mc]
                  cp("act", kvf.t[:, 0:512], pm.t[:, :], [pm.b], [kvf.b])
                  dma("sp", memk_p[l, mc * 128:(mc + 1) * 128, :], kvf.t[:, 0:256], [kvf.b], [Buf()], is_out=True)
                  dma("sp", memv_p[l, mc * 128:(mc + 1) * 128, :], kvf.t[:, 256:512], [kvf.b], [Buf()], is_out=True)
                  cp("dve", mvA.t[:, mc, :, 0:64], kvf.t[:, 256:512].rearrange("p (h d) -> p h d", h=4), [kvf.b], [mvA.b])
              for c in range(2):
                  pm = pmm[c]
                  for kc in range(8):
                      mm(pm.t[:, 0:256], wmk(kc, c * 128, (c + 1) * 128), memT.t[:, kc, :], kc == 0, kc == 7,
                         [memT.b, wm[kc // 2].b], [pm.b])
                  cp("act", mkT.t[:, c, :], pm.t[:, 0:256], [pm.b], [mkT.b])

              stage(4)
              sgen = sample_layer(l) if DO_SAMPLE else iter(())
              mset("dve", hst.t[:, :, :].rearrange("p a b -> p (a b)"), 0.0, [hst.b])
              mset("dve", vbuf.t[:, :, 0:2], 0.0, [vbuf.b])

              for t in range(NT):
                  tok0 = t * TT
                  src = xp if l == 0 else x1
                  dma("sp", xt.t[:, :, :], src[tok0:tok0 + TT, :].rearrange("(s p) d -> p s d", p=128),
                      [x1bufs[t]] if l == 1 else [], [xt.b])
                  ss = sm[0]
                  for s in range(NS):
                      tt("dve", wk[0].t[:, :], xt.t[:, s, :], xt.t[:, s, :], ALU.mult, [xt.b], [wk[0].b])
                      p.op("dve", lambda e: e.reduce_sum(out=ss.t[:, s:s + 1], in_=wk[0].t[:, :], axis=AX.X),
                           [wk[0].b], [ss.b])
                  rstd_of(ss.t[:, 0:NS], ss.t[:, 0:NS], 1024.0, [ss.b], [ss.b])
                  for s in range(NS):
                      xnb = wkb[s]
                      stt(xnb.t[:, :], xt.t[:, s, :], ss.t[:, s:s + 1], gt.t[:, :], ALU.mult, ALU.mult,
                          [xt.b, ss.b, gt.b], [xnb.b])
                      for kc in range(8):
                          tr(ptr.t[:, kc * 128:(kc + 1) * 128], xnb.t[:, kc * 128:(kc + 1) * 128], identb.t[:],
                             [xnb.b, identb.b], ptr.bs)
                      cp("act", xnT.t[:, :, s * 128:(s + 1) * 128], ptr.t[:, :].rearrange("p (k m) -> p k m", k=8),
                         ptr.bs, [xnT.b])

                  stage(5)
                  def proj_fm(j, evac):
                      pm = pmm[j % 2]
                      for kc in range(8):
                          mm(pm.t[:, 0:TT], win.t[:, kc, j * 128:(j + 1) * 128], xnT.t[:, kc, :], kc == 0, kc == 7,
                             wb(kc, j * 128, (j + 1) * 128) + [xnT.b], [pm.b])
                          stage(5.01 + kc * 0.001)
                      stage(5.05)
                      evac(pm)

                  for hf in range(2):
                      def ev(pm, hf=hf):
                          cp("act", au_f.t[:, hf, :], pm.t[:, 0:TT], [pm.b], [au_f.b])
                          stage(5.06)
                          cp("act", au_b.t[:, hf, :], pm.t[:, 0:TT], [pm.b], [au_b.b])
                      proj_fm(0 + hf, ev)
                      stage(5.1)
                      proj_fm(2 + hf, lambda pm, hf=hf: act(sgA.t[:, hf, :], pm.t[:, 0:TT], AF.Silu, [pm.b], [sgA.b]))
                      stage(5.2)
                      for bi in range(3):
                          proj_fm(4 + 2 * bi + hf, lambda pm, hf=hf, bi=bi: cp("act", bbx.t[:, 2 * bi + hf, :], pm.t[:, 0:TT], [pm.b], [bbx.b]))
                      proj_fm(10 + hf, lambda pm, hf=hf: act(sgB.t[:, hf, :], pm.t[:, 0:TT], AF.Silu, [pm.b], [sgB.b]))
                      proj_fm(12 + hf, lambda pm, hf=hf: cp("act", qT.t[:, hf, :], pm.t[:, 0:TT], [pm.b], [qT.b]))

                      def evk(pm, hf=hf):
                          for s in range(NS):
                              cp("act", KcT.t[:, hf, tok0 + s * 128:tok0 + (s + 1) * 128], pm.t[:, s * 128:(s + 1) * 128],
                                 [pm.b], [KcT.bs[t * NS + s]])
                      proj_fm(14 + hf, evk)
                      proj_fm(20 + hf, lambda pm, hf=hf: cp("act", mqT.t[:, hf, :], pm.t[:, 0:TT], [pm.b], [mqT.b]))
                  stage(5.3)
                  for s in range(NS):
                      gs = t * NS + s
                      pm = pmm[s]
                      for kc in range(8):
                          mm(pm.t[:, :], xnT.t[:, kc, s * 128:(s + 1) * 128], win.t[:, kc, 1792:2304], kc == 0, kc == 7,
                             wb(kc, 1792, 2304) + [xnT.b], [pm.b])
                      kvf = wk[1 + s]
                      cp("act", kvf.t[:, 0:512], pm.t[:, :], [pm.b], [kvf.b])
                      r0 = tok0 + s * 128
                      dma("sp", sbk_p[l, r0:r0 + 128, :], kvf.t[:, 0:256], [kvf.b], [Buf()], is_out=True)
                      dma("sp", sbv_p[l, r0:r0 + 128, :], kvf.t[:, 256:512], [kvf.b], [Buf()], is_out=True)
                      cp("dve", Vc.t[:, gs, :], kvf.t[:, 256:512], [kvf.b], [Vc.bs[gs]])
                      stage(5.4)
                      for (c0, dst) in ((2304, sgC), (2816, sgM)):
                          pm2 = pmm[1 - s]
                          for kc in range(8):
                              mm(pm2.t[:, 0:256], xnT.t[:, kc, s * 128:(s + 1) * 128], win.t[:, kc, c0:c0 + 256],
                                 kc == 0, kc == 7, wb(kc, c0, c0 + 256) + [xnT.b], [pm2.b])
                          act(dst.t[:, s, :], pm2.t[:, 0:256], AF.Silu, [pm2.b], [dst.b])

                  stage(6)
                  def merge_fm(br, yv, yb_, sg):
                      sq = wk[7]
                      pm = pmm[0]
                      for hf in range(2):
                          tt("dve", sq.t[:, hf * TT:(hf + 1) * TT], yv(hf), yv(hf), ALU.mult, [yb_], [sq.b])
                      for hf in range(2):
                          mm(pm.t[:, 0:TT], onesf.t[:, :], sq.t[:, hf * TT:(hf + 1) * TT], hf == 0, hf == 1,
                             [onesf.b, sq.b], [pm.b])
                      rs = wk[6]
                      rstd_of(rs.t[:, 0:TT], pm.t[:, 0:TT], 256.0, [pm.b], [rs.b])
                      for hf in range(2):
                          stt(sq.t[:, hf * TT:(hf + 1) * TT], yv(hf), gnAB.t[:, br * 2 + hf:br * 2 + hf + 1], rs.t[:, 0:TT],
                              ALU.mult, ALU.mult, [yb_, gnAB.b, rs.b], [sq.b])
                          tt("dve", mergedT.t[:, br * 2 + hf, :], sq.t[:, hf * TT:(hf + 1) * TT], sg.t[:, hf, :], ALU.mult,
                             [sq.b, sg.b], [mergedT.bs[br * 2 + hf]])

                  yb = wk[3]
                  for hf in range(2):
                      tt("dve", vbuf.t[:, hf, 2:TT + 2], bbx.t[:, 2 + hf, :], bbx.t[:, 4 + hf, :], ALU.mult, [bbx.b], [vbuf.b])
                      acc = yb.t[:, hf * TT:(hf + 1) * TT]
                      ts("dve", acc, vbuf.t[:, hf, 2:TT + 2], cw.t[:, hf, 2:3], ALU.mult, [vbuf.b, cw.b], [yb.b])
                      stt(acc, vbuf.t[:, hf, 1:TT + 1], cw.t[:, hf, 1:2], acc, ALU.mult, ALU.add, [vbuf.b, cw.b, yb.b], [yb.b])
                      stt(acc, vbuf.t[:, hf, 0:TT], cw.t[:, hf, 0:1], acc, ALU.mult, ALU.add, [vbuf.b, cw.b, yb.b], [yb.b])
                      tt("dve", acc, acc, bbx.t[:, 0 + hf, :], ALU.mult, [yb.b, bbx.b], [yb.b])
                  if t == NT - 1:
                      for h_ in range(2):
                          for j_ in range(2):
                              dma("sp", col(conv_p[l, j_, h_ * 128:(h_ + 1) * 128]), vbuf.t[:, h_, TT + j_:TT + j_ + 1], [vbuf.b], [Buf()],
                                  slow=True, is_out=True)
                  for hf in range(2):
                      cp("dve", vbuf.t[:, hf, 0:2], vbuf.t[:, hf, TT:TT + 2], [vbuf.b], [vbuf.b])
                  merge_fm(1, lambda hf: yb.t[:, hf * TT:(hf + 1) * TT], yb.b, sgB)

                  stage(7)
                  ya = wk[3]
                  yab = wkb[2]
                  F4 = 4 * TT
                  for hf in range(2):
                      pre, pim = pss[0], pss[1]
                      for g4 in range(4):
                          gp = hf * 4 + g4
                          mm(pre.t[:, g4 * TT:(g4 + 1) * TT], BtR.t[:, gp, :], au_b.t[:, hf, :], True, True, [BtR.b, au_b.b], [pre.b])
                          mm(pim.t[:, g4 * TT:(g4 + 1) * TT], BtI.t[:, gp, :], au_b.t[:, hf, :], True, True, [BtI.b, au_b.b], [pim.b])
                      c_ = cosT.t[:, hf * 4:(hf + 1) * 4, :].rearrange("p g t -> p (g t)")
                      s_ = sinT.t[:, hf * 4:(hf + 1) * 4, :].rearrange("p g t -> p (g t)")
                      wa, wb_ = wk[0], wk[1]
                      q1, q3 = wa.t[:, 0:F4], wa.t[:, F4:2 * F4]
                      q2, q4 = wb_.t[:, 0:F4], wb_.t[:, F4:2 * F4]
                      tt("dve", q1, pre.t[:, 0:F4], c_, ALU.mult, [pre.b, cosT.b], [wa.b])
                      tt("dve", q2, pim.t[:, 0:F4], s_, ALU.mult, [pim.b, sinT.b], [wb_.b])
                      tt("dve", q3, pim.t[:, 0:F4], c_, ALU.mult, [pim.b, cosT.b], [wa.b])
                      tt("dve", q4, pre.t[:, 0:F4], s_, ALU.mult, [pre.b, sinT.b], [wb_.b])
                      tt("dve", q1, q1, q2, ALU.add, [wa.b, wb_.b], [wa.b])
                      tt("dve", q3, q3, q4, ALU.subtract, [wa.b, wb_.b], [wa.b])
                      g_ = wk[2]
                      gre, gim = g_.t[:, 0:F4], g_.t[:, F4:2 * F4]
                      for g4 in range(4):
                          gp = hf * 4 + g4
                          rb = rsp.t[:, gp:gp + 1].to_broadcast([128, TT])
                          sl = slice(g4 * TT, (g4 + 1) * TT)
                          scan(gre[:, sl], rb, q1[:, sl], hst.t[:, gp, 0:1], [wa.b, rsp.b, hst.b], [g_.b])
                          scan(gim[:, sl], rb, q3[:, sl], hst.t[:, gp, 1:2], [wa.b, rsp.b, hst.b], [g_.b])
                      tt("dve", q1, gre, c_, ALU.mult, [g_.b, cosT.b], [wa.b])
                      tt("pool", q2, gim, s_, ALU.mult, [g_.b, sinT.b], [wb_.b])
                      tt("dve", q3, gim, c_, ALU.mult, [g_.b, cosT.b], [wa.b])
                      tt("pool", q4, gre, s_, ALU.mult, [g_.b, sinT.b], [wb_.b])
                      hb = wkb[3]
                      hR, hI = hb.t[:, 0:F4], hb.t[:, F4:2 * F4]
                      tt("dve", hR, q1, q2, ALU.subtract, [wa.b, wb_.b], [hb.b])
                      tt("dve", hI, q3, q4, ALU.add, [wa.b, wb_.b], [hb.b])

                      def lastc(q):
                          return q.rearrange("p (g t) -> p g t", g=4)[:, :, TT - 1]
                      tt("dve", hst.t[:, hf * 4:(hf + 1) * 4, 0], lastc(q1), lastc(q2), ALU.subtract, [wa.b, wb_.b], [hst.b])
                      tt("dve", hst.t[:, hf * 4:(hf + 1) * 4, 1], lastc(q3), lastc(q4), ALU.add, [wa.b, wb_.b], [hst.b])
                      for g4 in range(4):
                          gp = hf * 4 + g4
                          sl = slice(g4 * TT, (g4 + 1) * TT)
                          mm(po.t[:, 256 + hf * TT:256 + (hf + 1) * TT], CtR.t[:, gp, :], hR[:, sl], g4 == 0, False, [CtR.b, hb.b], [po.b])
                          mm(po.t[:, 256 + hf * TT:256 + (hf + 1) * TT], CtI.t[:, gp, :], hI[:, sl], False, g4 == 3, [CtI.b, hb.b], [po.b])
                      yh = ya.t[:, hf * TT:(hf + 1) * TT]
                      stt(yh, au_f.t[:, hf, :], dvec.t[:, hf:hf + 1], po.t[:, 256 + hf * TT:256 + (hf + 1) * TT], ALU.mult, ALU.add,
                          [au_f.b, dvec.b, po.b], [ya.b])
                      act(yh, yh, AF.Gelu, [ya.b], [ya.b])
                      cp("dve", yab.t[:, hf * TT:(hf + 1) * TT], yh, [ya.b], [yab.b])
                  if t == NT - 1:
                      for gi in range(2):
                          dma("sp", ssmre_p[l].rearrange("(gp gi) p -> gi p gp", gi=2)[gi], hst.t[gi * 64:(gi + 1) * 64, :, 0], [hst.b], [Buf()],
                              slow=True, is_out=True)
                          dma("sp", ssmim_p[l].rearrange("(gp gi) p -> gi p gp", gi=2)[gi], hst.t[gi * 64:(gi + 1) * 64, :, 1], [hst.b], [Buf()],
                              slow=True, is_out=True)
                  for oc in range(2):
                      pm = pmm[oc]
                      for k2 in range(2):
                          mm(pm.t[:, 0:TT], wglu.t[:, k2, oc * 128:(oc + 1) * 128], yab.t[:, k2 * TT:(k2 + 1) * TT],
                             k2 == 0, k2 == 1, [wglu.b, yab.b], [pm.b])
                      sg_ = wk[5]
                      act(sg_.t[:, 0:TT], pm.t[:, 0:TT], AF.Sigmoid, [pm.b], [sg_.b])
                      tt("dve", ya.t[:, oc * TT:(oc + 1) * TT], ya.t[:, oc * TT:(oc + 1) * TT], sg_.t[:, 0:TT], ALU.mult,
                         [ya.b, sg_.b], [ya.b])
                  merge_fm(0, lambda hf: ya.t[:, hf * TT:(hf + 1) * TT], ya.b, sgA)

                  stage(8)
                  def merge_tm(idx, yv, yb_, sg, s, mtm):
                      sq = wk[7]
                      ssq = sm[1]
                      tt("dve", sq.t[:, 0:256], yv, yv, ALU.mult, [yb_], [sq.b])
                      p.op("dve", lambda e: e.reduce_sum(out=ssq.t[:, 0:1], in_=sq.t[:, 0:256], axis=AX.X), [sq.b], [ssq.b])
                      rstd_of(ssq.t[:, 0:1], ssq.t[:, 0:1], 256.0, [ssq.b], [ssq.b])
                      stt(sq.t[:, 0:256], yv, ssq.t[:, 0:1], gnCM.t[:, idx, :], ALU.mult, ALU.mult, [yb_, ssq.b, gnCM.b], [sq.b])
                      tt("dve", mtm.t[:, idx * 256:(idx + 1) * 256], sq.t[:, 0:256], sg.t[:, s, :], ALU.mult, [sq.b, sg.b], [mtm.b])

                  for s in range(NS):
                      gq = t * NS + s
                      nkeys = (gq + 1) * 128
                      nblk = (nkeys + 511) // 512
                      mtm = wkb[0]
                      ncars = [sm[2], sm[5]]
                      for nc_ in ncars:
                          mset("dve", nc_.t[:, 0:4], 0.0, [nc_.b])
                      Wbs, WTs = [wkb[1], wkb[3]], [wkb[2], wkb[0]]
                      first = True
                      for kb in range(nblk - 1, -1, -1):
                          ncol = nkeys - kb * 512 if kb == nblk - 1 else 512
                          nc4 = ncol // 128
                          kbufs = [KcT.bs[kb * 4 + c4] for c4 in range(nc4)]

                          def head_steps(h, par, kb=kb, ncol=ncol, nc4=nc4, kbufs=kbufs, first=first):
                              pr, hc = (h % 2) * 64, h // 2
                              S, E, Lb, P_ = pss[par], wk[0 + par], wk[2 + par], wk[4 + par]
                              ncar, Wb, WT = ncars[par], Wbs[par], WTs[par]
                              trp = ptr.t[:, 0:512] if par == 0 else po2.t[:, :].bitcast(BF16)[:, 0:512]
                              trb = ptr.bs if par == 0 else [po2.b]

                              def s_S():
                                  mm(S.t[:, 0:ncol], qT.t[pr:pr + 64, hc, s * 128:(s + 1) * 128],
                                     KcT.t[pr:pr + 64, hc, kb * 512:kb * 512 + ncol], True, True, [qT.b] + kbufs, [S.b])

                              def s_E():
                                  act(E.t[:, 0:ncol], S.t[:, 0:ncol], AF.Exp, [S.b, sbb.b], [E.b], bias=sbb.t[:, h:h + 1], scale=0.125)
                                  if kb == nblk - 1:
                                      p.op("pool", lambda e: e.affine_select(out=E.t[:, ncol - 128:ncol], in_=E.t[:, ncol - 128:ncol],
                                                                             pattern=[[-1, 128]], compare_op=ALU.is_gt, fill=0.0,
                                                                             base=0, channel_multiplier=1), [E.b], [E.b])
                                  mset("pool", Lb.t[:, 0:1], 0.0, [Lb.b])

                              def s_L():
                                  act(Lb.t[:, 1:ncol + 1], E.t[:, 0:ncol], AF.Ln, [E.b], [Lb.b], bias=1.0)

                              def s_scan():
                                  scan(P_.t[:, 0:ncol + 1], onec.t[:, 0:1].to_broadcast([128, ncol + 1]), Lb.t[:, 0:ncol + 1], 0.0,
                                       [onec.b, Lb.b], [P_.b])
                                  tt("dve", ncar.t[:, h:h + 1], ncar.t[:, h:h + 1], P_.t[:, ncol:ncol + 1], ALU.subtract,
                                     [ncar.b, P_.b], [ncar.b])

                              def s_X():
                                  act(P_.t[:, 0:ncol], P_.t[:, 0:ncol], AF.Exp, [P_.b, ncar.b], [P_.b], bias=ncar.t[:, h:h + 1])

                              def s_W():
                                  tt("dve", Wb.t[:, 0:ncol], E.t[:, 0:ncol], P_.t[:, 0:ncol], ALU.mult, [E.b, P_.b], [Wb.b])

                              def s_tr():
                                  for c4 in range(nc4):
                                      tr(trp[:, c4 * 128:(c4 + 1) * 128], Wb.t[:, c4 * 128:(c4 + 1) * 128], identb.t[:],
                                         [Wb.b, identb.b], trb)

                              def s_ev():
                                  cp("act", WT.t[:, 0:ncol], trp[:, 0:ncol], trb, [WT.b])

                              def s_pv():
                                  for c4 in range(nc4):
                                      mm(po.t[:, h * 64:(h + 1) * 64], WT.t[:, c4 * 128:(c4 + 1) * 128],
                                         Vc.t[:, kb * 4 + c4, h * 64:(h + 1) * 64], first and c4 == 0 and h == 0 and par == 0, kb == 0 and c4 == nc4 - 1,
                                         [WT.b, Vc.bs[kb * 4 + c4]], [po.b])
                              return [s_S, s_E, s_L, s_scan, s_X, s_W, s_tr, s_ev, s_pv]

                          for hp in range(2):
                              sa, sb_ = head_steps(hp, 0), head_steps(hp + 2, 1)
                              if os.environ.get("NOILV"):
                                  for f_ in sa + sb_:
                                      f_()
                              else:
                                  for fa, fb in zip(sa, sb_):
                                      fa()
                                      fb()
                          first = False
                          next(sgen, None)
                      yc = wk[6]
                      cp("act", yc.t[:, 0:256], po.t[:, 0:256], [po.b], [yc.b])
                      merge_tm(0, yc.t[:, 0:256], yc.b, sgC, s, mtm)
                      stage(9)
                      for h in range(4):
                          pr, hc = (h % 2) * 64, h // 2
                          S = pss[h % 2]
                          mm(S.t[:, 0:256], mqT.t[pr:pr + 64, hc, s * 128:(s + 1) * 128], mkT.t[pr:pr + 64, hc, :], True, True,
                             [mqT.b, mkT.b], [S.b])
                          mx = sm[3]
                          p.op("dve", lambda e: e.reduce_max(out=mx.t[:, h:h + 1], in_=S.t[:, 0:256], axis=AX.X), [S.b], [mx.b])
                          ts("dve", mx.t[:, h:h + 1], mx.t[:, h:h + 1], -0.125, ALU.mult, [mx.b], [mx.b])
                          Pm = wkb[1]
                          act(Pm.t[:, 0:256], S.t[:, 0:256], AF.Exp, [S.b, mx.b], [Pm.b], bias=mx.t[:, h:h + 1], scale=0.125)
                          for mc in range(2):
                              tr(ptr.t[:, mc * 128:(mc + 1) * 128], Pm.t[:, mc * 128:(mc + 1) * 128], identb.t[:],
                                 [Pm.b, identb.b], ptr.bs)
                          PT = wkb[2]
                          cp("act", PT.t[:, 0:256], ptr.t[:, 0:256], ptr.bs, [PT.b])
                          for mc in range(2):
                              mm(po2.t[:, h * 65:(h + 1) * 65], PT.t[:, mc * 128:(mc + 1) * 128], mvA.t[:, mc, h, :],
                                 mc == 0, mc == 1, [PT.b, mvA.b], [po2.b])
                      ym = wk[6]
                      rd = sm[4]
                      for h in range(4):
                          p.op("dve", lambda e: e.reciprocal(out=rd.t[:, h:h + 1], in_=po2.t[:, h * 65 + 64:h * 65 + 65]),
                               [po2.b], [rd.b])
                          ts("dve", ym.t[:, 256 + h * 64:256 + (h + 1) * 64], po2.t[:, h * 65:h * 65 + 64], rd.t[:, h:h + 1],
                             ALU.mult, [po2.b, rd.b], [ym.b])
                      merge_tm(1, ym.t[:, 256:512], ym.b, sgM, s, mtm)
                      stage(10)
                      for c in range(4):
                          tr(ptr.t[:, c * 128:(c + 1) * 128], mtm.t[:, c * 128:(c + 1) * 128], identb.t[:], [mtm.b, identb.b], ptr.bs)
                      for c in range(4):
                          cp("act", mergedT.t[:, 4 + c, s * 128:(s + 1) * 128], ptr.t[:, c * 128:(c + 1) * 128], ptr.bs,
                             [mergedT.bs[4 + c]])

                  stage(11)
                  for s in range(NS):
                      for oc in range(2):
                          pm = pmm[oc]
                          for kc in range(8):
                              mm(pm.t[:, :], mergedT.t[:, kc, s * 128:(s + 1) * 128], wout.t[:, kc, oc * 512:(oc + 1) * 512],
                                 kc == 0, kc == 7, [mergedT.bs[kc], wout.bs[kc * 2 + oc]], [pm.b])
                          tt("dve", xt.t[:, s, oc * 512:(oc + 1) * 512], xt.t[:, s, oc * 512:(oc + 1) * 512], pm.t[:, :], ALU.add,
                             [xt.b, pm.b], [xt.b])
                  if l == 0:
                      dma("sp", x1[tok0:tok0 + TT, :].rearrange("(s p) d -> p s d", p=128), xt.t[:, :, :], [xt.b], [x1bufs[t]])
                  else:
                      fg = wk[5]
                      dma("sp", fg.t[:, :], fin_g.partition_broadcast(128), [], [fg.b])
                      ss = sm[0]
                      for s in range(NS):
                          tt("dve", wk[0].t[:, :], xt.t[:, s, :], xt.t[:, s, :], ALU.mult, [xt.b], [wk[0].b])
                          p.op("dve", lambda e: e.reduce_sum(out=ss.t[:, s:s + 1], in_=wk[0].t[:, :], axis=AX.X),
                               [wk[0].b], [ss.b])
                      rstd_of(ss.t[:, 0:NS], ss.t[:, 0:NS], 1024.0, [ss.b], [ss.b])
                      for s in range(NS):
                          yo = wk[1 + s]
                          stt(yo.t[:, :], xt.t[:, s, :], ss.t[:, s:s + 1], fg.t[:, :], ALU.mult, ALU.mult, [xt.b, ss.b, fg.b], [yo.b])
                          r0 = tok0 + s * 128
                          dma("sp", y_p[r0:r0 + 128, :], yo.t[:, :], [yo.b], [Buf()], is_out=True)
              stage(20)
              for _ in sgen:
                  pass


        try:
            body()
        except _Stop:
            pass
        p._wait("sp", set(p.out_tks))
    return nc


def kernel(**inputs):
    f32 = np.float32
    xpr = np.asarray(inputs["x_prompt"], f32)
    B, L, _ = xpr.shape
    NPH = inputs["cache_sb_k"].shape[0]
    NPG = inputs["page_table"].shape[1]
    nc = bass.Bass("TRN2", target_bir_lowering=False)
    build(nc, L, NPG, NPH)
    wnames = ["norm_g", "w_in", "w_out", "group_norm_g", "ssm_lambda_re", "ssm_lambda_im", "ssm_b_re", "ssm_b_im",
              "ssm_c_re", "ssm_c_im", "ssm_log_dt", "ssm_d", "ssm_w_glu", "conv_w", "sb_bias", "w_mem_kv", "final_norm_g"]
    W = {n: np.ascontiguousarray(np.asarray(inputs[n], f32)) for n in wnames}
    ck = np.ascontiguousarray(np.asarray(inputs["cache_sb_k"], f32)).reshape(-1, 256)
    cv = np.ascontiguousarray(np.asarray(inputs["cache_sb_v"], f32)).reshape(-1, 256)
    xsm = np.asarray(inputs["x_sample"], f32)
    in_maps = []
    for c in range(8):
        sl = slice(4 * c, 4 * c + 4)
        m = dict(W)
        m["xp"] = np.ascontiguousarray(xpr[c])
        m["memp"] = np.ascontiguousarray(np.asarray(inputs["mem_prompt"], f32)[c])
        m["xs"] = np.ascontiguousarray(xsm[sl]).reshape(16, D)
        m["ck"] = ck
        m["cv"] = cv
        m["sre"] = np.ascontiguousarray(np.asarray(inputs["state_ssm_re"], f32)[sl])
        m["sim"] = np.ascontiguousarray(np.asarray(inputs["state_ssm_im"], f32)[sl])
        m["sconv"] = np.ascontiguousarray(np.asarray(inputs["state_conv"], f32)[sl])
        m["cmk"] = np.ascontiguousarray(np.asarray(inputs["cache_mem_k"], f32)[sl]).reshape(4, DEPTH, 256, 256)
        m["cmv"] = np.ascontiguousarray(np.asarray(inputs["cache_mem_v"], f32)[sl]).reshape(4, DEPTH, 256, 256)
        m["ptab"] = np.ascontiguousarray(np.asarray(inputs["page_table"], np.int32)[sl]).reshape(-1)
        in_maps.append(m)
    res = run_bass_kernel_spmd(nc, in_maps, core_ids=list(range(8))).results

    def cat(k, shp):
        return np.ascontiguousarray(np.stack([np.asarray(r[k], f32) for r in res])).reshape(shp)

    return (cat("y_p", (8, L, D)), cat("y_s", (32, 4, D)),
            cat("sbk_p", (8, DEPTH, L, 4, 64)), cat("sbv_p", (8, DEPTH, L, 4, 64)),
            cat("ssmre_p", (8, DEPTH, 16, 64)), cat("ssmim_p", (8, DEPTH, 16, 64)), cat("conv_p", (8, DEPTH, 2, 256)),
            cat("memk_p", (8, DEPTH, 256, 4, 64)), cat("memv_p", (8, DEPTH, 256, 4, 64)),
            cat("sbk_s", (32, DEPTH, 4, 4, 64)), cat("sbv_s", (32, DEPTH, 4, 4, 64)),
            cat("ssmre_s", (32, DEPTH, 16, 64)), cat("ssmim_s", (32, DEPTH, 16, 64)), cat("conv_s", (32, DEPTH, 2, 256)))
```

```python
from contextlib import ExitStack
import math
from itertools import zip_longest
import numpy as np
import concourse.bass as bass
import concourse.mybir as mybir
from concourse.bass_utils import run_bass_kernel_spmd

F32 = mybir.dt.float32
BF16 = mybir.dt.bfloat16
I32 = mybir.dt.int32
AF = mybir.ActivationFunctionType
ALU = mybir.AluOpType
AX = mybir.AxisListType

D = 1024
DEPTH = 2
TT = 128
NS = TT // 128
EPS = 1e-6
TWO_PI = 2.0 * math.pi


class Buf:
    __slots__ = ("lw", "rd")

    def __init__(self):
        self.lw = None
        self.rd = {}


class BufGroup:
    __slots__ = ("parts",)

    def __init__(self, parts):
        self.parts = parts


def _flat(bs):
    out = []
    for b in bs:
        if isinstance(b, BufGroup):
            out.extend(b.parts)
        else:
            out.append(b)
    return out


class Prog:
    EPOCH = 12000

    def __init__(self, nc, stack):
        self.nc = nc
        self.stack = stack
        self.engs = {"pe": nc.tensor, "dve": nc.vector, "act": nc.scalar, "pool": nc.gpsimd, "sp": nc.sync}
        self.esem, self.ecnt = {}, {}
        self.seen = {e: {} for e in self.engs}
        self.nsem = 0
        for e in self.engs:
            self._new_esem(e)
        self.dsems, self.dcnt = {}, {}
        self.out_tks = []

    def _mksem(self, name):
        self.nsem += 1
        return self.stack.enter_context(self.nc.semaphore(f"{name}_{self.nsem}"))

    def _new_esem(self, e):
        self.esem[e] = self._mksem("e" + e)
        self.ecnt[e] = 0

    def sbuf(self, name, shape, dt):
        return self.stack.enter_context(self.nc.sbuf_tensor(name, shape, dt))

    def psum(self, name, shape, dt):
        return self.stack.enter_context(self.nc.psum_tensor(name, shape, dt))

    def _wait(self, eng, deps):
        E = self.engs[eng]
        seen = self.seen[eng]
        for (sem, val) in deps:
            k = id(sem)
            if seen.get(k, 0) < val:
                E.wait_ge(sem, val)
                seen[k] = val

    def _deps(self, eng, reads, writes, is_dma):
        reads, writes = _flat(reads), _flat(writes)
        deps = set()
        own = None if is_dma else id(self.esem[eng])
        for b in reads:
            if b.lw is not None and not (eng == "pe" and not is_dma and id(b.lw[0]) == own):
                deps.add(b.lw)
        for b in writes:
            if b.lw is not None and id(b.lw[0]) != own:
                deps.add(b.lw)
            for sem_id, tk in b.rd.items():
                if sem_id != own:
                    deps.add(tk)
        return deps

    def _record(self, tk, reads, writes):
        reads, writes = _flat(reads), _flat(writes)
        k = id(tk[0])
        for b in reads:
            if b.rd.get(k, (None, 0))[1] < tk[1]:
                b.rd[k] = tk
        for b in writes:
            b.lw = tk
            b.rd = {}

    def op(self, eng, fn, reads=(), writes=()):
        self._wait(eng, self._deps(eng, reads, writes, False))
        inst = fn(self.engs[eng])
        if self.ecnt[eng] >= self.EPOCH:
            self._new_esem(eng)
        self.ecnt[eng] += 1
        tk = (self.esem[eng], self.ecnt[eng])
        inst.then_inc(tk[0], 1)
        self._record(tk, reads, writes)
        return tk

    def dma(self, q, fn, reads=(), writes=(), nsem=8, is_out=False):
        self._wait(q, self._deps(q, reads, writes, True))
        if q not in self.dsems:
            self.dsems[q] = [self._mksem("d" + q) for _ in range(nsem)]
            self.dcnt[q] = 0
        i = self.dcnt[q]
        self.dcnt[q] += 1
        sems = self.dsems[q]
        sem = sems[i % len(sems)]
        tk = (sem, 16 * (i // len(sems) + 1))
        fn(self.engs[q]).then_inc(sem, 16)
        self._record(tk, reads, writes)
        if is_out:
            self.out_tks.append(tk)
        return tk


class T:
    def __init__(self, t, nb=1):
        self.t = t
        self.bs = [Buf() for _ in range(nb)]
        self.b = self.bs[0]


class _Stop(Exception):
    pass


def build(nc, L, NPG, NPH, STOP=999, DO_SAMPLE=True):
    def stage(n):
        if n >= STOP:
            raise _Stop()

    NT = L // TT
    NSUB = L // 128

    def din(name, shape, dt=F32):
        return nc.dram_tensor(name, shape, dt, kind="ExternalInput").ap()

    def dout(name, shape, dt=F32):
        return nc.dram_tensor(name, shape, dt, kind="ExternalOutput").ap()

    xp = din("xp", [L, D])
    memp = din("memp", [256, D])
    norm_g = din("norm_g", [DEPTH, D])
    w_in = din("w_in", [DEPTH, D, 3072])
    w_out = din("w_out", [DEPTH, D, D])
    gng = din("group_norm_g", [DEPTH, 4, 256])
    lam_re = din("ssm_lambda_re", [DEPTH, 16, 64])
    lam_im = din("ssm_lambda_im", [DEPTH, 16, 64])
    b_re = din("ssm_b_re", [DEPTH, 16, 64, 16])
    b_im = din("ssm_b_im", [DEPTH, 16, 64, 16])
    c_re = din("ssm_c_re", [DEPTH, 16, 16, 64])
    c_im = din("ssm_c_im", [DEPTH, 16, 16, 64])
    log_dt = din("ssm_log_dt", [DEPTH, 16])
    ssm_d = din("ssm_d", [DEPTH, 256])
    w_glu = din("ssm_w_glu", [DEPTH, 256, 256])
    conv_w = din("conv_w", [DEPTH, 3, 256])
    sb_bias = din("sb_bias", [DEPTH, 4])
    w_mem = din("w_mem_kv", [DEPTH, D, 512])
    fin_g = din("final_norm_g", [D])

    y_p = dout("y_p", [L, D])
    sbk_p = dout("sbk_p", [DEPTH, L, 256])
    sbv_p = dout("sbv_p", [DEPTH, L, 256])
    ssmre_p = dout("ssmre_p", [DEPTH, 16, 64])
    ssmim_p = dout("ssmim_p", [DEPTH, 16, 64])
    conv_p = dout("conv_p", [DEPTH, 2, 256])
    memk_p = dout("memk_p", [DEPTH, 256, 256])
    memv_p = dout("memv_p", [DEPTH, 256, 256])
    x1 = nc.dram_tensor("x1_scratch", [L, D], F32, kind="Internal").ap()
    xs = din("xs", [16, D])
    ck = din("ck", [NPH * 2 * 128, 256])
    cv = din("cv", [NPH * 2 * 128, 256])
    sre = din("sre", [4, DEPTH, 16, 64])
    sim = din("sim", [4, DEPTH, 16, 64])
    sconv = din("sconv", [4, DEPTH, 2, 256])
    cmk = din("cmk", [4, DEPTH, 256, 256])
    cmv = din("cmv", [4, DEPTH, 256, 256])
    ptab = din("ptab", [4 * NPG], I32)
    y_s = dout("y_s", [16, D])
    sbk_s = dout("sbk_s", [4, DEPTH, 4, 256])
    sbv_s = dout("sbv_s", [4, DEPTH, 4, 256])
    ssmre_s = dout("ssmre_s", [4, DEPTH, 16, 64])
    ssmim_s = dout("ssmim_s", [4, DEPTH, 16, 64])
    conv_s = dout("conv_s", [4, DEPTH, 2, 256])

    with ExitStack() as st:
        p = Prog(nc, st)

        def sb(name, shape, dt=F32, nb=1):
            return T(p.sbuf(name, shape, dt), nb)

        def ps(name, shape, dt=F32):
            return T(p.psum(name, shape, dt))

        x1bufs = [Buf() for _ in range(NT)]

        def act(out, in_, func, R, W, bias=None, scale=None):
            kw = {}
            if bias is not None:
                kw["bias"] = bias
            if scale is not None:
                kw["scale"] = scale
            return p.op("act", lambda e: e.activation(out=out, in_=in_, func=func, **kw), R, W)

        def tt(eng, out, a, b, op, R, W):
            return p.op(eng, lambda e: e.tensor_tensor(out=out, in0=a, in1=b, op=op), R, W)

        def ts(eng, out, a, s1, op0, R, W, s2=None, op1=None):
            if op1 is None:
                return p.op(eng, lambda e: e.tensor_scalar(out=out, in0=a, scalar1=s1, scalar2=None, op0=op0), R, W)
            return p.op(eng, lambda e: e.tensor_scalar(out=out, in0=a, scalar1=s1, scalar2=s2, op0=op0, op1=op1), R, W)

        def stt(out, a, s, b, op0, op1, R, W):
            return p.op("dve", lambda e: e.scalar_tensor_tensor(out=out, in0=a, scalar=s, in1=b, op0=op0, op1=op1), R, W)

        def cp(eng, out, in_, R, W):
            if eng == "act":
                return act(out, in_, AF.Copy, R, W)
            return p.op(eng, lambda e: e.tensor_copy(out=out, in_=in_), R, W)

        def mset(eng, ap, v, W):
            return p.op(eng, lambda e: e.memset(ap, v), (), W)

        def mm(out, lhsT, rhs, start, stop, R, W):
            return p.op("pe", lambda e: e.matmul(out, lhsT=lhsT, rhs=rhs, start=start, stop=stop), R, W)

        def tr(out, in_, ident, R, W):
            return p.op("pe", lambda e: e.transpose(out=out, in_=in_, identity=ident), R, W)

        def dma(q, out, in_, R, W, slow=False, is_out=False):
            if slow:
                return p.dma(q, lambda e: e.dma_start(out=out, in_=in_, allow_slow_non_contiguous=True), R, W, is_out=is_out)
            return p.dma(q, lambda e: e.dma_start(out=out, in_=in_), R, W, is_out=is_out)

        def scan(out, d0, d1, init, R, W):
            return p.op("dve", lambda e: e.tensor_tensor_scan(out=out, data0=d0, data1=d1, initial=init,
                                                               op0=ALU.mult, op1=ALU.add), R, W)

        identf = sb("identf", [128, 128])
        identb = sb("identb", [128, 128], BF16)
        onesf = sb("onesf", [128, 128])
        epsT = sb("epsT", [128, 1])
        onec = sb("onec", [128, 1])
        mset("pool", identf.t[:], 1.0, [identf.b])
        p.op("pool", lambda e: e.affine_select(out=identf.t[:], in_=identf.t[:], pattern=[[-1, 128]],
                                               compare_op=ALU.is_equal, fill=0.0, base=0, channel_multiplier=1),
             [identf.b], [identf.b])
        cp("dve", identb.t[:], identf.t[:], [identf.b], [identb.b])
        mset("dve", onesf.t[:], 1.0, [onesf.b])
        mset("dve", epsT.t[:], EPS, [epsT.b])
        mset("dve", onec.t[:], 1.0, [onec.b])

        win = sb("win", [128, 8, 3072], BF16, 48)

        def wb(kc, c0, c1):
            return [win.bs[kc * 6 + c] for c in range(c0 // 512, (c1 - 1) // 512 + 1)]

        wout = sb("wout", [128, 8, 1024], BF16, 16)
        wglu = sb("wglu", [128, 2, 256], BF16)
        gt = sb("gt", [128, 1024])
        gnAB = sb("gnAB", [128, 4])
        gnCM = sb("gnCM", [128, 2, 256])
        dvec = sb("dvec", [128, 2])
        cw = sb("cw", [128, 2, 3])
        sbb = sb("sbb", [128, 4])
        KcT = sb("KcT", [128, 2, L], BF16, NSUB)
        Vc = sb("Vc", [128, NSUB, 256], BF16, NSUB)
        cosT = sb("cosT", [128, 8, TT])
        sinT = sb("sinT", [128, 8, TT])
        rsp = sb("rsp", [128, 8])
        BtR = sb("BtR", [128, 8, 128], BF16)
        BtI = sb("BtI", [128, 8, 128], BF16)
        CtR = sb("CtR", [128, 8, 128], BF16)
        CtI = sb("CtI", [128, 8, 128], BF16)
        memT = sb("memT", [128, 8, 256], BF16)
        mkT = sb("mkT", [128, 2, 256], BF16)
        mvA = sb("mvA", [128, 2, 4, 65], BF16)
        hst = sb("hst", [128, 8, 2])
        xt = sb("xt", [128, NS, 1024])
        xnT = sb("xnT", [128, 8, TT], BF16)
        mergedT = sb("mergedT", [128, 8, TT], BF16, 8)
        au_f = sb("au_f", [128, 2, TT])
        au_b = sb("au_b", [128, 2, TT], BF16)
        sgA = sb("sgA", [128, 2, TT])
        sgB = sb("sgB", [128, 2, TT])
        bbx = sb("bbx", [128, 6, TT])
        qT = sb("qT", [128, 2, TT], BF16)
        mqT = sb("mqT", [128, 2, TT], BF16)
        sgC = sb("sgC", [128, NS, 256])
        sgM = sb("sgM", [128, NS, 256])
        vbuf = sb("vbuf", [128, 2, TT + 2])
        wk = [sb(f"wk{i}", [128, 1024]) for i in range(8)]
        wkb = [sb(f"wkb{i}", [128, 1024], BF16) for i in range(4)]
        sm = [sb(f"sm{i}", [128, 16]) for i in range(6)]
        for t_ in (wk[0], wk[2], wkb[0], wkb[1], wkb[2], wkb[3]):
            t_.bl, t_.br = Buf(), Buf()
            t_.b = BufGroup([t_.bl, t_.br])
            t_.bs = [t_.b]
        pmm = [ps(f"pmm{i}", [128, 512]) for i in range(2)]
        ptr = T(p.psum("ptr", [128, 1024], BF16), 2)
        pss = [ps(f"pss{i}", [128, 512]) for i in range(2)]
        po = ps("po", [128, 512])
        po2 = ps("po2", [128, 512])
        py = ps("py", [128, 2, 256])

        def rstd_of(out, in_, n, R, W):
            act(out, in_, AF.Ln, R + [epsT.b], W, bias=epsT.t[:in_.shape[0], 0:1], scale=1.0 / n)
            act(out, out, AF.Exp, W, W, scale=-0.5)

        def sincos(ang, angb, N, sin_out, sin_b, cos_out, cos_b, tmpf, tmpi):
            for (o_, ob, shift) in ((sin_out, sin_b, 0.0), (cos_out, cos_b, 0.25)):
                for c0 in range(0, N, 512):
                    n_ = min(512, N - c0)
                    o = o_[:, c0:c0 + n_]
                    ts("dve", tmpf.t[:, 0:n_], ang[:, c0:c0 + n_], 1.0 / TWO_PI, ALU.mult, [angb], [tmpf.b], s2=shift, op1=ALU.add)
                    cp("dve", tmpi.t[:, 0:n_], tmpf.t[:, 0:n_], [tmpf.b], [tmpi.b])
                    cp("dve", o, tmpi.t[:, 0:n_], [tmpi.b], [ob])
                    tt("dve", tmpf.t[:, 0:n_], tmpf.t[:, 0:n_], o, ALU.subtract, [tmpf.b, ob], [tmpf.b])
                    act(o, tmpf.t[:, 0:n_], AF.Sin, [tmpf.b], [ob], scale=TWO_PI * (1.0 - 1e-6))

        tmpi = sb("tmpi", [128, 512], I32)

        def prep_mem():
            for mc in range(2):
                dma("sp", wk[0].t[:, :], memp[mc * 128:(mc + 1) * 128, :], [], [wk[0].b])
                cp("dve", wkb[0].t[:, :], wk[0].t[:, :], [wk[0].b], [wkb[0].b])
                for kc in range(8):
                    tr(ptr.t[:, kc * 128:(kc + 1) * 128], wkb[0].t[:, kc * 128:(kc + 1) * 128], identb.t[:],
                       [wkb[0].b, identb.b], ptr.bs)
                cp("act", memT.t[:, :, mc * 128:(mc + 1) * 128],
                   ptr.t[:, :].rearrange("p (k m) -> p k m", k=8), ptr.bs, [memT.b])


        NQ = 16
        xs_t = sb("xs_t", [NQ, 1024])
        idxT = sb("idxT", [128, 4 * NPG], I32)
        iot = sb("iot", [128, 1], I32)
        mask16 = sb("mask16", [NQ, 4])
        sbias16 = sb("sbias16", [NQ, 1])
        s_xnT = sb("s_xnT", [128, 8, NQ], BF16)
        s_auf = sb("s_auf", [128, 2, NQ])
        s_aub = sb("s_aub", [128, 2, NQ], BF16)
        s_sgA = sb("s_sgA", [128, 2, NQ])
        s_sgB = sb("s_sgB", [128, 2, NQ])
        s_bbx = sb("s_bbx", [128, 6, NQ])
        s_qT = sb("s_qT", [128, 2, NQ], BF16)
        s_kT = sb("s_kT", [128, 2, NQ], BF16)
        s_mqT = sb("s_mqT", [128, 2, NQ], BF16)
        s_sgC = sb("s_sgC", [NQ, 256])
        s_sgM = sb("s_sgM", [NQ, 256])
        s_vbn = sb("s_vbn", [4, 4, 256], BF16)
        s_vbuf = sb("s_vbuf", [128, 2, 4, 6])
        s_hst = sb("s_hst", [128, 8, 4, 2])
        s_mT = sb("s_mT", [128, 8, NQ], BF16)
        s_yc = sb("s_yc", [NQ, 256])
        s_ym = sb("s_ym", [NQ, 256])
        qbd = sb("qbd", [128, 2, NQ], BF16)
        s_ncar = sb("s_ncar", [NQ, 1])
        Kb = sb("Kb", [128, 2, 4, 256], BF16, 2)
        Vb = sb("Vb", [128, 1, 4, 256], BF16, 1)

        def colv(ap1d):
            return ap1d.rearrange("(p o) -> p o", o=1)

        def v4(ap):
            return ap.rearrange("p (n t) -> p n t", n=4)

        def sample_setup():
            dma("sp", xs_t.t[:, :], xs, [], [xs_t.b])
            dma("sp", idxT.t[:, :], ptab.partition_broadcast(128), [], [idxT.b])
            pi = sm[5]
            pI = tmpi
            p.op("pool", lambda e: e.iota(pI.t[:NQ, 0:1], pattern=[[0, 1]], base=0, channel_multiplier=1), [], [pI.b])
            p.op("dve", lambda e: e.tensor_single_scalar(out=pI.t[:NQ, 0:1], in_=pI.t[:NQ, 0:1], scalar=3, op=ALU.bitwise_and),
                 [pI.b], [pI.b])
            cp("dve", pi.t[:NQ, 0:1], pI.t[:NQ, 0:1], [pI.b], [pi.b])
            p.op("pool", lambda e: e.iota(pI.t[:NQ, 8:12], pattern=[[1, 4]], base=0, channel_multiplier=0), [pI.b], [pI.b])
            cp("dve", pi.t[:NQ, 4:8], pI.t[:NQ, 8:12], [pI.b], [pi.b])
            ts("dve", mask16.t[:, :], pi.t[:NQ, 4:8], pi.t[:NQ, 0:1], ALU.is_lt, [pi.b], [mask16.b])

        def s_merge_fm(br, y_, sg):
            sq = wk[7]
            pm = pmm[0]
            for hf in range(2):
                tt("dve", sq.t[:, hf * NQ:(hf + 1) * NQ], y_.t[:, hf * NQ:(hf + 1) * NQ], y_.t[:, hf * NQ:(hf + 1) * NQ], ALU.mult,
                   [y_.b], [sq.b])
            for hf in range(2):
                mm(pm.t[:, 0:NQ], onesf.t[:, :], sq.t[:, hf * NQ:(hf + 1) * NQ], hf == 0, hf == 1, [onesf.b, sq.b], [pm.b])
            rs = wk[6]
            rstd_of(rs.t[:, 0:NQ], pm.t[:, 0:NQ], 256.0, [pm.b], [rs.b])
            for hf in range(2):
                stt(sq.t[:, hf * NQ:(hf + 1) * NQ], y_.t[:, hf * NQ:(hf + 1) * NQ], gnAB.t[:, br * 2 + hf:br * 2 + hf + 1],
                    rs.t[:, 0:NQ], ALU.mult, ALU.mult, [y_.b, gnAB.b, rs.b], [sq.b])
                tt("dve", s_mT.t[:, br * 2 + hf, :], sq.t[:, hf * NQ:(hf + 1) * NQ], sg.t[:, hf, :], ALU.mult, [sq.b, sg.b], [s_mT.b])

        def s_merge_tm(idx, y_, sg, mtm):
            sq = wk[7]
            ssq = sm[1]
            tt("dve", sq.t[:NQ, 0:256], y_.t[:, :], y_.t[:, :], ALU.mult, [y_.b], [sq.b])
            p.op("dve", lambda e: e.reduce_sum(out=ssq.t[:NQ, 0:1], in_=sq.t[:NQ, 0:256], axis=AX.X), [sq.b], [ssq.b])
            rstd_of(ssq.t[:NQ, 0:1], ssq.t[:NQ, 0:1], 256.0, [ssq.b], [ssq.b])
            stt(sq.t[:NQ, 0:256], y_.t[:, :], ssq.t[:NQ, 0:1], gnCM.t[:NQ, idx, :], ALU.mult, ALU.mult, [y_.b, ssq.b, gnCM.b], [sq.b])
            tt("dve", mtm.t[:NQ, idx * 256:(idx + 1) * 256], sq.t[:NQ, 0:256], sg.t[:, :], ALU.mult, [sq.b, sg.b], [mtm.b])

        def sample_layer(l):
            if l == 0:
                p.op("pool", lambda e: e.iota(iot.t[:, 0:1], pattern=[[0, 1]], base=0, channel_multiplier=1), [], [iot.b])
                ts("dve", idxT.t[:, :], idxT.t[:, :], 256, ALU.mult, [idxT.b, iot.b], [idxT.b], s2=iot.t[:, 0:1], op1=ALU.add)
            else:
                ts("dve", idxT.t[:, :], idxT.t[:, :], 128, ALU.add, [idxT.b], [idxT.b])
            for h in range(4):
                dma("sp", sbias16.t[h * 4:(h + 1) * 4, 0:1], sb_bias[l, h:h + 1].partition_broadcast(4), [], [sbias16.b], slow=True)
            sq, ss = wk[0], sm[0]
            tt("dve", sq.t[:NQ, :], xs_t.t[:, :], xs_t.t[:, :], ALU.mult, [xs_t.b], [sq.b])
            p.op("dve", lambda e: e.reduce_sum(out=ss.t[:NQ, 0:1], in_=sq.t[:NQ, :], axis=AX.X), [sq.b], [ss.b])
            rstd_of(ss.t[:NQ, 0:1], ss.t[:NQ, 0:1], 1024.0, [ss.b], [ss.b])
            xnb = wkb[0]
            stt(xnb.t[:NQ, :], xs_t.t[:, :], ss.t[:NQ, 0:1], gt.t[:NQ, :], ALU.mult, ALU.mult, [xs_t.b, ss.b, gt.b], [xnb.b])
            for kc in range(8):
                tr(ptr.t[:, kc * NQ:(kc + 1) * NQ], xnb.t[:NQ, kc * 128:(kc + 1) * 128], identb.t[:NQ, :NQ], [xnb.b, identb.b], ptr.bs)
            cp("act", s_xnT.t[:, :, :], ptr.t[:, 0:8 * NQ].rearrange("p (k m) -> p k m", k=8), ptr.bs, [s_xnT.b])

            def sproj(j, evac):
                pm = pmm[j % 2]
                for kc in range(8):
                    mm(pm.t[:, 0:NQ], win.t[:, kc, j * 128:(j + 1) * 128], s_xnT.t[:, kc, :], kc == 0, kc == 7,
                       wb(kc, j * 128, (j + 1) * 128) + [s_xnT.b], [pm.b])
                evac(pm)

            for hf in range(2):
                def ev(pm, hf=hf):
                    cp("act", s_auf.t[:, hf, :], pm.t[:, 0:NQ], [pm.b], [s_auf.b])
                    cp("act", s_aub.t[:, hf, :], pm.t[:, 0:NQ], [pm.b], [s_aub.b])
                sproj(0 + hf, ev)
                sproj(2 + hf, lambda pm, hf=hf: act(s_sgA.t[:, hf, :], pm.t[:, 0:NQ], AF.Silu, [pm.b], [s_sgA.b]))
                for bi in range(3):
                    sproj(4 + 2 * bi + hf, lambda pm, hf=hf, bi=bi: cp("act", s_bbx.t[:, 2 * bi + hf, :], pm.t[:, 0:NQ], [pm.b], [s_bbx.b]))
                sproj(10 + hf, lambda pm, hf=hf: act(s_sgB.t[:, hf, :], pm.t[:, 0:NQ], AF.Silu, [pm.b], [s_sgB.b]))
                sproj(12 + hf, lambda pm, hf=hf: cp("act", s_qT.t[:, hf, :], pm.t[:, 0:NQ], [pm.b], [s_qT.b]))
                sproj(14 + hf, lambda pm, hf=hf: cp("act", s_kT.t[:, hf, :], pm.t[:, 0:NQ], [pm.b], [s_kT.b]))
                sproj(20 + hf, lambda pm, hf=hf: cp("act", s_mqT.t[:, hf, :], pm.t[:, 0:NQ], [pm.b], [s_mqT.b]))
            for (c0, dst) in ((2304, s_sgC), (2816, s_sgM)):
                pm = pmm[0]
                for kc in range(8):
                    mm(pm.t[:NQ, 0:256], s_xnT.t[:, kc, :], win.t[:, kc, c0:c0 + 256], kc == 0, kc == 7,
                       wb(kc, c0, c0 + 256) + [s_xnT.b], [pm.b])
                act(dst.t[:, :], pm.t[:NQ, 0:256], AF.Silu, [pm.b], [dst.b])
            for n in range(4):
                pm = pmm[n % 2]
                for kc in range(8):
                    mm(pm.t[:4, 0:512], s_xnT.t[:, kc, n * 4:(n + 1) * 4], win.t[:, kc, 1792:2304], kc == 0, kc == 7,
                       wb(kc, 1792, 2304) + [s_xnT.b], [pm.b])
                kvf = wk[1 + n % 2]
                cp("act", kvf.t[:4, 0:512], pm.t[:4, :], [pm.b], [kvf.b])
                dma("sp", sbk_s[n, l], kvf.t[:4, 0:256], [kvf.b], [Buf()], is_out=True)
                dma("sp", sbv_s[n, l], kvf.t[:4, 256:512], [kvf.b], [Buf()], is_out=True)
                cp("act", s_vbn.t[:, n, :], kvf.t[:4, 256:512], [kvf.b], [s_vbn.b])

            for hf in range(2):
                for n in range(4):
                    for j in range(2):
                        dma("sp", s_vbuf.t[:, hf, n, j:j + 1], colv(sconv[n, l, j, hf * 128:(hf + 1) * 128]), [], [s_vbuf.b], slow=True)
            yb = wk[3]
            for hf in range(2):
                tt("dve", s_vbuf.t[:, hf, :, 2:6], v4(s_bbx.t[:, 2 + hf, :]), v4(s_bbx.t[:, 4 + hf, :]), ALU.mult, [s_bbx.b], [s_vbuf.b])
                acc = yb.t[:, hf * NQ:(hf + 1) * NQ]
                ts("dve", v4(acc), s_vbuf.t[:, hf, :, 2:6], cw.t[:, hf, 2:3], ALU.mult, [s_vbuf.b, cw.b], [yb.b])
                stt(v4(acc), s_vbuf.t[:, hf, :, 1:5], cw.t[:, hf, 1:2], v4(acc), ALU.mult, ALU.add, [s_vbuf.b, cw.b, yb.b], [yb.b])
                stt(v4(acc), s_vbuf.t[:, hf, :, 0:4], cw.t[:, hf, 0:1], v4(acc), ALU.mult, ALU.add, [s_vbuf.b, cw.b, yb.b], [yb.b])
                tt("dve", acc, acc, s_bbx.t[:, 0 + hf, :], ALU.mult, [yb.b, s_bbx.b], [yb.b])
            for hf in range(2):
                for n in range(4):
                    for j in range(2):
                        dma("sp", colv(conv_s[n, l, j, hf * 128:(hf + 1) * 128]), s_vbuf.t[:, hf, n, 4 + j:5 + j], [s_vbuf.b], [Buf()],
                            slow=True, is_out=True)
            s_merge_fm(1, yb, s_sgB)

            for n in range(4):
                for gi in range(2):
                    dma("sp", s_hst.t[gi * 64:(gi + 1) * 64, :, n, 0], sre[n, l].rearrange("(gp gi) p -> gi p gp", gi=2)[gi], [], [s_hst.b], slow=True)
                    dma("sp", s_hst.t[gi * 64:(gi + 1) * 64, :, n, 1], sim[n, l].rearrange("(gp gi) p -> gi p gp", gi=2)[gi], [], [s_hst.b], slow=True)
            ya = wk[3]
            yab = wkb[2]
            for gp in range(8):
                hf = gp // 4
                pb = pss[gp % 2]
                mm(pb.t[:, 0:NQ], BtR.t[:, gp, :], s_aub.t[:, hf, :], True, True, [BtR.b, s_aub.b], [pb.b])
                mm(pb.t[:, NQ:2 * NQ], BtI.t[:, gp, :], s_aub.t[:, hf, :], True, True, [BtI.b, s_aub.b], [pb.b])
                c_ = cosT.t[:, gp, 0:4].unsqueeze(1).to_broadcast([128, 4, 4])
                s_ = sinT.t[:, gp, 0:4].unsqueeze(1).to_broadcast([128, 4, 4])
                w = wk[4]
                q1, q2, q3, q4 = (w.t[:, i * NQ:(i + 1) * NQ] for i in range(4))
                bre_, bim_ = v4(pb.t[:, 0:NQ]), v4(pb.t[:, NQ:2 * NQ])
                tt("dve", v4(q1), bre_, c_, ALU.mult, [pb.b, cosT.b], [w.b])
                tt("dve", v4(q2), bim_, s_, ALU.mult, [pb.b, sinT.b], [w.b])
                tt("dve", v4(q3), bim_, c_, ALU.mult, [pb.b, cosT.b], [w.b])
                tt("dve", v4(q4), bre_, s_, ALU.mult, [pb.b, sinT.b], [w.b])
                tt("dve", q1, q1, q2, ALU.add, [w.b], [w.b])
                tt("dve", q3, q3, q4, ALU.subtract, [w.b], [w.b])
                g_ = wk[5]
                gre, gim = g_.t[:, 0:NQ], g_.t[:, NQ:2 * NQ]
                rb = rsp.t[:, gp:gp + 1].to_broadcast([128, 4])
                for n in range(4):
                    scan(gre[:, n * 4:(n + 1) * 4], rb, q1[:, n * 4:(n + 1) * 4], s_hst.t[:, gp, n, 0:1], [w.b, rsp.b, s_hst.b], [g_.b])
                    scan(gim[:, n * 4:(n + 1) * 4], rb, q3[:, n * 4:(n + 1) * 4], s_hst.t[:, gp, n, 1:2], [w.b, rsp.b, s_hst.b], [g_.b])
                tt("dve", v4(q1), v4(gre), c_, ALU.mult, [g_.b, cosT.b], [w.b])
                tt("dve", v4(q2), v4(gim), s_, ALU.mult, [g_.b, sinT.b], [w.b])
                tt("dve", v4(q3), v4(gim), c_, ALU.mult, [g_.b, cosT.b], [w.b])
                tt("dve", v4(q4), v4(gre), s_, ALU.mult, [g_.b, sinT.b], [w.b])
                hb = wkb[3]
                hR, hI = hb.t[:, 0:NQ], hb.t[:, NQ:2 * NQ]
                tt("dve", hR, q1, q2, ALU.subtract, [w.b], [hb.b])
                tt("dve", hI, q3, q4, ALU.add, [w.b], [hb.b])
                tt("dve", s_hst.t[:, gp, :, 0], v4(q1)[:, :, 3], v4(q2)[:, :, 3], ALU.subtract, [w.b], [s_hst.b])
                tt("dve", s_hst.t[:, gp, :, 1], v4(q3)[:, :, 3], v4(q4)[:, :, 3], ALU.add, [w.b], [s_hst.b])
                mm(py.t[:, hf, 0:NQ], CtR.t[:, gp, :], hR, gp % 4 == 0, False, [CtR.b, hb.b], [py.b])
                mm(py.t[:, hf, 0:NQ], CtI.t[:, gp, :], hI, False, gp % 4 == 3, [CtI.b, hb.b], [py.b])
                if gp % 4 == 3:
                    yh = ya.t[:, hf * NQ:(hf + 1) * NQ]
                    stt(yh, s_auf.t[:, hf, :], dvec.t[:, hf:hf + 1], py.t[:, hf, 0:NQ], ALU.mult, ALU.add, [s_auf.b, dvec.b, py.b], [ya.b])
                    act(yh, yh, AF.Gelu, [ya.b], [ya.b])
                    cp("dve", yab.t[:, hf * NQ:(hf + 1) * NQ], yh, [ya.b], [yab.b])
            for n in range(4):
                for gi in range(2):
                    dma("sp", ssmre_s[n, l].rearrange("(gp gi) p -> gi p gp", gi=2)[gi], s_hst.t[gi * 64:(gi + 1) * 64, :, n, 0], [s_hst.b], [Buf()],
                        slow=True, is_out=True)
                    dma("sp", ssmim_s[n, l].rearrange("(gp gi) p -> gi p gp", gi=2)[gi], s_hst.t[gi * 64:(gi + 1) * 64, :, n, 1], [s_hst.b], [Buf()],
                        slow=True, is_out=True)
            for oc in range(2):
                pm = pmm[oc]
                for k2 in range(2):
                    mm(pm.t[:, 0:NQ], wglu.t[:, k2, oc * 128:(oc + 1) * 128], yab.t[:, k2 * NQ:(k2 + 1) * NQ], k2 == 0, k2 == 1,
                       [wglu.b, yab.b], [pm.b])
                sg_ = wk[5]
                act(sg_.t[:, 0:NQ], pm.t[:, 0:NQ], AF.Sigmoid, [pm.b], [sg_.b])
                tt("dve", ya.t[:, oc * NQ:(oc + 1) * NQ], ya.t[:, oc * NQ:(oc + 1) * NQ], sg_.t[:, 0:NQ], ALU.mult, [ya.b, sg_.b], [ya.b])
            s_merge_fm(0, ya, s_sgA)

            NB = NPG // 4
            ncar = s_ncar
            o16 = wk[7]
            RO = 512
            E_s, Eb = wk[0].t, wk[0].br
            L_s, Lbb = wk[2].t, wk[2].br
            P_s = wk[6]
            kTs = [(wkb[1].t, wkb[1].br), (wkb[3].t, wkb[3].br)]
            Wb_s, Wbb = wkb[2].t, wkb[2].br
            WT_s, WTb = wkb[0].t, wkb[0].br
            pK, pS = pmm[0], pmm[1]
            pK16 = pK.t[:, :].bitcast(BF16)
            pS16 = pS.t[:, :].bitcast(BF16)

            def gatherK(n, kb, slot):
                for pg in range(4):
                    col_ = n * NPG + kb * 4 + pg
                    off = bass.IndirectOffsetOnAxis(ap=idxT.t[:, col_:col_ + 1], axis=0)
                    p.dma("pool", lambda e, off=off, pg=pg: e.indirect_dma_start(out=Kb.t[:, slot, pg, :], out_offset=None, in_=ck, in_offset=off),
                          [idxT.b], [Kb.bs[slot]])

            def gatherV(n, kb):
                for pg in range(4):
                    col_ = n * NPG + kb * 4 + pg
                    off = bass.IndirectOffsetOnAxis(ap=idxT.t[:, col_:col_ + 1], axis=0)
                    p.dma("pool", lambda e, off=off, pg=pg: e.indirect_dma_start(out=Vb.t[:, 0, pg, :], out_offset=None, in_=cv, in_offset=off),
                          [idxT.b], [Vb.bs[0]])

            def tail_steps(ncol, kw, nch, vfn, first, last):
                def t_L():
                    mset("dve", P_s.t[:NQ, 0:1], 0.0, [P_s.b])
                    act(L_s[:NQ, RO:RO + ncol], E_s[:NQ, RO:RO + ncol], AF.Ln, [Eb], [Lbb], bias=1.0)

                def t_scan():
                    scan(P_s.t[:NQ, 1:ncol + 1], onec.t[:NQ, 0:1].to_broadcast([NQ, ncol]), L_s[:NQ, RO:RO + ncol], 0.0,
                         [onec.b, Lbb], [P_s.b])
                    tt("dve", ncar.t[:NQ, 0:1], ncar.t[:NQ, 0:1], P_s.t[:NQ, ncol:ncol + 1], ALU.subtract, [ncar.b, P_s.b], [ncar.b])

                def t_X():
                    act(P_s.t[:NQ, 0:ncol], P_s.t[:NQ, 0:ncol], AF.Exp, [P_s.b, ncar.b], [P_s.b], bias=ncar.t[:NQ, 0:1])

                def t_W():
                    tt("dve", Wb_s[:NQ, RO:RO + ncol], E_s[:NQ, RO:RO + ncol], P_s.t[:NQ, 0:ncol], ALU.mult, [Eb, P_s.b], [Wbb])

                def t_tr():
                    for c4 in range(nch):
                        tr(pS16[:kw, c4 * NQ:(c4 + 1) * NQ], Wb_s[:NQ, RO + c4 * kw:RO + (c4 + 1) * kw], identb.t[:NQ, :NQ],
                           [Wbb, identb.b], [pS.b])

                def t_ev():
                    cp("act", WT_s[:kw, RO:RO + nch * NQ], pS16[:kw, 0:nch * NQ], [pS.b], [WTb])

                def t_pv():
                    for c4 in range(nch):
                        vap, vbufs = vfn(c4)
                        mm(py.t[:NQ, 0, :], WT_s[:kw, RO + c4 * NQ:RO + (c4 + 1) * NQ], vap, first and c4 == 0, last and c4 == nch - 1,
                           [WTb] + vbufs, [py.b])
                return [t_L, t_scan, t_X, t_W, t_tr, t_ev, t_pv]

            def new_steps(n):
                def a_S():
                    for hc in range(2):
                        mm(pS.t[:NQ, 0:4], qbd.t[:, hc, :], s_kT.t[:, hc, n * 4:(n + 1) * 4], hc == 0, hc == 1, [qbd.b, s_kT.b], [pS.b])

                def a_E():
                    act(E_s[:NQ, RO:RO + 4], pS.t[:NQ, 0:4], AF.Exp, [pS.b, sbias16.b], [Eb], bias=sbias16.t[:, 0:1], scale=0.125)
                    tt("dve", E_s[:NQ, RO:RO + 4], E_s[:NQ, RO:RO + 4], mask16.t[:, :], ALU.mult, [Eb, mask16.b], [Eb])
                return [a_S, a_E] + tail_steps(4, 4, 1, lambda c4: (s_vbn.t[:, n, :], [s_vbn.b]), True, NB == 0)

            def past_steps(n, kb):
                slot = kb % 2

                def b_kt():
                    if kb > 0:
                        gatherK(n, kb - 1, (kb - 1) % 2)
                    for pg in range(4):
                        for hc in range(2):
                            tr(pK16[:, (hc * 4 + pg) * 128:(hc * 4 + pg + 1) * 128], Kb.t[:, slot, pg, hc * 128:(hc + 1) * 128], identb.t[:, :],
                               [Kb.bs[slot], identb.b], [pK.b])

                def b_ktev():
                    for hc in range(2):
                        cp("act", kTs[hc][0][:, RO:RO + 512], pK16[:, hc * 512:(hc + 1) * 512], [pK.b], [kTs[hc][1]])

                def b_S():
                    for hc in range(2):
                        mm(pS.t[:NQ, 0:512], qbd.t[:, hc, :], kTs[hc][0][:, RO:RO + 512], hc == 0, hc == 1, [qbd.b, kTs[hc][1]], [pS.b])

                def b_E():
                    act(E_s[:NQ, RO:RO + 512], pS.t[:NQ, 0:512], AF.Exp, [pS.b, sbias16.b], [Eb], bias=sbias16.t[:, 0:1], scale=0.125)

                def b_post():
                    if kb > 0:
                        gatherV(n, kb - 1)
                return ([b_kt, b_ktev, b_S, b_E]
                        + tail_steps(512, 128, 4, lambda c4: (Vb.t[:, 0, c4, :], [Vb.bs[0]]), False, kb == 0) + [b_post])

            for n in range(4):
                mset("pool", qbd.t[:, :, :].rearrange("p a b -> p (a b)"), 0.0, [qbd.b])
                for h in range(4):
                    pr, hc = (h % 2) * 64, h // 2
                    cp("pool", qbd.t[pr:pr + 64, hc, h * 4:(h + 1) * 4], s_qT.t[pr:pr + 64, hc, n * 4:(n + 1) * 4], [s_qT.b], [qbd.b])
                mset("dve", ncar.t[:NQ, 0:1], 0.0, [ncar.b])
                if NB > 0:
                    gatherK(n, NB - 1, (NB - 1) % 2)
                    gatherV(n, NB - 1)
                yield new_steps(n)
                for kb in range(NB - 1, -1, -1):
                    yield past_steps(n, kb)
                cp("act", o16.t[:NQ, 0:256], py.t[:NQ, 0, :], [py.b], [o16.b])
                for h in range(4):
                    dma("sp", s_yc.t[n * 4:(n + 1) * 4, h * 64:(h + 1) * 64], o16.t[h * 4:(h + 1) * 4, h * 64:(h + 1) * 64], [o16.b], [s_yc.b])
            mtm = wkb[0]
            s_merge_tm(0, s_yc, s_sgC, mtm)

            for n in range(4):
                mkf, mvf = wk[0], wk[1]
                dma("sp", mkf.t[:, 0:512].rearrange("p (c d) -> p c d", c=2), cmk[n, l].rearrange("(c p) d -> p c d", p=128), [], [mkf.b])
                dma("sp", mvf.t[:, 0:512].rearrange("p (c d) -> p c d", c=2), cmv[n, l].rearrange("(c p) d -> p c d", p=128), [], [mvf.b])
                mkb = wkb[1]
                cp("dve", mkb.t[:, 0:512], mkf.t[:, 0:512], [mkf.b], [mkb.b])
                for mc in range(2):
                    for hc in range(2):
                        tr(ptr.t[:, (hc * 2 + mc) * 128:(hc * 2 + mc + 1) * 128], mkb.t[:, mc * 256 + hc * 128:mc * 256 + (hc + 1) * 128],
                           identb.t[:, :], [mkb.b, identb.b], ptr.bs)
                mkTs = wkb[2]
                cp("act", mkTs.t[:, 0:512], ptr.t[:, 0:512], ptr.bs, [mkTs.b])
                mvAs = wkb[3]
                mset("dve", mvAs.t[:, 0:520], 1.0, [mvAs.b])
                cp("dve", mvAs.t[:, 0:520].rearrange("p (c h e) -> p c h e", c=2, h=4)[:, :, :, 0:64],
                   mvf.t[:, 0:512].rearrange("p (c h d) -> p c h d", c=2, h=4), [mvf.b], [mvAs.b])
                for h in range(4):
                    pr, hc = (h % 2) * 64, h // 2
                    S = pss[h % 2]
                    mm(S.t[:4, 0:256], s_mqT.t[pr:pr + 64, hc, n * 4:(n + 1) * 4], mkTs.t[pr:pr + 64, hc * 256:(hc + 1) * 256], True, True,
                       [s_mqT.b, mkTs.b], [S.b])
                    mx = sm[3]
                    p.op("dve", lambda e: e.reduce_max(out=mx.t[:4, h:h + 1], in_=S.t[:4, 0:256], axis=AX.X), [S.b], [mx.b])
                    ts("dve", mx.t[:4, h:h + 1], mx.t[:4, h:h + 1], -0.125, ALU.mult, [mx.b], [mx.b])
                    Pm = wkb[0]
                    act(Pm.t[:4, 512:768], S.t[:4, 0:256], AF.Exp, [S.b, mx.b], [Pm.b], bias=mx.t[:4, h:h + 1], scale=0.125)
                    for mc in range(2):
                        tr(ptr.t[:, 512 + mc * 4:512 + (mc + 1) * 4], Pm.t[:4, 512 + mc * 128:512 + (mc + 1) * 128], identb.t[:4, :4],
                           [Pm.b, identb.b], ptr.bs)
                    PT = wkb[0]
                    cp("act", PT.t[:, 768:776], ptr.t[:, 512:520], ptr.bs, [PT.b])
                    for mc in range(2):
                        mm(po2.t[:4, h * 65:(h + 1) * 65], PT.t[:, 768 + mc * 4:768 + (mc + 1) * 4],
                           mvAs.t[:, 0:520].rearrange("p (c h e) -> p c h e", c=2, h=4)[:, mc, h, :], mc == 0, mc == 1,
                           [PT.b, mvAs.b], [po2.b])
                ym4 = wk[5]
                rd = sm[4]
                for h in range(4):
                    p.op("dve", lambda e: e.reciprocal(out=rd.t[:4, h:h + 1], in_=po2.t[:4, h * 65 + 64:h * 65 + 65]), [po2.b], [rd.b])
                    ts("dve", ym4.t[:4, h * 64:(h + 1) * 64], po2.t[:4, h * 65:h * 65 + 64], rd.t[:4, h:h + 1], ALU.mult, [po2.b, rd.b], [ym4.b])
                dma("sp", s_ym.t[n * 4:(n + 1) * 4, :], ym4.t[:4, 0:256], [ym4.b], [s_ym.b])
            s_merge_tm(1, s_ym, s_sgM, mtm)
            for c in range(4):
                tr(ptr.t[:, c * NQ:(c + 1) * NQ], mtm.t[:NQ, c * 128:(c + 1) * 128], identb.t[:NQ, :NQ], [mtm.b, identb.b], ptr.bs)
            cp("act", s_mT.t[:, 4:8, :], ptr.t[:, 0:4 * NQ].rearrange("p (k m) -> p k m", k=4), ptr.bs, [s_mT.b])
            for oc in range(2):
                pm = pmm[oc]
                for kc in range(8):
                    mm(pm.t[:NQ, :], s_mT.t[:, kc, :], wout.t[:, kc, oc * 512:(oc + 1) * 512], kc == 0, kc == 7,
                       [s_mT.b, wout.bs[kc * 2 + oc]], [pm.b])
                tt("dve", xs_t.t[:, oc * 512:(oc + 1) * 512], xs_t.t[:, oc * 512:(oc + 1) * 512], pm.t[:NQ, :], ALU.add, [xs_t.b, pm.b], [xs_t.b])
            if l == DEPTH - 1:
                fg = wk[5]
                dma("sp", fg.t[:, :], fin_g.partition_broadcast(128), [], [fg.b])
                tt("dve", sq.t[:NQ, :], xs_t.t[:, :], xs_t.t[:, :], ALU.mult, [xs_t.b], [sq.b])
                p.op("dve", lambda e: e.reduce_sum(out=ss.t[:NQ, 0:1], in_=sq.t[:NQ, :], axis=AX.X), [sq.b], [ss.b])
                rstd_of(ss.t[:NQ, 0:1], ss.t[:NQ, 0:1], 1024.0, [ss.b], [ss.b])
                yo = wk[1]
                stt(yo.t[:NQ, :], xs_t.t[:, :], ss.t[:NQ, 0:1], fg.t[:NQ, :], ALU.mult, ALU.mult, [xs_t.b, ss.b, fg.b], [yo.b])
                dma("sp", y_s, yo.t[:NQ, :], [yo.b], [Buf()], is_out=True)

        def body():
          prep_mem()
          sample_setup()
          stage(1)
          for l in range(DEPTH):
              for kc in range(8):
                  for c3 in range(6):
                      dma("pool", win.t[:, kc, c3 * 512:(c3 + 1) * 512], w_in[l, kc * 128:(kc + 1) * 128, c3 * 512:(c3 + 1) * 512],
                          [], [win.bs[kc * 6 + c3]])
              for kc in range(8):
                  for c2 in range(2):
                      dma("pool", wout.t[:, kc, c2 * 512:(c2 + 1) * 512], w_out[l, kc * 128:(kc + 1) * 128, c2 * 512:(c2 + 1) * 512],
                          [], [wout.bs[kc * 2 + c2]])
              for k2 in range(2):
                  dma("pool", wglu.t[:, k2, :], w_glu[l, k2 * 128:(k2 + 1) * 128, :], [], [wglu.b])
              wdeps = set(b.lw for b in win.bs + wout.bs + [wglu.b])
              p._wait("dve", wdeps)
              p._wait("act", wdeps)
              dma("sp", gt.t[:, :], norm_g[l].partition_broadcast(128), [], [gt.b])
              def col(ap1d):
                  return ap1d.rearrange("(p o) -> p o", o=1)
              for g_ in range(2):
                  for h_ in range(2):
                      dma("sp", gnAB.t[:, g_ * 2 + h_:g_ * 2 + h_ + 1], col(gng[l, g_, h_ * 128:(h_ + 1) * 128]), [], [gnAB.b], slow=True)
              dma("sp", gnCM.t[:, :, :], gng[l, 2:4, :].partition_broadcast(128), [], [gnCM.b])
              for h_ in range(2):
                  dma("sp", dvec.t[:, h_:h_ + 1], col(ssm_d[l, h_ * 128:(h_ + 1) * 128]), [], [dvec.b], slow=True)
                  for j_ in range(3):
                      dma("sp", cw.t[:, h_, j_:j_ + 1], col(conv_w[l, j_, h_ * 128:(h_ + 1) * 128]), [], [cw.b], slow=True)
              dma("sp", sbb.t[:, :], sb_bias[l].partition_broadcast(128), [], [sbb.b])

              stage(2)
              lr, li, a1, th, er, cs, sn, tf = wk[0], wk[1], wk[2], wk[3], wk[4], wk[5], wk[6], wk[7]
              ldt = sm[0]
              dma("sp", lr.t[:, :], lam_re[l].rearrange("g p -> (g p)").partition_broadcast(128), [], [lr.b])
              dma("sp", li.t[:, :], lam_im[l].rearrange("g p -> (g p)").partition_broadcast(128), [], [li.b])
              dma("sp", ldt.t[:, :], log_dt[l].partition_broadcast(128), [], [ldt.b])
              act(ldt.t[:, :], ldt.t[:, :], AF.Exp, [ldt.b], [ldt.b])
              dtb = ldt.t[:, :].unsqueeze(2).to_broadcast([128, 16, 64])

              def v3(x):
                  return x.t[:, :].rearrange("p (g s) -> p g s", g=16)

              tt("dve", v3(a1), v3(lr), dtb, ALU.mult, [lr.b, ldt.b], [a1.b])
              tt("dve", v3(th), v3(li), dtb, ALU.mult, [li.b, ldt.b], [th.b])
              act(er.t[:, :], a1.t[:, :], AF.Exp, [a1.b], [er.b])
              sincos(th.t[:, :], th.b, 1024, sn.t[:, :], sn.b, cs.t[:, :], cs.b, tf, tmpi)
              tt("dve", cs.t[:, :], cs.t[:, :], er.t[:, :], ALU.mult, [cs.b, er.b], [cs.b])
              ts("dve", cs.t[:, :], cs.t[:, :], -1.0, ALU.add, [cs.b], [cs.b])
              tt("dve", sn.t[:, :], sn.t[:, :], er.t[:, :], ALU.mult, [sn.b, er.b], [sn.b])
              tt("dve", er.t[:, :], lr.t[:, :], lr.t[:, :], ALU.mult, [lr.b], [er.b])
              tt("dve", a1.t[:, :], li.t[:, :], li.t[:, :], ALU.mult, [li.b], [a1.b])
              tt("dve", er.t[:, :], er.t[:, :], a1.t[:, :], ALU.add, [er.b, a1.b], [er.b])
              p.op("dve", lambda e: e.reciprocal(out=er.t[:, :], in_=er.t[:, :]), [er.b], [er.b])
              tt("dve", a1.t[:, :], cs.t[:, :], lr.t[:, :], ALU.mult, [cs.b, lr.b], [a1.b])
              tt("dve", tf.t[:, :], sn.t[:, :], li.t[:, :], ALU.mult, [sn.b, li.b], [tf.b])
              tt("dve", a1.t[:, :], a1.t[:, :], tf.t[:, :], ALU.add, [a1.b, tf.b], [a1.b])
              tt("dve", a1.t[:, :], a1.t[:, :], er.t[:, :], ALU.mult, [a1.b, er.b], [a1.b])
              tt("dve", th.t[:, :], sn.t[:, :], lr.t[:, :], ALU.mult, [sn.b, lr.b], [th.b])
              tt("dve", tf.t[:, :], cs.t[:, :], li.t[:, :], ALU.mult, [cs.b, li.b], [tf.b])
              tt("dve", th.t[:, :], th.t[:, :], tf.t[:, :], ALU.subtract, [th.b, tf.b], [th.b])
              tt("dve", th.t[:, :], th.t[:, :], er.t[:, :], ALU.mult, [th.b, er.b], [th.b])
              cre, cim = a1, th
              BfR, BfI = wk[0], wk[1]
              mset("dve", BfR.t[:, :], 0.0, [BfR.b])
              mset("dve", BfI.t[:, :], 0.0, [BfI.b])
              for g in range(16):
                  r0 = (g % 8) * 16
                  c0 = (g // 2) * 128 + (g % 2) * 64
                  dma("sp", BfR.t[r0:r0 + 16, c0:c0 + 64], b_re[l, g].rearrange("p c -> c p"), [], [BfR.b], slow=True)
                  dma("sp", BfI.t[r0:r0 + 16, c0:c0 + 64], b_im[l, g].rearrange("p c -> c p"), [], [BfI.b], slow=True)
              t1, t2 = wk[4], wk[5]
              tt("dve", t1.t[:, :], cre.t[:, :], BfR.t[:, :], ALU.mult, [cre.b, BfR.b], [t1.b])
              tt("dve", t2.t[:, :], cim.t[:, :], BfI.t[:, :], ALU.mult, [cim.b, BfI.b], [t2.b])
              tt("dve", BtR.t[:, :, :].rearrange("p g s -> p (g s)"), t1.t[:, :], t2.t[:, :], ALU.subtract,
                 [t1.b, t2.b], [BtR.b])
              tt("dve", t1.t[:, :], cre.t[:, :], BfI.t[:, :], ALU.mult, [cre.b, BfI.b], [t1.b])
              tt("dve", t2.t[:, :], cim.t[:, :], BfR.t[:, :], ALU.mult, [cim.b, BfR.b], [t2.b])
              tt("dve", BtI.t[:, :, :].rearrange("p g s -> p (g s)"), t1.t[:, :], t2.t[:, :], ALU.add,
                 [t1.b, t2.b], [BtI.b])
              CfR, CfI = wk[6], wk[7]
              mset("dve", CfR.t[:, :], 0.0, [CfR.b])
              mset("dve", CfI.t[:, :], 0.0, [CfI.b])
              for g in range(16):
                  gp, gi = g // 2, g % 2
                  c0 = gp * 128 + (gp % 4) * 32 + gi * 16
                  dma("sp", CfR.t[gi * 64:gi * 64 + 64, c0:c0 + 16], c_re[l, g].rearrange("c p -> p c"), [], [CfR.b], slow=True)
                  dma("sp", CfI.t[gi * 64:gi * 64 + 64, c0:c0 + 16], c_im[l, g].rearrange("c p -> p c"), [], [CfI.b], slow=True)
              cp("dve", CtR.t[:, :, :].rearrange("p g s -> p (g s)"), CfR.t[:, :], [CfR.b], [CtR.b])
              ts("dve", CtI.t[:, :, :].rearrange("p g s -> p (g s)"), CfI.t[:, :], -1.0, ALU.mult, [CfI.b], [CtI.b])
              lrs, lis, dts, ths = sm[1], sm[2], sm[3], sm[4]
              for gi in range(2):
                  dma("sp", lrs.t[gi * 64:(gi + 1) * 64, 0:8], lam_re[l].rearrange("(gp gi) p -> gi p gp", gi=2)[gi], [], [lrs.b], slow=True)
                  dma("sp", lis.t[gi * 64:(gi + 1) * 64, 0:8], lam_im[l].rearrange("(gp gi) p -> gi p gp", gi=2)[gi], [], [lis.b], slow=True)
              ldt2 = log_dt[l].rearrange("(gp gi) -> gi gp", gi=2)
              for gi in range(2):
                  dma("sp", dts.t[gi * 64:(gi + 1) * 64, 0:8], ldt2[gi].partition_broadcast(64), [], [dts.b], slow=True)
              act(dts.t[:, 0:8], dts.t[:, 0:8], AF.Exp, [dts.b], [dts.b])
              tt("dve", lrs.t[:, 0:8], lrs.t[:, 0:8], dts.t[:, 0:8], ALU.mult, [lrs.b, dts.b], [lrs.b])
              act(rsp.t[:, :], lrs.t[:, 0:8], AF.Exp, [lrs.b], [rsp.b])
              tt("dve", ths.t[:, 0:8], lis.t[:, 0:8], dts.t[:, 0:8], ALU.mult, [lis.b, dts.b], [ths.b])
              tpos_i = tmpi
              p.op("pool", lambda e: e.iota(tpos_i.t[:, 0:TT], pattern=[[1, TT]], base=1, channel_multiplier=0),
                   [], [tpos_i.b])
              tpos = wk[4]
              cp("dve", tpos.t[:, 0:TT], tpos_i.t[:, 0:TT], [tpos_i.b], [tpos.b])
              ang = wk[5]
              for half in range(2):
                  for g4 in range(4):
                      gp = half * 4 + g4
                      ts("dve", ang.t[:, g4 * TT:(g4 + 1) * TT], tpos.t[:, 0:TT], ths.t[:, gp:gp + 1], ALU.mult,
                         [tpos.b, ths.b], [ang.b])
                  sincos(ang.t[:, 0:4 * TT], ang.b, 4 * TT,
                         sinT.t[:, half * 4:(half + 1) * 4, :].rearrange("p g t -> p (g t)"), sinT.b,
                         cosT.t[:, half * 4:(half + 1) * 4, :].rearrange("p g t -> p (g t)"), cosT.b, wk[6], tmpi)

              stage(3)
              wmem = wkb[0]
              wm = [wkb[0], wkb[1], wkb[2], wkb[3]]
              for kc in range(8):
                  dma("pool", wm[kc // 2].t[:, (kc % 2) * 512:(kc % 2 + 1) * 512], w_mem[l, kc * 128:(kc + 1) * 128, :],
                      [], [wm[kc // 2].b])

              def wmk(kc, c0, c1):
                  return wm[kc // 2].t[:, (kc % 2) * 512 + c0:(kc % 2) * 512 + c1]

              mset("dve", mvA.t[:, :, :, :].rearrange("p a b c -> p (a b c)"), 1.0, [mvA.b])
              for mc in range(2):
                  pm = pmm[mc]
                  for kc in range(8):
                      mm(pm.t[:, :], memT.t[:, kc, mc * 128:(mc + 1) * 128], wmk(kc, 0, 512), kc == 0, kc == 7,
                         [memT.b, wm[kc // 2].b], [pm.b])
                  kvf = wk[mc]
                  cp("act", kvf.t[:, 0:512], pm.t[:, :], [pm.b], [kvf.b])
                  dma("sp", memk_p[l, mc * 128:(mc + 1) * 128, :], kvf.t[:, 0:256], [kvf.b], [Buf()], is_out=True)
                  dma("sp", memv_p[l, mc * 128:(mc + 1) * 128, :], kvf.t[:, 256:512], [kvf.b], [Buf()], is_out=True)
                  cp("dve", mvA.t[:, mc, :, 0:64], kvf.t[:, 256:512].rearrange("p (h d) -> p h d", h=4), [kvf.b], [mvA.b])
              for c in range(2):
                  pm = pmm[c]
                  for kc in range(8):
                      mm(pm.t[:, 0:256], wmk(kc, c * 128, (c + 1) * 128), memT.t[:, kc, :], kc == 0, kc == 7,
                         [memT.b, wm[kc // 2].b], [pm.b])
                  cp("act", mkT.t[:, c, :], pm.t[:, 0:256], [pm.b], [mkT.b])

              stage(4)
              sgen = sample_layer(l) if DO_SAMPLE else iter(())
              mset("dve", hst.t[:, :, :].rearrange("p a b -> p (a b)"), 0.0, [hst.b])
              mset("dve", vbuf.t[:, :, 0:2], 0.0, [vbuf.b])

              for t in range(NT):
                  tok0 = t * TT
                  src = xp if l == 0 else x1
                  dma("sp", xt.t[:, :, :], src[tok0:tok0 + TT, :].rearrange("(s p) d -> p s d", p=128),
                      [x1bufs[t]] if l == 1 else [], [xt.b])
                  ss = sm[0]
                  for s in range(NS):
                      tt("dve", wk[0].t[:, :], xt.t[:, s, :], xt.t[:, s, :], ALU.mult, [xt.b], [wk[0].b])
                      p.op("dve", lambda e: e.reduce_sum(out=ss.t[:, s:s + 1], in_=wk[0].t[:, :], axis=AX.X),
                           [wk[0].b], [ss.b])
                  rstd_of(ss.t[:, 0:NS], ss.t[:, 0:NS], 1024.0, [ss.b], [ss.b])
                  for s in range(NS):
                      xnb = wkb[s]
                      stt(xnb.t[:, :], xt.t[:, s, :], ss.t[:, s:s + 1], gt.t[:, :], ALU.mult, ALU.mult,
                          [xt.b, ss.b, gt.b], [xnb.b])
                      for kc in range(8):
                          tr(ptr.t[:, kc * 128:(kc + 1) * 128], xnb.t[:, kc * 128:(kc + 1) * 128], identb.t[:],
                             [xnb.b, identb.b], ptr.bs)
                      cp("act", xnT.t[:, :, s * 128:(s + 1) * 128], ptr.t[:, :].rearrange("p (k m) -> p k m", k=8),
                         ptr.bs, [xnT.b])

                  stage(5)
                  def proj_fm(j, evac):
                      pm = pmm[j % 2]
                      for kc in range(8):
                          mm(pm.t[:, 0:TT], win.t[:, kc, j * 128:(j + 1) * 128], xnT.t[:, kc, :], kc == 0, kc == 7,
                             wb(kc, j * 128, (j + 1) * 128) + [xnT.b], [pm.b])
                          stage(5.01 + kc * 0.001)
                      stage(5.05)
                      evac(pm)

                  for hf in range(2):
                      def ev(pm, hf=hf):
                          cp("act", au_f.t[:, hf, :], pm.t[:, 0:TT], [pm.b], [au_f.b])
                          stage(5.06)
                          cp("act", au_b.t[:, hf, :], pm.t[:, 0:TT], [pm.b], [au_b.b])
                      proj_fm(0 + hf, ev)
                      stage(5.1)
                      proj_fm(2 + hf, lambda pm, hf=hf: act(sgA.t[:, hf, :], pm.t[:, 0:TT], AF.Silu, [pm.b], [sgA.b]))
                      stage(5.2)
                      for bi in range(3):
                          proj_fm(4 + 2 * bi + hf, lambda pm, hf=hf, bi=bi: cp("act", bbx.t[:, 2 * bi + hf, :], pm.t[:, 0:TT], [pm.b], [bbx.b]))
                      proj_fm(10 + hf, lambda pm, hf=hf: act(sgB.t[:, hf, :], pm.t[:, 0:TT], AF.Silu, [pm.b], [sgB.b]))
                      proj_fm(12 + hf, lambda pm, hf=hf: cp("act", qT.t[:, hf, :], pm.t[:, 0:TT], [pm.b], [qT.b]))

                      def evk(pm, hf=hf):
                          for s in range(NS):
                              cp("act", KcT.t[:, hf, tok0 + s * 128:tok0 + (s + 1) * 128], pm.t[:, s * 128:(s + 1) * 128],
                                 [pm.b], [KcT.bs[t * NS + s]])
                      proj_fm(14 + hf, evk)
                      proj_fm(20 + hf, lambda pm, hf=hf: cp("act", mqT.t[:, hf, :], pm.t[:, 0:TT], [pm.b], [mqT.b]))
                  stage(5.3)
                  for s in range(NS):
                      gs = t * NS + s
                      pm = pmm[s]
                      for kc in range(8):
                          mm(pm.t[:, :], xnT.t[:, kc, s * 128:(s + 1) * 128], win.t[:, kc, 1792:2304], kc == 0, kc == 7,
                             wb(kc, 1792, 2304) + [xnT.b], [pm.b])
                      kvf = wk[1 + s]
                      cp("act", kvf.t[:, 0:512], pm.t[:, :], [pm.b], [kvf.b])
                      r0 = tok0 + s * 128
                      dma("sp", sbk_p[l, r0:r0 + 128, :], kvf.t[:, 0:256], [kvf.b], [Buf()], is_out=True)
                      dma("sp", sbv_p[l, r0:r0 + 128, :], kvf.t[:, 256:512], [kvf.b], [Buf()], is_out=True)
                      cp("dve", Vc.t[:, gs, :], kvf.t[:, 256:512], [kvf.b], [Vc.bs[gs]])
                      stage(5.4)
                      for (c0, dst) in ((2304, sgC), (2816, sgM)):
                          pm2 = pmm[1 - s]
                          for kc in range(8):
                              mm(pm2.t[:, 0:256], xnT.t[:, kc, s * 128:(s + 1) * 128], win.t[:, kc, c0:c0 + 256],
                                 kc == 0, kc == 7, wb(kc, c0, c0 + 256) + [xnT.b], [pm2.b])
                          act(dst.t[:, s, :], pm2.t[:, 0:256], AF.Silu, [pm2.b], [dst.b])

                  stage(6)
                  def merge_fm(br, yv, yb_, sg):
                      sq = wk[7]
                      pm = pmm[0]
                      for hf in range(2):
                          tt("dve", sq.t[:, hf * TT:(hf + 1) * TT], yv(hf), yv(hf), ALU.mult, [yb_], [sq.b])
                      for hf in range(2):
                          mm(pm.t[:, 0:TT], onesf.t[:, :], sq.t[:, hf * TT:(hf + 1) * TT], hf == 0, hf == 1,
                             [onesf.b, sq.b], [pm.b])
                      rs = wk[6]
                      rstd_of(rs.t[:, 0:TT], pm.t[:, 0:TT], 256.0, [pm.b], [rs.b])
                      for hf in range(2):
                          stt(sq.t[:, hf * TT:(hf + 1) * TT], yv(hf), gnAB.t[:, br * 2 + hf:br * 2 + hf + 1], rs.t[:, 0:TT],
                              ALU.mult, ALU.mult, [yb_, gnAB.b, rs.b], [sq.b])
                          tt("dve", mergedT.t[:, br * 2 + hf, :], sq.t[:, hf * TT:(hf + 1) * TT], sg.t[:, hf, :], ALU.mult,
                             [sq.b, sg.b], [mergedT.bs[br * 2 + hf]])

                  yb = wk[3]
                  for hf in range(2):
                      tt("dve", vbuf.t[:, hf, 2:TT + 2], bbx.t[:, 2 + hf, :], bbx.t[:, 4 + hf, :], ALU.mult, [bbx.b], [vbuf.b])
                      acc = yb.t[:, hf * TT:(hf + 1) * TT]
                      ts("dve", acc, vbuf.t[:, hf, 2:TT + 2], cw.t[:, hf, 2:3], ALU.mult, [vbuf.b, cw.b], [yb.b])
                      stt(acc, vbuf.t[:, hf, 1:TT + 1], cw.t[:, hf, 1:2], acc, ALU.mult, ALU.add, [vbuf.b, cw.b, yb.b], [yb.b])
                      stt(acc, vbuf.t[:, hf, 0:TT], cw.t[:, hf, 0:1], acc, ALU.mult, ALU.add, [vbuf.b, cw.b, yb.b], [yb.b])
                      tt("dve", acc, acc, bbx.t[:, 0 + hf, :], ALU.mult, [yb.b, bbx.b], [yb.b])
                  if t == NT - 1:
                      for h_ in range(2):
                          for j_ in range(2):
                              dma("sp", col(conv_p[l, j_, h_ * 128:(h_ + 1) * 128]), vbuf.t[:, h_, TT + j_:TT + j_ + 1], [vbuf.b], [Buf()],
                                  slow=True, is_out=True)
                  for hf in range(2):
                      cp("dve", vbuf.t[:, hf, 0:2], vbuf.t[:, hf, TT:TT + 2], [vbuf.b], [vbuf.b])
                  merge_fm(1, lambda hf: yb.t[:, hf * TT:(hf + 1) * TT], yb.b, sgB)

                  stage(7)
                  ya = wk[3]
                  yab = wkb[2]
                  F4 = 4 * TT
                  for hf in range(2):
                      pre, pim = pss[0], pss[1]
                      for g4 in range(4):
                          gp = hf * 4 + g4
                          mm(pre.t[:, g4 * TT:(g4 + 1) * TT], BtR.t[:, gp, :], au_b.t[:, hf, :], True, True, [BtR.b, au_b.b], [pre.b])
                          mm(pim.t[:, g4 * TT:(g4 + 1) * TT], BtI.t[:, gp, :], au_b.t[:, hf, :], True, True, [BtI.b, au_b.b], [pim.b])
                      c_ = cosT.t[:, hf * 4:(hf + 1) * 4, :].rearrange("p g t -> p (g t)")
                      s_ = sinT.t[:, hf * 4:(hf + 1) * 4, :].rearrange("p g t -> p (g t)")
                      wa, wb_ = wk[0], wk[1]
                      q1, q3 = wa.t[:, 0:F4], wa.t[:, F4:2 * F4]
                      q2, q4 = wb_.t[:, 0:F4], wb_.t[:, F4:2 * F4]
                      tt("dve", q1, pre.t[:, 0:F4], c_, ALU.mult, [pre.b, cosT.b], [wa.b])
                      tt("dve", q2, pim.t[:, 0:F4], s_, ALU.mult, [pim.b, sinT.b], [wb_.b])
                      tt("dve", q3, pim.t[:, 0:F4], c_, ALU.mult, [pim.b, cosT.b], [wa.b])
                      tt("dve", q4, pre.t[:, 0:F4], s_, ALU.mult, [pre.b, sinT.b], [wb_.b])
                      tt("dve", q1, q1, q2, ALU.add, [wa.b, wb_.b], [wa.b])
                      tt("dve", q3, q3, q4, ALU.subtract, [wa.b, wb_.b], [wa.b])
                      g_ = wk[2]
                      gre, gim = g_.t[:, 0:F4], g_.t[:, F4:2 * F4]
                      for g4 in range(4):
                          gp = hf * 4 + g4
                          rb = rsp.t[:, gp:gp + 1].to_broadcast([128, TT])
                          sl = slice(g4 * TT, (g4 + 1) * TT)
                          scan(gre[:, sl], rb, q1[:, sl], hst.t[:, gp, 0:1], [wa.b, rsp.b, hst.b], [g_.b])
                          scan(gim[:, sl], rb, q3[:, sl], hst.t[:, gp, 1:2], [wa.b, rsp.b, hst.b], [g_.b])
                      tt("dve", q1, gre, c_, ALU.mult, [g_.b, cosT.b], [wa.b])
                      tt("pool", q2, gim, s_, ALU.mult, [g_.b, sinT.b], [wb_.b])
                      tt("dve", q3, gim, c_, ALU.mult, [g_.b, cosT.b], [wa.b])
                      tt("pool", q4, gre, s_, ALU.mult, [g_.b, sinT.b], [wb_.b])
                      hb = wkb[3]
                      hR, hI = hb.t[:, 0:F4], hb.t[:, F4:2 * F4]
                      tt("dve", hR, q1, q2, ALU.subtract, [wa.b, wb_.b], [hb.b])
                      tt("dve", hI, q3, q4, ALU.add, [wa.b, wb_.b], [hb.b])

                      def lastc(q):
                          return q.rearrange("p (g t) -> p g t", g=4)[:, :, TT - 1]
                      tt("dve", hst.t[:, hf * 4:(hf + 1) * 4, 0], lastc(q1), lastc(q2), ALU.subtract, [wa.b, wb_.b], [hst.b])
                      tt("dve", hst.t[:, hf * 4:(hf + 1) * 4, 1], lastc(q3), lastc(q4), ALU.add, [wa.b, wb_.b], [hst.b])
                      for g4 in range(4):
                          gp = hf * 4 + g4
                          sl = slice(g4 * TT, (g4 + 1) * TT)
                          mm(po.t[:, 256 + hf * TT:256 + (hf + 1) * TT], CtR.t[:, gp, :], hR[:, sl], g4 == 0, False, [CtR.b, hb.b], [po.b])
                          mm(po.t[:, 256 + hf * TT:256 + (hf + 1) * TT], CtI.t[:, gp, :], hI[:, sl], False, g4 == 3, [CtI.b, hb.b], [po.b])
                      yh = ya.t[:, hf * TT:(hf + 1) * TT]
                      stt(yh, au_f.t[:, hf, :], dvec.t[:, hf:hf + 1], po.t[:, 256 + hf * TT:256 + (hf + 1) * TT], ALU.mult, ALU.add,
                          [au_f.b, dvec.b, po.b], [ya.b])
                      act(yh, yh, AF.Gelu, [ya.b], [ya.b])
                      cp("dve", yab.t[:, hf * TT:(hf + 1) * TT], yh, [ya.b], [yab.b])
                  if t == NT - 1:
                      for gi in range(2):
                          dma("sp", ssmre_p[l].rearrange("(gp gi) p -> gi p gp", gi=2)[gi], hst.t[gi * 64:(gi + 1) * 64, :, 0], [hst.b], [Buf()],
                              slow=True, is_out=True)
                          dma("sp", ssmim_p[l].rearrange("(gp gi) p -> gi p gp", gi=2)[gi], hst.t[gi * 64:(gi + 1) * 64, :, 1], [hst.b], [Buf()],
                              slow=True, is_out=True)
                  for oc in range(2):
                      pm = pmm[oc]
                      for k2 in range(2):
                          mm(pm.t[:, 0:TT], wglu.t[:, k2, oc * 128:(oc + 1) * 128], yab.t[:, k2 * TT:(k2 + 1) * TT],
                             k2 == 0, k2 == 1, [wglu.b, yab.b], [pm.b])
                      sg_ = wk[5]
                      act(sg_.t[:, 0:TT], pm.t[:, 0:TT], AF.Sigmoid, [pm.b], [sg_.b])
                      tt("dve", ya.t[:, oc * TT:(oc + 1) * TT], ya.t[:, oc * TT:(oc + 1) * TT], sg_.t[:, 0:TT], ALU.mult,
                         [ya.b, sg_.b], [ya.b])
                  merge_fm(0, lambda hf: ya.t[:, hf * TT:(hf + 1) * TT], ya.b, sgA)

                  stage(8)
                  def merge_tm(idx, yv, yb_, sg, s, mtm):
                      sq = wk[7]
                      ssq = sm[1]
                      tt("dve", sq.t[:, 0:256], yv, yv, ALU.mult, [yb_], [sq.b])
                      p.op("dve", lambda e: e.reduce_sum(out=ssq.t[:, 0:1], in_=sq.t[:, 0:256], axis=AX.X), [sq.b], [ssq.b])
                      rstd_of(ssq.t[:, 0:1], ssq.t[:, 0:1], 256.0, [ssq.b], [ssq.b])
                      stt(sq.t[:, 0:256], yv, ssq.t[:, 0:1], gnCM.t[:, idx, :], ALU.mult, ALU.mult, [yb_, ssq.b, gnCM.b], [sq.b])
                      tt("dve", mtm.t[:, idx * 256:(idx + 1) * 256], sq.t[:, 0:256], sg.t[:, s, :], ALU.mult, [sq.b, sg.b], [mtm.b])

                  for s in range(NS):
                      gq = t * NS + s
                      nkeys = (gq + 1) * 128
                      nblk = (nkeys + 511) // 512
                      mtm = wkb[0]
                      ncars = [sm[2], sm[5]]
                      for nc_ in ncars:
                          mset("dve", nc_.t[:, 0:4], 0.0, [nc_.b])
                      Wbs, WTs = [wkb[1], wkb[3]], [wkb[2], wkb[0]]
                      first = True
                      for kb in range(nblk - 1, -1, -1):
                          ncol = nkeys - kb * 512 if kb == nblk - 1 else 512
                          nc4 = ncol // 128
                          kbufs = [KcT.bs[kb * 4 + c4] for c4 in range(nc4)]

                          def head_steps(h, par, kb=kb, ncol=ncol, nc4=nc4, kbufs=kbufs, first=first):
                              pr, hc = (h % 2) * 64, h // 2
                              S, E, Lb, P_ = pss[par], wk[0 + par], wk[2 + par], wk[4 + par]
                              ncar, Wb, WT = ncars[par], Wbs[par], WTs[par]
                              Eb_ = wk[0].bl if par == 0 else wk[1].b
                              Lbb_ = wk[2].bl if par == 0 else wk[3].b
                              Wbb_, WTb_ = Wb.bl, WT.bl
                              trp = ptr.t[:, 0:512] if par == 0 else po2.t[:, :].bitcast(BF16)[:, 0:512]
                              trb = ptr.bs if par == 0 else [po2.b]

                              def s_S():
                                  mm(S.t[:, 0:ncol], qT.t[pr:pr + 64, hc, s * 128:(s + 1) * 128],
                                     KcT.t[pr:pr + 64, hc, kb * 512:kb * 512 + ncol], True, True, [qT.b] + kbufs, [S.b])

                              def s_E():
                                  act(E.t[:, 0:ncol], S.t[:, 0:ncol], AF.Exp, [S.b, sbb.b], [Eb_], bias=sbb.t[:, h:h + 1], scale=0.125)
                                  if kb == nblk - 1:
                                      p.op("pool", lambda e: e.affine_select(out=E.t[:, ncol - 128:ncol], in_=E.t[:, ncol - 128:ncol],
                                                                             pattern=[[-1, 128]], compare_op=ALU.is_gt, fill=0.0,
                                                                             base=0, channel_multiplier=1), [Eb_], [Eb_])
                                  mset("dve", Lb.t[:, 0:1], 0.0, [Lbb_])

                              def s_L():
                                  act(Lb.t[:, 1:ncol + 1], E.t[:, 0:ncol], AF.Ln, [Eb_], [Lbb_], bias=1.0)

                              def s_scan():
                                  scan(P_.t[:, 0:ncol + 1], onec.t[:, 0:1].to_broadcast([128, ncol + 1]), Lb.t[:, 0:ncol + 1], 0.0,
                                       [onec.b, Lbb_], [P_.b])
                                  tt("dve", ncar.t[:, h:h + 1], ncar.t[:, h:h + 1], P_.t[:, ncol:ncol + 1], ALU.subtract,
                                     [ncar.b, P_.b], [ncar.b])

                              def s_X():
                                  act(P_.t[:, 0:ncol], P_.t[:, 0:ncol], AF.Exp, [P_.b, ncar.b], [P_.b], bias=ncar.t[:, h:h + 1])

                              def s_W():
                                  tt("dve", Wb.t[:, 0:ncol], E.t[:, 0:ncol], P_.t[:, 0:ncol], ALU.mult, [Eb_, P_.b], [Wbb_])

                              def s_tr():
                                  for c4 in range(nc4):
                                      tr(trp[:, c4 * 128:(c4 + 1) * 128], Wb.t[:, c4 * 128:(c4 + 1) * 128], identb.t[:],
                                         [Wbb_, identb.b], trb)

                              def s_ev():
                                  cp("act", WT.t[:, 0:ncol], trp[:, 0:ncol], trb, [WTb_])

                              def s_pv():
                                  for c4 in range(nc4):
                                      mm(po.t[:, h * 64:(h + 1) * 64], WT.t[:, c4 * 128:(c4 + 1) * 128],
                                         Vc.t[:, kb * 4 + c4, h * 64:(h + 1) * 64], first and c4 == 0 and h == 0 and par == 0, kb == 0 and c4 == nc4 - 1,
                                         [WTb_, Vc.bs[kb * 4 + c4]], [po.b])
                              return [s_S, s_E, s_L, s_scan, s_X, s_W, s_tr, s_ev, s_pv]

                          ssteps = next(sgen, None) or []
                          for hp in range(2):
                              sa, sb_ = head_steps(hp, 0), head_steps(hp + 2, 1)
                              third = ssteps if hp == 0 else []
                              for fs in zip_longest(third, sa, sb_):
                                  for f_ in fs:
                                      if f_ is not None:
                                          f_()
                          first = False
                      yc = wk[6]
                      cp("act", yc.t[:, 0:256], po.t[:, 0:256], [po.b], [yc.b])
                      merge_tm(0, yc.t[:, 0:256], yc.b, sgC, s, mtm)
                      stage(9)
                      for h in range(4):
                          pr, hc = (h % 2) * 64, h // 2
                          S = pss[h % 2]
                          mm(S.t[:, 0:256], mqT.t[pr:pr + 64, hc, s * 128:(s + 1) * 128], mkT.t[pr:pr + 64, hc, :], True, True,
                             [mqT.b, mkT.b], [S.b])
                          mx = sm[3]
                          p.op("dve", lambda e: e.reduce_max(out=mx.t[:, h:h + 1], in_=S.t[:, 0:256], axis=AX.X), [S.b], [mx.b])
                          ts("dve", mx.t[:, h:h + 1], mx.t[:, h:h + 1], -0.125, ALU.mult, [mx.b], [mx.b])
                          Pm = wkb[1]
                          act(Pm.t[:, 0:256], S.t[:, 0:256], AF.Exp, [S.b, mx.b], [Pm.b], bias=mx.t[:, h:h + 1], scale=0.125)
                          for mc in range(2):
                              tr(ptr.t[:, mc * 128:(mc + 1) * 128], Pm.t[:, mc * 128:(mc + 1) * 128], identb.t[:],
                                 [Pm.b, identb.b], ptr.bs)
                          PT = wkb[2]
                          cp("act", PT.t[:, 0:256], ptr.t[:, 0:256], ptr.bs, [PT.b])
                          for mc in range(2):
                              mm(po2.t[:, h * 65:(h + 1) * 65], PT.t[:, mc * 128:(mc + 1) * 128], mvA.t[:, mc, h, :],
                                 mc == 0, mc == 1, [PT.b, mvA.b], [po2.b])
                      ym = wk[6]
                      rd = sm[4]
                      for h in range(4):
                          p.op("dve", lambda e: e.reciprocal(out=rd.t[:, h:h + 1], in_=po2.t[:, h * 65 + 64:h * 65 + 65]),
                               [po2.b], [rd.b])
                          ts("dve", ym.t[:, 256 + h * 64:256 + (h + 1) * 64], po2.t[:, h * 65:h * 65 + 64], rd.t[:, h:h + 1],
                             ALU.mult, [po2.b, rd.b], [ym.b])
                      merge_tm(1, ym.t[:, 256:512], ym.b, sgM, s, mtm)
                      stage(10)
                      for c in range(4):
                          tr(ptr.t[:, c * 128:(c + 1) * 128], mtm.t[:, c * 128:(c + 1) * 128], identb.t[:], [mtm.b, identb.b], ptr.bs)
                      for c in range(4):
                          cp("act", mergedT.t[:, 4 + c, s * 128:(s + 1) * 128], ptr.t[:, c * 128:(c + 1) * 128], ptr.bs,
                             [mergedT.bs[4 + c]])

                  stage(11)
                  for s in range(NS):
                      for oc in range(2):
                          pm = pmm[oc]
                          for kc in range(8):
                              mm(pm.t[:, :], mergedT.t[:, kc, s * 128:(s + 1) * 128], wout.t[:, kc, oc * 512:(oc + 1) * 512],
                                 kc == 0, kc == 7, [mergedT.bs[kc], wout.bs[kc * 2 + oc]], [pm.b])
                          tt("dve", xt.t[:, s, oc * 512:(oc + 1) * 512], xt.t[:, s, oc * 512:(oc + 1) * 512], pm.t[:, :], ALU.add,
                             [xt.b, pm.b], [xt.b])
                  if l == 0:
                      dma("sp", x1[tok0:tok0 + TT, :].rearrange("(s p) d -> p s d", p=128), xt.t[:, :, :], [xt.b], [x1bufs[t]])
                  else:
                      fg = wk[5]
                      dma("sp", fg.t[:, :], fin_g.partition_broadcast(128), [], [fg.b])
                      ss = sm[0]
                      for s in range(NS):
                          tt("dve", wk[0].t[:, :], xt.t[:, s, :], xt.t[:, s, :], ALU.mult, [xt.b], [wk[0].b])
                          p.op("dve", lambda e: e.reduce_sum(out=ss.t[:, s:s + 1], in_=wk[0].t[:, :], axis=AX.X),
                               [wk[0].b], [ss.b])
                      rstd_of(ss.t[:, 0:NS], ss.t[:, 0:NS], 1024.0, [ss.b], [ss.b])
                      for s in range(NS):
                          yo = wk[1 + s]
                          stt(yo.t[:, :], xt.t[:, s, :], ss.t[:, s:s + 1], fg.t[:, :], ALU.mult, ALU.mult, [xt.b, ss.b, fg.b], [yo.b])
                          r0 = tok0 + s * 128
                          dma("sp", y_p[r0:r0 + 128, :], yo.t[:, :], [yo.b], [Buf()], is_out=True)
              stage(20)
              for st_ in sgen:
                  for f_ in (st_ or []):
                      f_()


        try:
            body()
        except _Stop:
            pass
        p._wait("sp", set(p.out_tks))
    return nc


def kernel(**inputs):
    f32 = np.float32
    xpr = np.asarray(inputs["x_prompt"], f32)
    B, L, _ = xpr.shape
    NPH = inputs["cache_sb_k"].shape[0]
    NPG = inputs["page_table"].shape[1]
    nc = bass.Bass("TRN2", target_bir_lowering=False)
    build(nc, L, NPG, NPH)
    wnames = ["norm_g", "w_in", "w_out", "group_norm_g", "ssm_lambda_re", "ssm_lambda_im", "ssm_b_re", "ssm_b_im",
              "ssm_c_re", "ssm_c_im", "ssm_log_dt", "ssm_d", "ssm_w_glu", "conv_w", "sb_bias", "w_mem_kv", "final_norm_g"]
    W = {n: np.ascontiguousarray(np.asarray(inputs[n], f32)) for n in wnames}
    ck = np.ascontiguousarray(np.asarray(inputs["cache_sb_k"], f32)).reshape(-1, 256)
    cv = np.ascontiguousarray(np.asarray(inputs["cache_sb_v"], f32)).reshape(-1, 256)
    xsm = np.asarray(inputs["x_sample"], f32)
    in_maps = []
    for c in range(8):
        sl = slice(4 * c, 4 * c + 4)
        m = dict(W)
        m["xp"] = np.ascontiguousarray(xpr[c])
        m["memp"] = np.ascontiguousarray(np.asarray(inputs["mem_prompt"], f32)[c])
        m["xs"] = np.ascontiguousarray(xsm[sl]).reshape(16, D)
        m["ck"] = ck
        m["cv"] = cv
        m["sre"] = np.ascontiguousarray(np.asarray(inputs["state_ssm_re"], f32)[sl])
        m["sim"] = np.ascontiguousarray(np.asarray(inputs["state_ssm_im"], f32)[sl])
        m["sconv"] = np.ascontiguousarray(np.asarray(inputs["state_conv"], f32)[sl])
        m["cmk"] = np.ascontiguousarray(np.asarray(inputs["cache_mem_k"], f32)[sl]).reshape(4, DEPTH, 256, 256)
        m["cmv"] = np.ascontiguousarray(np.asarray(inputs["cache_mem_v"], f32)[sl]).reshape(4, DEPTH, 256, 256)
        m["ptab"] = np.ascontiguousarray(np.asarray(inputs["page_table"], np.int32)[sl]).reshape(-1)
        in_maps.append(m)
    res = run_bass_kernel_spmd(nc, in_maps, core_ids=list(range(8))).results

    def cat(k, shp):
        return np.ascontiguousarray(np.stack([np.asarray(r[k], f32) for r in res])).reshape(shp)

    return (cat("y_p", (8, L, D)), cat("y_s", (32, 4, D)),
            cat("sbk_p", (8, DEPTH, L, 4, 64)), cat("sbv_p", (8, DEPTH, L, 4, 64)),
            cat("ssmre_p", (8, DEPTH, 16, 64)), cat("ssmim_p", (8, DEPTH, 16, 64)), cat("conv_p", (8, DEPTH, 2, 256)),
            cat("memk_p", (8, DEPTH, 256, 4, 64)), cat("memv_p", (8, DEPTH, 256, 4, 64)),
            cat("sbk_s", (32, DEPTH, 4, 4, 64)), cat("sbv_s", (32, DEPTH, 4, 4, 64)),
            cat("ssmre_s", (32, DEPTH, 16, 64)), cat("ssmim_s", (32, DEPTH, 16, 64)), cat("conv_s", (32, DEPTH, 2, 256)))
```

```python
from contextlib import ExitStack
import math
from itertools import zip_longest
import numpy as np
import concourse.bass as bass
import concourse.mybir as mybir
from concourse.bass_utils import run_bass_kernel_spmd

F32 = mybir.dt.float32
BF16 = mybir.dt.bfloat16
I32 = mybir.dt.int32
AF = mybir.ActivationFunctionType
ALU = mybir.AluOpType
AX = mybir.AxisListType

D = 1024
DEPTH = 2
TT = 128
NS = TT // 128
EPS = 1e-6
TWO_PI = 2.0 * math.pi


class Buf:
    __slots__ = ("lw", "rd")

    def __init__(self):
        self.lw = None
        self.rd = {}


class BufGroup:
    __slots__ = ("parts",)

    def __init__(self, parts):
        self.parts = parts


def _flat(bs):
    out = []
    for b in bs:
        if isinstance(b, BufGroup):
            out.extend(b.parts)
        else:
            out.append(b)
    return out


class Prog:
    EPOCH = 12000

    def __init__(self, nc, stack):
        self.nc = nc
        self.stack = stack
        self.engs = {"pe": nc.tensor, "dve": nc.vector, "act": nc.scalar, "pool": nc.gpsimd, "sp": nc.sync}
        self.esem, self.ecnt = {}, {}
        self.seen = {e: {} for e in self.engs}
        self.nsem = 0
        for e in self.engs:
            self._new_esem(e)
        self.dsems, self.dcnt = {}, {}
        self.out_tks = []

    def _mksem(self, name):
        self.nsem += 1
        return self.stack.enter_context(self.nc.semaphore(f"{name}_{self.nsem}"))

    def _new_esem(self, e):
        self.esem[e] = self._mksem("e" + e)
        self.ecnt[e] = 0

    def sbuf(self, name, shape, dt):
        return self.stack.enter_context(self.nc.sbuf_tensor(name, shape, dt))

    def psum(self, name, shape, dt):
        return self.stack.enter_context(self.nc.psum_tensor(name, shape, dt))

    def _wait(self, eng, deps):
        E = self.engs[eng]
        seen = self.seen[eng]
        for (sem, val) in deps:
            k = id(sem)
            if seen.get(k, 0) < val:
                E.wait_ge(sem, val)
                seen[k] = val

    def _deps(self, eng, reads, writes, is_dma):
        reads, writes = _flat(reads), _flat(writes)
        deps = set()
        own = None if is_dma else id(self.esem[eng])
        for b in reads:
            if b.lw is not None and not (eng == "pe" and not is_dma and id(b.lw[0]) == own):
                deps.add(b.lw)
        for b in writes:
            if b.lw is not None and id(b.lw[0]) != own:
                deps.add(b.lw)
            for sem_id, tk in b.rd.items():
                if sem_id != own:
                    deps.add(tk)
        return deps

    def _record(self, tk, reads, writes):
        reads, writes = _flat(reads), _flat(writes)
        k = id(tk[0])
        for b in reads:
            if b.rd.get(k, (None, 0))[1] < tk[1]:
                b.rd[k] = tk
        for b in writes:
            b.lw = tk
            b.rd = {}

    def op(self, eng, fn, reads=(), writes=()):
        self._wait(eng, self._deps(eng, reads, writes, False))
        inst = fn(self.engs[eng])
        if self.ecnt[eng] >= self.EPOCH:
            self._new_esem(eng)
        self.ecnt[eng] += 1
        tk = (self.esem[eng], self.ecnt[eng])
        inst.then_inc(tk[0], 1)
        self._record(tk, reads, writes)
        return tk

    def dma(self, q, fn, reads=(), writes=(), nsem=8, is_out=False):
        self._wait(q, self._deps(q, reads, writes, True))
        if q not in self.dsems:
            self.dsems[q] = [self._mksem("d" + q) for _ in range(nsem)]
            self.dcnt[q] = 0
        i = self.dcnt[q]
        self.dcnt[q] += 1
        sems = self.dsems[q]
        sem = sems[i % len(sems)]
        tk = (sem, 16 * (i // len(sems) + 1))
        fn(self.engs[q]).then_inc(sem, 16)
        self._record(tk, reads, writes)
        if is_out:
            self.out_tks.append(tk)
        return tk


class T:
    def __init__(self, t, nb=1):
        self.t = t
        self.bs = [Buf() for _ in range(nb)]
        self.b = self.bs[0]


class _Stop(Exception):
    pass


def build(nc, L, NPG, NPH, STOP=999, DO_SAMPLE=True):
    def stage(n):
        if n >= STOP:
            raise _Stop()

    NT = L // TT
    NSUB = L // 128

    def din(name, shape, dt=F32):
        return nc.dram_tensor(name, shape, dt, kind="ExternalInput").ap()

    def dout(name, shape, dt=F32):
        return nc.dram_tensor(name, shape, dt, kind="ExternalOutput").ap()

    xp = din("xp", [L, D])
    memp = din("memp", [256, D])
    norm_g = din("norm_g", [DEPTH, D])
    w_in = din("w_in", [DEPTH, D, 3072])
    w_out = din("w_out", [DEPTH, D, D])
    gng = din("group_norm_g", [DEPTH, 4, 256])
    lam_re = din("ssm_lambda_re", [DEPTH, 16, 64])
    lam_im = din("ssm_lambda_im", [DEPTH, 16, 64])
    b_re = din("ssm_b_re", [DEPTH, 16, 64, 16])
    b_im = din("ssm_b_im", [DEPTH, 16, 64, 16])
    c_re = din("ssm_c_re", [DEPTH, 16, 16, 64])
    c_im = din("ssm_c_im", [DEPTH, 16, 16, 64])
    log_dt = din("ssm_log_dt", [DEPTH, 16])
    ssm_d = din("ssm_d", [DEPTH, 256])
    w_glu = din("ssm_w_glu", [DEPTH, 256, 256])
    conv_w = din("conv_w", [DEPTH, 3, 256])
    sb_bias = din("sb_bias", [DEPTH, 4])
    w_mem = din("w_mem_kv", [DEPTH, D, 512])
    fin_g = din("final_norm_g", [D])

    y_p = dout("y_p", [L, D])
    sbk_p = dout("sbk_p", [DEPTH, L, 256])
    sbv_p = dout("sbv_p", [DEPTH, L, 256])
    ssmre_p = dout("ssmre_p", [DEPTH, 16, 64])
    ssmim_p = dout("ssmim_p", [DEPTH, 16, 64])
    conv_p = dout("conv_p", [DEPTH, 2, 256])
    memk_p = dout("memk_p", [DEPTH, 256, 256])
    memv_p = dout("memv_p", [DEPTH, 256, 256])
    x1 = nc.dram_tensor("x1_scratch", [L, D], F32, kind="Internal").ap()
    xs = din("xs", [16, D])
    ck = din("ck", [NPH * 2 * 128, 256])
    cv = din("cv", [NPH * 2 * 128, 256])
    sre = din("sre", [4, DEPTH, 16, 64])
    sim = din("sim", [4, DEPTH, 16, 64])
    sconv = din("sconv", [4, DEPTH, 2, 256])
    cmk = din("cmk", [4, DEPTH, 256, 256])
    cmv = din("cmv", [4, DEPTH, 256, 256])
    ptab = din("ptab", [4 * NPG], I32)
    y_s = dout("y_s", [16, D])
    sbk_s = dout("sbk_s", [4, DEPTH, 4, 256])
    sbv_s = dout("sbv_s", [4, DEPTH, 4, 256])
    ssmre_s = dout("ssmre_s", [4, DEPTH, 16, 64])
    ssmim_s = dout("ssmim_s", [4, DEPTH, 16, 64])
    conv_s = dout("conv_s", [4, DEPTH, 2, 256])

    with ExitStack() as st:
        p = Prog(nc, st)

        def sb(name, shape, dt=F32, nb=1):
            return T(p.sbuf(name, shape, dt), nb)

        def ps(name, shape, dt=F32):
            return T(p.psum(name, shape, dt))

        x1bufs = [Buf() for _ in range(NT)]

        def act(out, in_, func, R, W, bias=None, scale=None):
            kw = {}
            if bias is not None:
                kw["bias"] = bias
            if scale is not None:
                kw["scale"] = scale
            return p.op("act", lambda e: e.activation(out=out, in_=in_, func=func, **kw), R, W)

        def tt(eng, out, a, b, op, R, W):
            return p.op(eng, lambda e: e.tensor_tensor(out=out, in0=a, in1=b, op=op), R, W)

        def ts(eng, out, a, s1, op0, R, W, s2=None, op1=None):
            if op1 is None:
                return p.op(eng, lambda e: e.tensor_scalar(out=out, in0=a, scalar1=s1, scalar2=None, op0=op0), R, W)
            return p.op(eng, lambda e: e.tensor_scalar(out=out, in0=a, scalar1=s1, scalar2=s2, op0=op0, op1=op1), R, W)

        def stt(out, a, s, b, op0, op1, R, W):
            return p.op("dve", lambda e: e.scalar_tensor_tensor(out=out, in0=a, scalar=s, in1=b, op0=op0, op1=op1), R, W)

        def cp(eng, out, in_, R, W):
            if eng == "act":
                return act(out, in_, AF.Copy, R, W)
            return p.op(eng, lambda e: e.tensor_copy(out=out, in_=in_), R, W)

        def mset(eng, ap, v, W):
            return p.op(eng, lambda e: e.memset(ap, v), (), W)

        def mm(out, lhsT, rhs, start, stop, R, W):
            return p.op("pe", lambda e: e.matmul(out, lhsT=lhsT, rhs=rhs, start=start, stop=stop), R, W)

        def tr(out, in_, ident, R, W):
            return p.op("pe", lambda e: e.transpose(out=out, in_=in_, identity=ident), R, W)

        def dma(q, out, in_, R, W, slow=False, is_out=False):
            if slow:
                return p.dma(q, lambda e: e.dma_start(out=out, in_=in_, allow_slow_non_contiguous=True), R, W, is_out=is_out)
            return p.dma(q, lambda e: e.dma_start(out=out, in_=in_), R, W, is_out=is_out)

        def scan(out, d0, d1, init, R, W):
            return p.op("dve", lambda e: e.tensor_tensor_scan(out=out, data0=d0, data1=d1, initial=init,
                                                               op0=ALU.mult, op1=ALU.add), R, W)

        identf = sb("identf", [128, 128])
        identb = sb("identb", [128, 128], BF16)
        onesf = sb("onesf", [128, 128])
        epsT = sb("epsT", [128, 1])
        onec = sb("onec", [128, 1])
        mset("pool", identf.t[:], 1.0, [identf.b])
        p.op("pool", lambda e: e.affine_select(out=identf.t[:], in_=identf.t[:], pattern=[[-1, 128]],
                                               compare_op=ALU.is_equal, fill=0.0, base=0, channel_multiplier=1),
             [identf.b], [identf.b])
        cp("dve", identb.t[:], identf.t[:], [identf.b], [identb.b])
        mset("dve", onesf.t[:], 1.0, [onesf.b])
        mset("dve", epsT.t[:], EPS, [epsT.b])
        mset("dve", onec.t[:], 1.0, [onec.b])

        win = sb("win", [128, 8, 3072], BF16, 48)

        def wb(kc, c0, c1):
            return [win.bs[kc * 6 + c] for c in range(c0 // 512, (c1 - 1) // 512 + 1)]

        wout = sb("wout", [128, 8, 1024], BF16, 16)
        wglu = sb("wglu", [128, 2, 256], BF16)
        gt = sb("gt", [128, 1024])
        gnAB = sb("gnAB", [128, 4])
        gnCM = sb("gnCM", [128, 2, 256])
        dvec = sb("dvec", [128, 2])
        cw = sb("cw", [128, 2, 3])
        sbb = sb("sbb", [128, 4])
        KcT = sb("KcT", [128, 2, L], BF16, NSUB)
        Vc = sb("Vc", [128, NSUB, 256], BF16, NSUB)
        cosT = sb("cosT", [128, 8, TT])
        sinT = sb("sinT", [128, 8, TT])
        rsp = sb("rsp", [128, 8])
        BtR = sb("BtR", [128, 8, 128], BF16)
        BtI = sb("BtI", [128, 8, 128], BF16)
        CtR = sb("CtR", [128, 8, 128], BF16)
        CtI = sb("CtI", [128, 8, 128], BF16)
        memT = sb("memT", [128, 8, 256], BF16)
        mkT = sb("mkT", [128, 2, 256], BF16)
        mvA = sb("mvA", [128, 2, 4, 65], BF16)
        hst = sb("hst", [128, 8, 2])
        xt = sb("xt", [128, NS, 1024])
        xnT = sb("xnT", [128, 8, TT], BF16)
        mergedT = sb("mergedT", [128, 8, TT], BF16, 8)
        au_f = sb("au_f", [128, 2, TT])
        au_b = sb("au_b", [128, 2, TT], BF16)
        sgA = sb("sgA", [128, 2, TT])
        sgB = sb("sgB", [128, 2, TT])
        bbx = sb("bbx", [128, 6, TT])
        qT = sb("qT", [128, 2, TT], BF16)
        mqT = sb("mqT", [128, 2, TT], BF16)
        sgC = sb("sgC", [128, NS, 256])
        sgM = sb("sgM", [128, NS, 256])
        vbuf = sb("vbuf", [128, 2, TT + 2])
        wk = [sb(f"wk{i}", [128, 1024]) for i in range(8)]
        wkb = [sb(f"wkb{i}", [128, 1024], BF16) for i in range(4)]
        sm = [sb(f"sm{i}", [128, 16]) for i in range(6)]
        for t_ in (wk[0], wk[2], wkb[0], wkb[1], wkb[2], wkb[3]):
            t_.bl, t_.br = Buf(), Buf()
            t_.b = BufGroup([t_.bl, t_.br])
            t_.bs = [t_.b]
        pmm = [ps(f"pmm{i}", [128, 512]) for i in range(2)]
        ptr = T(p.psum("ptr", [128, 1024], BF16), 2)
        pss = [ps(f"pss{i}", [128, 512]) for i in range(2)]
        po = ps("po", [128, 512])
        po2 = ps("po2", [128, 512])
        py = ps("py", [128, 2, 256])

        def rstd_of(out, in_, n, R, W):
            act(out, in_, AF.Ln, R + [epsT.b], W, bias=epsT.t[:in_.shape[0], 0:1], scale=1.0 / n)
            act(out, out, AF.Exp, W, W, scale=-0.5)

        def sincos(ang, angb, N, sin_out, sin_b, cos_out, cos_b, tmpf, tmpi):
            for (o_, ob, shift) in ((sin_out, sin_b, 0.0), (cos_out, cos_b, 0.25)):
                for c0 in range(0, N, 512):
                    n_ = min(512, N - c0)
                    o = o_[:, c0:c0 + n_]
                    ts("dve", tmpf.t[:, 0:n_], ang[:, c0:c0 + n_], 1.0 / TWO_PI, ALU.mult, [angb], [tmpf.b], s2=shift, op1=ALU.add)
                    cp("dve", tmpi.t[:, 0:n_], tmpf.t[:, 0:n_], [tmpf.b], [tmpi.b])
                    cp("dve", o, tmpi.t[:, 0:n_], [tmpi.b], [ob])
                    tt("dve", tmpf.t[:, 0:n_], tmpf.t[:, 0:n_], o, ALU.subtract, [tmpf.b, ob], [tmpf.b])
                    act(o, tmpf.t[:, 0:n_], AF.Sin, [tmpf.b], [ob], scale=TWO_PI * (1.0 - 1e-6))

        tmpi = sb("tmpi", [128, 512], I32)

        def prep_mem():
            for mc in range(2):
                dma("sp", wk[0].t[:, :], memp[mc * 128:(mc + 1) * 128, :], [], [wk[0].b])
                cp("dve", wkb[0].t[:, :], wk[0].t[:, :], [wk[0].b], [wkb[0].b])
                for kc in range(8):
                    tr(ptr.t[:, kc * 128:(kc + 1) * 128], wkb[0].t[:, kc * 128:(kc + 1) * 128], identb.t[:],
                       [wkb[0].b, identb.b], ptr.bs)
                cp("act", memT.t[:, :, mc * 128:(mc + 1) * 128],
                   ptr.t[:, :].rearrange("p (k m) -> p k m", k=8), ptr.bs, [memT.b])


        NQ = 16
        xs_t = sb("xs_t", [NQ, 1024])
        idxT = sb("idxT", [128, 4 * NPG], I32)
        iot = sb("iot", [128, 1], I32)
        mask16 = sb("mask16", [NQ, 4])
        sbias16 = sb("sbias16", [NQ, 1])
        s_xnT = sb("s_xnT", [128, 8, NQ], BF16)
        s_auf = sb("s_auf", [128, 2, NQ])
        s_aub = sb("s_aub", [128, 2, NQ], BF16)
        s_sgA = sb("s_sgA", [128, 2, NQ])
        s_sgB = sb("s_sgB", [128, 2, NQ])
        s_bbx = sb("s_bbx", [128, 6, NQ])
        s_qT = sb("s_qT", [128, 2, NQ], BF16)
        s_kT = sb("s_kT", [128, 2, NQ], BF16)
        s_mqT = sb("s_mqT", [128, 2, NQ], BF16)
        s_sgC = sb("s_sgC", [NQ, 256])
        s_sgM = sb("s_sgM", [NQ, 256])
        s_vbn = sb("s_vbn", [4, 4, 256], BF16)
        s_vbuf = sb("s_vbuf", [128, 2, 4, 6])
        s_hst = sb("s_hst", [128, 8, 4, 2])
        s_mT = sb("s_mT", [128, 8, NQ], BF16)
        s_yc = sb("s_yc", [NQ, 256])
        s_ym = sb("s_ym", [NQ, 256])
        qbd = sb("qbd", [128, 2, NQ], BF16)
        s_ncar = sb("s_ncar", [NQ, 1])
        Kb = sb("Kb", [128, 2, 4, 256], BF16, 2)
        Vb = sb("Vb", [128, 1, 4, 256], BF16, 1)

        def colv(ap1d):
            return ap1d.rearrange("(p o) -> p o", o=1)

        def v4(ap):
            return ap.rearrange("p (n t) -> p n t", n=4)

        def sample_setup():
            dma("sp", xs_t.t[:, :], xs, [], [xs_t.b])
            dma("sp", idxT.t[:, :], ptab.partition_broadcast(128), [], [idxT.b])
            pi = sm[5]
            pI = tmpi
            p.op("pool", lambda e: e.iota(pI.t[:NQ, 0:1], pattern=[[0, 1]], base=0, channel_multiplier=1), [], [pI.b])
            p.op("dve", lambda e: e.tensor_single_scalar(out=pI.t[:NQ, 0:1], in_=pI.t[:NQ, 0:1], scalar=3, op=ALU.bitwise_and),
                 [pI.b], [pI.b])
            cp("dve", pi.t[:NQ, 0:1], pI.t[:NQ, 0:1], [pI.b], [pi.b])
            p.op("pool", lambda e: e.iota(pI.t[:NQ, 8:12], pattern=[[1, 4]], base=0, channel_multiplier=0), [pI.b], [pI.b])
            cp("dve", pi.t[:NQ, 4:8], pI.t[:NQ, 8:12], [pI.b], [pi.b])
            ts("dve", mask16.t[:, :], pi.t[:NQ, 4:8], pi.t[:NQ, 0:1], ALU.is_lt, [pi.b], [mask16.b])

        def s_merge_fm(br, y_, sg):
            sq = wk[7]
            pm = pmm[0]
            for hf in range(2):
                tt("dve", sq.t[:, hf * NQ:(hf + 1) * NQ], y_.t[:, hf * NQ:(hf + 1) * NQ], y_.t[:, hf * NQ:(hf + 1) * NQ], ALU.mult,
                   [y_.b], [sq.b])
            for hf in range(2):
                mm(pm.t[:, 0:NQ], onesf.t[:, :], sq.t[:, hf * NQ:(hf + 1) * NQ], hf == 0, hf == 1, [onesf.b, sq.b], [pm.b])
            rs = wk[6]
            rstd_of(rs.t[:, 0:NQ], pm.t[:, 0:NQ], 256.0, [pm.b], [rs.b])
            for hf in range(2):
                stt(sq.t[:, hf * NQ:(hf + 1) * NQ], y_.t[:, hf * NQ:(hf + 1) * NQ], gnAB.t[:, br * 2 + hf:br * 2 + hf + 1],
                    rs.t[:, 0:NQ], ALU.mult, ALU.mult, [y_.b, gnAB.b, rs.b], [sq.b])
                tt("dve", s_mT.t[:, br * 2 + hf, :], sq.t[:, hf * NQ:(hf + 1) * NQ], sg.t[:, hf, :], ALU.mult, [sq.b, sg.b], [s_mT.b])

        def s_merge_tm(idx, y_, sg, mtm):
            sq = wk[7]
            ssq = sm[1]
            tt("dve", sq.t[:NQ, 0:256], y_.t[:, :], y_.t[:, :], ALU.mult, [y_.b], [sq.b])
            p.op("dve", lambda e: e.reduce_sum(out=ssq.t[:NQ, 0:1], in_=sq.t[:NQ, 0:256], axis=AX.X), [sq.b], [ssq.b])
            rstd_of(ssq.t[:NQ, 0:1], ssq.t[:NQ, 0:1], 256.0, [ssq.b], [ssq.b])
            stt(sq.t[:NQ, 0:256], y_.t[:, :], ssq.t[:NQ, 0:1], gnCM.t[:NQ, idx, :], ALU.mult, ALU.mult, [y_.b, ssq.b, gnCM.b], [sq.b])
            tt("dve", mtm.t[:NQ, idx * 256:(idx + 1) * 256], sq.t[:NQ, 0:256], sg.t[:, :], ALU.mult, [sq.b, sg.b], [mtm.b])

        def sample_layer(l):
            if l == 0:
                p.op("pool", lambda e: e.iota(iot.t[:, 0:1], pattern=[[0, 1]], base=0, channel_multiplier=1), [], [iot.b])
                ts("dve", idxT.t[:, :], idxT.t[:, :], 256, ALU.mult, [idxT.b, iot.b], [idxT.b], s2=iot.t[:, 0:1], op1=ALU.add)
            else:
                ts("dve", idxT.t[:, :], idxT.t[:, :], 128, ALU.add, [idxT.b], [idxT.b])
            for h in range(4):
                dma("sp", sbias16.t[h * 4:(h + 1) * 4, 0:1], sb_bias[l, h:h + 1].partition_broadcast(4), [], [sbias16.b], slow=True)
            sq, ss = wk[0], sm[0]
            tt("dve", sq.t[:NQ, :], xs_t.t[:, :], xs_t.t[:, :], ALU.mult, [xs_t.b], [sq.b])
            p.op("dve", lambda e: e.reduce_sum(out=ss.t[:NQ, 0:1], in_=sq.t[:NQ, :], axis=AX.X), [sq.b], [ss.b])
            rstd_of(ss.t[:NQ, 0:1], ss.t[:NQ, 0:1], 1024.0, [ss.b], [ss.b])
            xnb = wkb[0]
            stt(xnb.t[:NQ, :], xs_t.t[:, :], ss.t[:NQ, 0:1], gt.t[:NQ, :], ALU.mult, ALU.mult, [xs_t.b, ss.b, gt.b], [xnb.b])
            for kc in range(8):
                tr(ptr.t[:, kc * NQ:(kc + 1) * NQ], xnb.t[:NQ, kc * 128:(kc + 1) * 128], identb.t[:NQ, :NQ], [xnb.b, identb.b], ptr.bs)
            cp("act", s_xnT.t[:, :, :], ptr.t[:, 0:8 * NQ].rearrange("p (k m) -> p k m", k=8), ptr.bs, [s_xnT.b])

            def sproj(j, evac):
                pm = pmm[j % 2]
                for kc in range(8):
                    mm(pm.t[:, 0:NQ], win.t[:, kc, j * 128:(j + 1) * 128], s_xnT.t[:, kc, :], kc == 0, kc == 7,
                       wb(kc, j * 128, (j + 1) * 128) + [s_xnT.b], [pm.b])
                evac(pm)

            for hf in range(2):
                def ev(pm, hf=hf):
                    cp("act", s_auf.t[:, hf, :], pm.t[:, 0:NQ], [pm.b], [s_auf.b])
                    cp("act", s_aub.t[:, hf, :], pm.t[:, 0:NQ], [pm.b], [s_aub.b])
                sproj(0 + hf, ev)
                sproj(2 + hf, lambda pm, hf=hf: act(s_sgA.t[:, hf, :], pm.t[:, 0:NQ], AF.Silu, [pm.b], [s_sgA.b]))
                for bi in range(3):
                    sproj(4 + 2 * bi + hf, lambda pm, hf=hf, bi=bi: cp("act", s_bbx.t[:, 2 * bi + hf, :], pm.t[:, 0:NQ], [pm.b], [s_bbx.b]))
                sproj(10 + hf, lambda pm, hf=hf: act(s_sgB.t[:, hf, :], pm.t[:, 0:NQ], AF.Silu, [pm.b], [s_sgB.b]))
                sproj(12 + hf, lambda pm, hf=hf: cp("act", s_qT.t[:, hf, :], pm.t[:, 0:NQ], [pm.b], [s_qT.b]))
                sproj(14 + hf, lambda pm, hf=hf: cp("act", s_kT.t[:, hf, :], pm.t[:, 0:NQ], [pm.b], [s_kT.b]))
                sproj(20 + hf, lambda pm, hf=hf: cp("act", s_mqT.t[:, hf, :], pm.t[:, 0:NQ], [pm.b], [s_mqT.b]))
            for (c0, dst) in ((2304, s_sgC), (2816, s_sgM)):
                pm = pmm[0]
                for kc in range(8):
                    mm(pm.t[:NQ, 0:256], s_xnT.t[:, kc, :], win.t[:, kc, c0:c0 + 256], kc == 0, kc == 7,
                       wb(kc, c0, c0 + 256) + [s_xnT.b], [pm.b])
                act(dst.t[:, :], pm.t[:NQ, 0:256], AF.Silu, [pm.b], [dst.b])
            for n in range(4):
                pm = pmm[n % 2]
                for kc in range(8):
                    mm(pm.t[:4, 0:512], s_xnT.t[:, kc, n * 4:(n + 1) * 4], win.t[:, kc, 1792:2304], kc == 0, kc == 7,
                       wb(kc, 1792, 2304) + [s_xnT.b], [pm.b])
                kvf = wk[1 + n % 2]
                cp("act", kvf.t[:4, 0:512], pm.t[:4, :], [pm.b], [kvf.b])
                dma("sp", sbk_s[n, l], kvf.t[:4, 0:256], [kvf.b], [Buf()], is_out=True)
                dma("sp", sbv_s[n, l], kvf.t[:4, 256:512], [kvf.b], [Buf()], is_out=True)
                cp("act", s_vbn.t[:, n, :], kvf.t[:4, 256:512], [kvf.b], [s_vbn.b])

            for hf in range(2):
                for n in range(4):
                    for j in range(2):
                        dma("sp", s_vbuf.t[:, hf, n, j:j + 1], colv(sconv[n, l, j, hf * 128:(hf + 1) * 128]), [], [s_vbuf.b], slow=True)
            yb = wk[3]
            for hf in range(2):
                tt("dve", s_vbuf.t[:, hf, :, 2:6], v4(s_bbx.t[:, 2 + hf, :]), v4(s_bbx.t[:, 4 + hf, :]), ALU.mult, [s_bbx.b], [s_vbuf.b])
                acc = yb.t[:, hf * NQ:(hf + 1) * NQ]
                ts("dve", v4(acc), s_vbuf.t[:, hf, :, 2:6], cw.t[:, hf, 2:3], ALU.mult, [s_vbuf.b, cw.b], [yb.b])
                stt(v4(acc), s_vbuf.t[:, hf, :, 1:5], cw.t[:, hf, 1:2], v4(acc), ALU.mult, ALU.add, [s_vbuf.b, cw.b, yb.b], [yb.b])
                stt(v4(acc), s_vbuf.t[:, hf, :, 0:4], cw.t[:, hf, 0:1], v4(acc), ALU.mult, ALU.add, [s_vbuf.b, cw.b, yb.b], [yb.b])
                tt("dve", acc, acc, s_bbx.t[:, 0 + hf, :], ALU.mult, [yb.b, s_bbx.b], [yb.b])
            for hf in range(2):
                for n in range(4):
                    for j in range(2):
                        dma("sp", colv(conv_s[n, l, j, hf * 128:(hf + 1) * 128]), s_vbuf.t[:, hf, n, 4 + j:5 + j], [s_vbuf.b], [Buf()],
                            slow=True, is_out=True)
            s_merge_fm(1, yb, s_sgB)

            for n in range(4):
                for gi in range(2):
                    dma("sp", s_hst.t[gi * 64:(gi + 1) * 64, :, n, 0], sre[n, l].rearrange("(gp gi) p -> gi p gp", gi=2)[gi], [], [s_hst.b], slow=True)
                    dma("sp", s_hst.t[gi * 64:(gi + 1) * 64, :, n, 1], sim[n, l].rearrange("(gp gi) p -> gi p gp", gi=2)[gi], [], [s_hst.b], slow=True)
            ya = wk[3]
            yab = wkb[2]
            for gp in range(8):
                hf = gp // 4
                pb = pss[gp % 2]
                mm(pb.t[:, 0:NQ], BtR.t[:, gp, :], s_aub.t[:, hf, :], True, True, [BtR.b, s_aub.b], [pb.b])
                mm(pb.t[:, NQ:2 * NQ], BtI.t[:, gp, :], s_aub.t[:, hf, :], True, True, [BtI.b, s_aub.b], [pb.b])
                c_ = cosT.t[:, gp, 0:4].unsqueeze(1).to_broadcast([128, 4, 4])
                s_ = sinT.t[:, gp, 0:4].unsqueeze(1).to_broadcast([128, 4, 4])
                w = wk[4]
                q1, q2, q3, q4 = (w.t[:, i * NQ:(i + 1) * NQ] for i in range(4))
                bre_, bim_ = v4(pb.t[:, 0:NQ]), v4(pb.t[:, NQ:2 * NQ])
                tt("dve", v4(q1), bre_, c_, ALU.mult, [pb.b, cosT.b], [w.b])
                tt("dve", v4(q2), bim_, s_, ALU.mult, [pb.b, sinT.b], [w.b])
                tt("dve", v4(q3), bim_, c_, ALU.mult, [pb.b, cosT.b], [w.b])
                tt("dve", v4(q4), bre_, s_, ALU.mult, [pb.b, sinT.b], [w.b])
                tt("dve", q1, q1, q2, ALU.add, [w.b], [w.b])
                tt("dve", q3, q3, q4, ALU.subtract, [w.b], [w.b])
                g_ = wk[5]
                gre, gim = g_.t[:, 0:NQ], g_.t[:, NQ:2 * NQ]
                rb = rsp.t[:, gp:gp + 1].to_broadcast([128, 4])
                for n in range(4):
                    scan(gre[:, n * 4:(n + 1) * 4], rb, q1[:, n * 4:(n + 1) * 4], s_hst.t[:, gp, n, 0:1], [w.b, rsp.b, s_hst.b], [g_.b])
                    scan(gim[:, n * 4:(n + 1) * 4], rb, q3[:, n * 4:(n + 1) * 4], s_hst.t[:, gp, n, 1:2], [w.b, rsp.b, s_hst.b], [g_.b])
                tt("dve", v4(q1), v4(gre), c_, ALU.mult, [g_.b, cosT.b], [w.b])
                tt("dve", v4(q2), v4(gim), s_, ALU.mult, [g_.b, sinT.b], [w.b])
                tt("dve", v4(q3), v4(gim), c_, ALU.mult, [g_.b, cosT.b], [w.b])
                tt("dve", v4(q4), v4(gre), s_, ALU.mult, [g_.b, sinT.b], [w.b])
                hb = wkb[3]
                hR, hI = hb.t[:, 0:NQ], hb.t[:, NQ:2 * NQ]
                tt("dve", hR, q1, q2, ALU.subtract, [w.b], [hb.b])
                tt("dve", hI, q3, q4, ALU.add, [w.b], [hb.b])
                tt("dve", s_hst.t[:, gp, :, 0], v4(q1)[:, :, 3], v4(q2)[:, :, 3], ALU.subtract, [w.b], [s_hst.b])
                tt("dve", s_hst.t[:, gp, :, 1], v4(q3)[:, :, 3], v4(q4)[:, :, 3], ALU.add, [w.b], [s_hst.b])
                mm(py.t[:, hf, 0:NQ], CtR.t[:, gp, :], hR, gp % 4 == 0, False, [CtR.b, hb.b], [py.b])
                mm(py.t[:, hf, 0:NQ], CtI.t[:, gp, :], hI, False, gp % 4 == 3, [CtI.b, hb.b], [py.b])
                if gp % 4 == 3:
                    yh = ya.t[:, hf * NQ:(hf + 1) * NQ]
                    stt(yh, s_auf.t[:, hf, :], dvec.t[:, hf:hf + 1], py.t[:, hf, 0:NQ], ALU.mult, ALU.add, [s_auf.b, dvec.b, py.b], [ya.b])
                    act(yh, yh, AF.Gelu, [ya.b], [ya.b])
                    cp("dve", yab.t[:, hf * NQ:(hf + 1) * NQ], yh, [ya.b], [yab.b])
            for n in range(4):
                for gi in range(2):
                    dma("sp", ssmre_s[n, l].rearrange("(gp gi) p -> gi p gp", gi=2)[gi], s_hst.t[gi * 64:(gi + 1) * 64, :, n, 0], [s_hst.b], [Buf()],
                        slow=True, is_out=True)
                    dma("sp", ssmim_s[n, l].rearrange("(gp gi) p -> gi p gp", gi=2)[gi], s_hst.t[gi * 64:(gi + 1) * 64, :, n, 1], [s_hst.b], [Buf()],
                        slow=True, is_out=True)
            for oc in range(2):
                pm = pmm[oc]
                for k2 in range(2):
                    mm(pm.t[:, 0:NQ], wglu.t[:, k2, oc * 128:(oc + 1) * 128], yab.t[:, k2 * NQ:(k2 + 1) * NQ], k2 == 0, k2 == 1,
                       [wglu.b, yab.b], [pm.b])
                sg_ = wk[5]
                act(sg_.t[:, 0:NQ], pm.t[:, 0:NQ], AF.Sigmoid, [pm.b], [sg_.b])
                tt("dve", ya.t[:, oc * NQ:(oc + 1) * NQ], ya.t[:, oc * NQ:(oc + 1) * NQ], sg_.t[:, 0:NQ], ALU.mult, [ya.b, sg_.b], [ya.b])
            s_merge_fm(0, ya, s_sgA)

            NB = NPG // 4
            ncar = s_ncar
            o16 = wk[7]
            RO = 512
            E_s, Eb = wk[0].t, wk[0].br
            L_s, Lbb = wk[2].t, wk[2].br
            P_s = wk[6]
            kTs = [(wkb[1].t, wkb[1].br), (wkb[3].t, wkb[3].br)]
            Wb_s, Wbb = wkb[2].t, wkb[2].br
            WT_s, WTb = wkb[0].t, wkb[0].br
            pK, pS = pmm[0], pmm[1]
            pK16 = pK.t[:, :].bitcast(BF16)
            pS16 = pS.t[:, :].bitcast(BF16)

            def gatherK(n, kb, slot):
                for pg in range(4):
                    col_ = n * NPG + kb * 4 + pg
                    off = bass.IndirectOffsetOnAxis(ap=idxT.t[:, col_:col_ + 1], axis=0)
                    p.dma("pool", lambda e, off=off, pg=pg: e.indirect_dma_start(out=Kb.t[:, slot, pg, :], out_offset=None, in_=ck, in_offset=off),
                          [idxT.b], [Kb.bs[slot]])

            def gatherV(n, kb):
                for pg in range(4):
                    col_ = n * NPG + kb * 4 + pg
                    off = bass.IndirectOffsetOnAxis(ap=idxT.t[:, col_:col_ + 1], axis=0)
                    p.dma("pool", lambda e, off=off, pg=pg: e.indirect_dma_start(out=Vb.t[:, 0, pg, :], out_offset=None, in_=cv, in_offset=off),
                          [idxT.b], [Vb.bs[0]])

            def tail_steps(ncol, kw, nch, vfn, first, last):
                def t_L():
                    mset("dve", P_s.t[:NQ, 0:1], 0.0, [P_s.b])
                    act(L_s[:NQ, RO:RO + ncol], E_s[:NQ, RO:RO + ncol], AF.Ln, [Eb], [Lbb], bias=1.0)

                def t_scan():
                    scan(P_s.t[:NQ, 1:ncol + 1], onec.t[:NQ, 0:1].to_broadcast([NQ, ncol]), L_s[:NQ, RO:RO + ncol], 0.0,
                         [onec.b, Lbb], [P_s.b])
                    tt("dve", ncar.t[:NQ, 0:1], ncar.t[:NQ, 0:1], P_s.t[:NQ, ncol:ncol + 1], ALU.subtract, [ncar.b, P_s.b], [ncar.b])

                def t_X():
                    act(P_s.t[:NQ, 0:ncol], P_s.t[:NQ, 0:ncol], AF.Exp, [P_s.b, ncar.b], [P_s.b], bias=ncar.t[:NQ, 0:1])

                def t_W():
                    tt("dve", Wb_s[:NQ, RO:RO + ncol], E_s[:NQ, RO:RO + ncol], P_s.t[:NQ, 0:ncol], ALU.mult, [Eb, P_s.b], [Wbb])

                def t_tr():
                    for c4 in range(nch):
                        tr(pS16[:kw, c4 * NQ:(c4 + 1) * NQ], Wb_s[:NQ, RO + c4 * kw:RO + (c4 + 1) * kw], identb.t[:NQ, :NQ],
                           [Wbb, identb.b], [pS.b])

                def t_ev():
                    cp("act", WT_s[:kw, RO:RO + nch * NQ], pS16[:kw, 0:nch * NQ], [pS.b], [WTb])

                def t_pv():
                    for c4 in range(nch):
                        vap, vbufs = vfn(c4)
                        mm(py.t[:NQ, 0, :], WT_s[:kw, RO + c4 * NQ:RO + (c4 + 1) * NQ], vap, first and c4 == 0, last and c4 == nch - 1,
                           [WTb] + vbufs, [py.b])
                return [t_L, t_scan, t_X, t_W, t_tr, t_ev, t_pv]

            def new_steps(n):
                def a_S():
                    for hc in range(2):
                        mm(pS.t[:NQ, 0:4], qbd.t[:, hc, :], s_kT.t[:, hc, n * 4:(n + 1) * 4], hc == 0, hc == 1, [qbd.b, s_kT.b], [pS.b])

                def a_E():
                    act(E_s[:NQ, RO:RO + 4], pS.t[:NQ, 0:4], AF.Exp, [pS.b, sbias16.b], [Eb], bias=sbias16.t[:, 0:1], scale=0.125)
                    tt("dve", E_s[:NQ, RO:RO + 4], E_s[:NQ, RO:RO + 4], mask16.t[:, :], ALU.mult, [Eb, mask16.b], [Eb])
                return [a_S, a_E] + tail_steps(4, 4, 1, lambda c4: (s_vbn.t[:, n, :], [s_vbn.b]), True, NB == 0)

            def past_steps(n, kb):
                slot = kb % 2

                def b_kt():
                    if kb > 0:
                        gatherK(n, kb - 1, (kb - 1) % 2)
                    for pg in range(4):
                        for hc in range(2):
                            tr(pK16[:, (hc * 4 + pg) * 128:(hc * 4 + pg + 1) * 128], Kb.t[:, slot, pg, hc * 128:(hc + 1) * 128], identb.t[:, :],
                               [Kb.bs[slot], identb.b], [pK.b])

                def b_ktev():
                    for hc in range(2):
                        cp("act", kTs[hc][0][:, RO:RO + 512], pK16[:, hc * 512:(hc + 1) * 512], [pK.b], [kTs[hc][1]])

                def b_S():
                    for hc in range(2):
                        mm(pS.t[:NQ, 0:512], qbd.t[:, hc, :], kTs[hc][0][:, RO:RO + 512], hc == 0, hc == 1, [qbd.b, kTs[hc][1]], [pS.b])

                def b_E():
                    act(E_s[:NQ, RO:RO + 512], pS.t[:NQ, 0:512], AF.Exp, [pS.b, sbias16.b], [Eb], bias=sbias16.t[:, 0:1], scale=0.125)

                def b_post():
                    if kb > 0:
                        gatherV(n, kb - 1)
                return ([b_kt, b_ktev, b_S, b_E]
                        + tail_steps(512, 128, 4, lambda c4: (Vb.t[:, 0, c4, :], [Vb.bs[0]]), False, kb == 0) + [b_post])

            for n in range(4):
                mset("pool", qbd.t[:, :, :].rearrange("p a b -> p (a b)"), 0.0, [qbd.b])
                for h in range(4):
                    pr, hc = (h % 2) * 64, h // 2
                    cp("pool", qbd.t[pr:pr + 64, hc, h * 4:(h + 1) * 4], s_qT.t[pr:pr + 64, hc, n * 4:(n + 1) * 4], [s_qT.b], [qbd.b])
                mset("dve", ncar.t[:NQ, 0:1], 0.0, [ncar.b])
                if NB > 0:
                    gatherK(n, NB - 1, (NB - 1) % 2)
                    gatherV(n, NB - 1)
                yield new_steps(n)
                for kb in range(NB - 1, -1, -1):
                    yield past_steps(n, kb)
                cp("act", o16.t[:NQ, 0:256], py.t[:NQ, 0, :], [py.b], [o16.b])
                for h in range(4):
                    dma("sp", s_yc.t[n * 4:(n + 1) * 4, h * 64:(h + 1) * 64], o16.t[h * 4:(h + 1) * 4, h * 64:(h + 1) * 64], [o16.b], [s_yc.b])
            mtm = wkb[0]
            s_merge_tm(0, s_yc, s_sgC, mtm)

            for n in range(4):
                mkf, mvf = wk[0], wk[1]
                dma("sp", mkf.t[:, 0:512].rearrange("p (c d) -> p c d", c=2), cmk[n, l].rearrange("(c p) d -> p c d", p=128), [], [mkf.b])
                dma("sp", mvf.t[:, 0:512].rearrange("p (c d) -> p c d", c=2), cmv[n, l].rearrange("(c p) d -> p c d", p=128), [], [mvf.b])
                mkb = wkb[1]
                cp("dve", mkb.t[:, 0:512], mkf.t[:, 0:512], [mkf.b], [mkb.b])
                for mc in range(2):
                    for hc in range(2):
                        tr(ptr.t[:, (hc * 2 + mc) * 128:(hc * 2 + mc + 1) * 128], mkb.t[:, mc * 256 + hc * 128:mc * 256 + (hc + 1) * 128],
                           identb.t[:, :], [mkb.b, identb.b], ptr.bs)
                mkTs = wkb[2]
                cp("act", mkTs.t[:, 0:512], ptr.t[:, 0:512], ptr.bs, [mkTs.b])
                mvAs = wkb[3]
                mset("dve", mvAs.t[:, 0:520], 1.0, [mvAs.b])
                cp("dve", mvAs.t[:, 0:520].rearrange("p (c h e) -> p c h e", c=2, h=4)[:, :, :, 0:64],
                   mvf.t[:, 0:512].rearrange("p (c h d) -> p c h d", c=2, h=4), [mvf.b], [mvAs.b])
                for h in range(4):
                    pr, hc = (h % 2) * 64, h // 2
                    S = pss[h % 2]
                    mm(S.t[:4, 0:256], s_mqT.t[pr:pr + 64, hc, n * 4:(n + 1) * 4], mkTs.t[pr:pr + 64, hc * 256:(hc + 1) * 256], True, True,
                       [s_mqT.b, mkTs.b], [S.b])
                    mx = sm[3]
                    p.op("dve", lambda e: e.reduce_max(out=mx.t[:4, h:h + 1], in_=S.t[:4, 0:256], axis=AX.X), [S.b], [mx.b])
                    ts("dve", mx.t[:4, h:h + 1], mx.t[:4, h:h + 1], -0.125, ALU.mult, [mx.b], [mx.b])
                    Pm = wkb[0]
                    act(Pm.t[:4, 512:768], S.t[:4, 0:256], AF.Exp, [S.b, mx.b], [Pm.b], bias=mx.t[:4, h:h + 1], scale=0.125)
                    for mc in range(2):
                        tr(ptr.t[:, 512 + mc * 4:512 + (mc + 1) * 4], Pm.t[:4, 512 + mc * 128:512 + (mc + 1) * 128], identb.t[:4, :4],
                           [Pm.b, identb.b], ptr.bs)
                    PT = wkb[0]
                    cp("act", PT.t[:, 768:776], ptr.t[:, 512:520], ptr.bs, [PT.b])
                    for mc in range(2):
                        mm(po2.t[:4, h * 65:(h + 1) * 65], PT.t[:, 768 + mc * 4:768 + (mc + 1) * 4],
                           mvAs.t[:, 0:520].rearrange("p (c h e) -> p c h e", c=2, h=4)[:, mc, h, :], mc == 0, mc == 1,
                           [PT.b, mvAs.b], [po2.b])
                ym4 = wk[5]
                rd = sm[4]
                for h in range(4):
                    p.op("dve", lambda e: e.reciprocal(out=rd.t[:4, h:h + 1], in_=po2.t[:4, h * 65 + 64:h * 65 + 65]), [po2.b], [rd.b])
                    ts("dve", ym4.t[:4, h * 64:(h + 1) * 64], po2.t[:4, h * 65:h * 65 + 64], rd.t[:4, h:h + 1], ALU.mult, [po2.b, rd.b], [ym4.b])
                dma("sp", s_ym.t[n * 4:(n + 1) * 4, :], ym4.t[:4, 0:256], [ym4.b], [s_ym.b])
            s_merge_tm(1, s_ym, s_sgM, mtm)
            for c in range(4):
                tr(ptr.t[:, c * NQ:(c + 1) * NQ], mtm.t[:NQ, c * 128:(c + 1) * 128], identb.t[:NQ, :NQ], [mtm.b, identb.b], ptr.bs)
            cp("act", s_mT.t[:, 4:8, :], ptr.t[:, 0:4 * NQ].rearrange("p (k m) -> p k m", k=4), ptr.bs, [s_mT.b])
            for oc in range(2):
                pm = pmm[oc]
                for kc in range(8):
                    mm(pm.t[:NQ, :], s_mT.t[:, kc, :], wout.t[:, kc, oc * 512:(oc + 1) * 512], kc == 0, kc == 7,
                       [s_mT.b, wout.bs[kc * 2 + oc]], [pm.b])
                tt("dve", xs_t.t[:, oc * 512:(oc + 1) * 512], xs_t.t[:, oc * 512:(oc + 1) * 512], pm.t[:NQ, :], ALU.add, [xs_t.b, pm.b], [xs_t.b])
            if l == DEPTH - 1:
                fg = wk[5]
                dma("sp", fg.t[:, :], fin_g.partition_broadcast(128), [], [fg.b])
                tt("dve", sq.t[:NQ, :], xs_t.t[:, :], xs_t.t[:, :], ALU.mult, [xs_t.b], [sq.b])
                p.op("dve", lambda e: e.reduce_sum(out=ss.t[:NQ, 0:1], in_=sq.t[:NQ, :], axis=AX.X), [sq.b], [ss.b])
                rstd_of(ss.t[:NQ, 0:1], ss.t[:NQ, 0:1], 1024.0, [ss.b], [ss.b])
                yo = wk[1]
                stt(yo.t[:NQ, :], xs_t.t[:, :], ss.t[:NQ, 0:1], fg.t[:NQ, :], ALU.mult, ALU.mult, [xs_t.b, ss.b, fg.b], [yo.b])
                dma("sp", y_s, yo.t[:NQ, :], [yo.b], [Buf()], is_out=True)

        def body():
          prep_mem()
          sample_setup()
          stage(1)
          for l in range(DEPTH):
              for kc in range(8):
                  for c3 in range(6):
                      dma("pool", win.t[:, kc, c3 * 512:(c3 + 1) * 512], w_in[l, kc * 128:(kc + 1) * 128, c3 * 512:(c3 + 1) * 512],
                          [], [win.bs[kc * 6 + c3]])
              for kc in range(8):
                  for c2 in range(2):
                      dma("pool", wout.t[:, kc, c2 * 512:(c2 + 1) * 512], w_out[l, kc * 128:(kc + 1) * 128, c2 * 512:(c2 + 1) * 512],
                          [], [wout.bs[kc * 2 + c2]])
              for k2 in range(2):
                  dma("pool", wglu.t[:, k2, :], w_glu[l, k2 * 128:(k2 + 1) * 128, :], [], [wglu.b])
              wdeps = set(b.lw for b in win.bs + wout.bs + [wglu.b])
              p._wait("dve", wdeps)
              p._wait("act", wdeps)
              dma("sp", gt.t[:, :], norm_g[l].partition_broadcast(128), [], [gt.b])
              def col(ap1d):
                  return ap1d.rearrange("(p o) -> p o", o=1)
              for g_ in range(2):
                  for h_ in range(2):
                      dma("sp", gnAB.t[:, g_ * 2 + h_:g_ * 2 + h_ + 1], col(gng[l, g_, h_ * 128:(h_ + 1) * 128]), [], [gnAB.b], slow=True)
              dma("sp", gnCM.t[:, :, :], gng[l, 2:4, :].partition_broadcast(128), [], [gnCM.b])
              for h_ in range(2):
                  dma("sp", dvec.t[:, h_:h_ + 1], col(ssm_d[l, h_ * 128:(h_ + 1) * 128]), [], [dvec.b], slow=True)
                  for j_ in range(3):
                      dma("sp", cw.t[:, h_, j_:j_ + 1], col(conv_w[l, j_, h_ * 128:(h_ + 1) * 128]), [], [cw.b], slow=True)
              dma("sp", sbb.t[:, :], sb_bias[l].partition_broadcast(128), [], [sbb.b])

              stage(2)
              lr, li, a1, th, er, cs, sn, tf = wk[0], wk[1], wk[2], wk[3], wk[4], wk[5], wk[6], wk[7]
              ldt = sm[0]
              dma("sp", lr.t[:, :], lam_re[l].rearrange("g p -> (g p)").partition_broadcast(128), [], [lr.b])
              dma("sp", li.t[:, :], lam_im[l].rearrange("g p -> (g p)").partition_broadcast(128), [], [li.b])
              dma("sp", ldt.t[:, :], log_dt[l].partition_broadcast(128), [], [ldt.b])
              act(ldt.t[:, :], ldt.t[:, :], AF.Exp, [ldt.b], [ldt.b])
              dtb = ldt.t[:, :].unsqueeze(2).to_broadcast([128, 16, 64])

              def v3(x):
                  return x.t[:, :].rearrange("p (g s) -> p g s", g=16)

              tt("dve", v3(a1), v3(lr), dtb, ALU.mult, [lr.b, ldt.b], [a1.b])
              tt("dve", v3(th), v3(li), dtb, ALU.mult, [li.b, ldt.b], [th.b])
              act(er.t[:, :], a1.t[:, :], AF.Exp, [a1.b], [er.b])
              sincos(th.t[:, :], th.b, 1024, sn.t[:, :], sn.b, cs.t[:, :], cs.b, tf, tmpi)
              tt("dve", cs.t[:, :], cs.t[:, :], er.t[:, :], ALU.mult, [cs.b, er.b], [cs.b])
              ts("dve", cs.t[:, :], cs.t[:, :], -1.0, ALU.add, [cs.b], [cs.b])
              tt("dve", sn.t[:, :], sn.t[:, :], er.t[:, :], ALU.mult, [sn.b, er.b], [sn.b])
              tt("dve", er.t[:, :], lr.t[:, :], lr.t[:, :], ALU.mult, [lr.b], [er.b])
              tt("dve", a1.t[:, :], li.t[:, :], li.t[:, :], ALU.mult, [li.b], [a1.b])
              tt("dve", er.t[:, :], er.t[:, :], a1.t[:, :], ALU.add, [er.b, a1.b], [er.b])
              p.op("dve", lambda e: e.reciprocal(out=er.t[:, :], in_=er.t[:, :]), [er.b], [er.b])
              tt("dve", a1.t[:, :], cs.t[:, :], lr.t[:, :], ALU.mult, [cs.b, lr.b], [a1.b])
              tt("dve", tf.t[:, :], sn.t[:, :], li.t[:, :], ALU.mult, [sn.b, li.b], [tf.b])
              tt("dve", a1.t[:, :], a1.t[:, :], tf.t[:, :], ALU.add, [a1.b, tf.b], [a1.b])
              tt("dve", a1.t[:, :], a1.t[:, :], er.t[:, :], ALU.mult, [a1.b, er.b], [a1.b])
              tt("dve", th.t[:, :], sn.t[:, :], lr.t[:, :], ALU.mult, [sn.b, lr.b], [th.b])
              tt("dve", tf.t[:, :], cs.t[:, :], li.t[:, :], ALU.mult, [cs.b, li.b], [tf.b])
              tt("dve", th.t[:, :], th.t[:, :], tf.t[:, :], ALU.subtract, [th.b, tf.b], [th.b])
              tt("dve", th.t[:, :], th.t[:, :], er.t[:, :], ALU.mult, [th.b, er.b], [th.b])
              cre, cim = a1, th
              BfR, BfI = wk[0], wk[1]
              mset("dve", BfR.t[:, :], 0.0, [BfR.b])
              mset("dve", BfI.t[:, :], 0.0, [BfI.b])
              for g in range(16):
                  r0 = (g % 8) * 16
                  c0 = (g // 2) * 128 + (g % 2) * 64
                  dma("sp", BfR.t[r0:r0 + 16, c0:c0 + 64], b_re[l, g].rearrange("p c -> c p"), [], [BfR.b], slow=True)
                  dma("sp", BfI.t[r0:r0 + 16, c0:c0 + 64], b_im[l, g].rearrange("p c -> c p"), [], [BfI.b], slow=True)
              t1, t2 = wk[4], wk[5]
              tt("dve", t1.t[:, :], cre.t[:, :], BfR.t[:, :], ALU.mult, [cre.b, BfR.b], [t1.b])
              tt("dve", t2.t[:, :], cim.t[:, :], BfI.t[:, :], ALU.mult, [cim.b, BfI.b], [t2.b])
              tt("dve", BtR.t[:, :, :].rearrange("p g s -> p (g s)"), t1.t[:, :], t2.t[:, :], ALU.subtract,
                 [t1.b, t2.b], [BtR.b])
              tt("dve", t1.t[:, :], cre.t[:, :], BfI.t[:, :], ALU.mult, [cre.b, BfI.b], [t1.b])
              tt("dve", t2.t[:, :], cim.t[:, :], BfR.t[:, :], ALU.mult, [cim.b, BfR.b], [t2.b])
              tt("dve", BtI.t[:, :, :].rearrange("p g s -> p (g s)"), t1.t[:, :], t2.t[:, :], ALU.add,
                 [t1.b, t2.b], [BtI.b])
              CfR, CfI = wk[6], wk[7]
              mset("dve", CfR.t[:, :], 0.0, [CfR.b])
              mset("dve", CfI.t[:, :], 0.0, [CfI.b])
              for g in range(16):
                  gp, gi = g // 2, g % 2
                  c0 = gp * 128 + (gp % 4) * 32 + gi * 16
                  dma("sp", CfR.t[gi * 64:gi * 64 + 64, c0:c0 + 16], c_re[l, g].rearrange("c p -> p c"), [], [CfR.b], slow=True)
                  dma("sp", CfI.t[gi * 64:gi * 64 + 64, c0:c0 + 16], c_im[l, g].rearrange("c p -> p c"), [], [CfI.b], slow=True)
              cp("dve", CtR.t[:, :, :].rearrange("p g s -> p (g s)"), CfR.t[:, :], [CfR.b], [CtR.b])
              ts("dve", CtI.t[:, :, :].rearrange("p g s -> p (g s)"), CfI.t[:, :], -1.0, ALU.mult, [CfI.b], [CtI.b])
              lrs, lis, dts, ths = sm[1], sm[2], sm[3], sm[4]
              for gi in range(2):
                  dma("sp", lrs.t[gi * 64:(gi + 1) * 64, 0:8], lam_re[l].rearrange("(gp gi) p -> gi p gp", gi=2)[gi], [], [lrs.b], slow=True)
                  dma("sp", lis.t[gi * 64:(gi + 1) * 64, 0:8], lam_im[l].rearrange("(gp gi) p -> gi p gp", gi=2)[gi], [], [lis.b], slow=True)
              ldt2 = log_dt[l].rearrange("(gp gi) -> gi gp", gi=2)
              for gi in range(2):
                  dma("sp", dts.t[gi * 64:(gi + 1) * 64, 0:8], ldt2[gi].partition_broadcast(64), [], [dts.b], slow=True)
              act(dts.t[:, 0:8], dts.t[:, 0:8], AF.Exp, [dts.b], [dts.b])
              tt("dve", lrs.t[:, 0:8], lrs.t[:, 0:8], dts.t[:, 0:8], ALU.mult, [lrs.b, dts.b], [lrs.b])
              act(rsp.t[:, :], lrs.t[:, 0:8], AF.Exp, [lrs.b], [rsp.b])
              tt("dve", ths.t[:, 0:8], lis.t[:, 0:8], dts.t[:, 0:8], ALU.mult, [lis.b, dts.b], [ths.b])
              tpos_i = tmpi
              p.op("pool", lambda e: e.iota(tpos_i.t[:, 0:TT], pattern=[[1, TT]], base=1, channel_multiplier=0),
                   [], [tpos_i.b])
              tpos = wk[4]
              cp("dve", tpos.t[:, 0:TT], tpos_i.t[:, 0:TT], [tpos_i.b], [tpos.b])
              ang = wk[5]
              for half in range(2):
                  for g4 in range(4):
                      gp = half * 4 + g4
                      ts("dve", ang.t[:, g4 * TT:(g4 + 1) * TT], tpos.t[:, 0:TT], ths.t[:, gp:gp + 1], ALU.mult,
                         [tpos.b, ths.b], [ang.b])
                  sincos(ang.t[:, 0:4 * TT], ang.b, 4 * TT,
                         sinT.t[:, half * 4:(half + 1) * 4, :].rearrange("p g t -> p (g t)"), sinT.b,
                         cosT.t[:, half * 4:(half + 1) * 4, :].rearrange("p g t -> p (g t)"), cosT.b, wk[6], tmpi)

              stage(3)
              wmem = wkb[0]
              wm = [wkb[0], wkb[1], wkb[2], wkb[3]]
              for kc in range(8):
                  dma("pool", wm[kc // 2].t[:, (kc % 2) * 512:(kc % 2 + 1) * 512], w_mem[l, kc * 128:(kc + 1) * 128, :],
                      [], [wm[kc // 2].b])

              def wmk(kc, c0, c1):
                  return wm[kc // 2].t[:, (kc % 2) * 512 + c0:(kc % 2) * 512 + c1]

              mset("dve", mvA.t[:, :, :, :].rearrange("p a b c -> p (a b c)"), 1.0, [mvA.b])
              for mc in range(2):
                  pm = pmm[mc]
                  for kc in range(8):
                      mm(pm.t[:, :], memT.t[:, kc, mc * 128:(mc + 1) * 128], wmk(kc, 0, 512), kc == 0, kc == 7,
                         [memT.b, wm[kc // 2].b], [pm.b])
                  kvf = wk[mc]
                  cp("act", kvf.t[:, 0:512], pm.t[:, :], [pm.b], [kvf.b])
                  dma("sp", memk_p[l, mc * 128:(mc + 1) * 128, :], kvf.t[:, 0:256], [kvf.b], [Buf()], is_out=True)
                  dma("sp", memv_p[l, mc * 128:(mc + 1) * 128, :], kvf.t[:, 256:512], [kvf.b], [Buf()], is_out=True)
                  cp("dve", mvA.t[:, mc, :, 0:64], kvf.t[:, 256:512].rearrange("p (h d) -> p h d", h=4), [kvf.b], [mvA.b])
              for c in range(2):
                  pm = pmm[c]
                  for kc in range(8):
                      mm(pm.t[:, 0:256], wmk(kc, c * 128, (c + 1) * 128), memT.t[:, kc, :], kc == 0, kc == 7,
                         [memT.b, wm[kc // 2].b], [pm.b])
                  cp("act", mkT.t[:, c, :], pm.t[:, 0:256], [pm.b], [mkT.b])

              stage(4)
              sgen = sample_layer(l) if DO_SAMPLE else iter(())
              mset("dve", hst.t[:, :, :].rearrange("p a b -> p (a b)"), 0.0, [hst.b])
              mset("dve", vbuf.t[:, :, 0:2], 0.0, [vbuf.b])

              for t in range(NT):
                  tok0 = t * TT
                  src = xp if l == 0 else x1
                  dma("sp", xt.t[:, :, :], src[tok0:tok0 + TT, :].rearrange("(s p) d -> p s d", p=128),
                      [x1bufs[t]] if l == 1 else [], [xt.b])
                  ss = sm[0]
                  for s in range(NS):
                      tt("dve", wk[0].t[:, :], xt.t[:, s, :], xt.t[:, s, :], ALU.mult, [xt.b], [wk[0].b])
                      p.op("dve", lambda e: e.reduce_sum(out=ss.t[:, s:s + 1], in_=wk[0].t[:, :], axis=AX.X),
                           [wk[0].b], [ss.b])
                  rstd_of(ss.t[:, 0:NS], ss.t[:, 0:NS], 1024.0, [ss.b], [ss.b])
                  for s in range(NS):
                      xnb = wkb[s]
                      stt(xnb.t[:, :], xt.t[:, s, :], ss.t[:, s:s + 1], gt.t[:, :], ALU.mult, ALU.mult,
                          [xt.b, ss.b, gt.b], [xnb.b])
                      for kc in range(8):
                          tr(ptr.t[:, kc * 128:(kc + 1) * 128], xnb.t[:, kc * 128:(kc + 1) * 128], identb.t[:],
                             [xnb.b, identb.b], ptr.bs)
                      cp("act", xnT.t[:, :, s * 128:(s + 1) * 128], ptr.t[:, :].rearrange("p (k m) -> p k m", k=8),
                         ptr.bs, [xnT.b])

                  stage(5)
                  def proj_fm(j, evac):
                      pm = pmm[j % 2]
                      for kc in range(8):
                          mm(pm.t[:, 0:TT], win.t[:, kc, j * 128:(j + 1) * 128], xnT.t[:, kc, :], kc == 0, kc == 7,
                             wb(kc, j * 128, (j + 1) * 128) + [xnT.b], [pm.b])
                          stage(5.01 + kc * 0.001)
                      stage(5.05)
                      evac(pm)

                  for hf in range(2):
                      def ev(pm, hf=hf):
                          cp("act", au_f.t[:, hf, :], pm.t[:, 0:TT], [pm.b], [au_f.b])
                          stage(5.06)
                          cp("act", au_b.t[:, hf, :], pm.t[:, 0:TT], [pm.b], [au_b.b])
                      proj_fm(0 + hf, ev)
                      stage(5.1)
                      proj_fm(2 + hf, lambda pm, hf=hf: act(sgA.t[:, hf, :], pm.t[:, 0:TT], AF.Silu, [pm.b], [sgA.b]))
                      stage(5.2)
                      for bi in range(3):
                          proj_fm(4 + 2 * bi + hf, lambda pm, hf=hf, bi=bi: cp("act", bbx.t[:, 2 * bi + hf, :], pm.t[:, 0:TT], [pm.b], [bbx.b]))
                      proj_fm(10 + hf, lambda pm, hf=hf: act(sgB.t[:, hf, :], pm.t[:, 0:TT], AF.Silu, [pm.b], [sgB.b]))
                      proj_fm(12 + hf, lambda pm, hf=hf: cp("act", qT.t[:, hf, :], pm.t[:, 0:TT], [pm.b], [qT.b]))

                      def evk(pm, hf=hf):
                          for s in range(NS):
                              cp("act", KcT.t[:, hf, tok0 + s * 128:tok0 + (s + 1) * 128], pm.t[:, s * 128:(s + 1) * 128],
                                 [pm.b], [KcT.bs[t * NS + s]])
                      proj_fm(14 + hf, evk)
                      proj_fm(20 + hf, lambda pm, hf=hf: cp("act", mqT.t[:, hf, :], pm.t[:, 0:TT], [pm.b], [mqT.b]))
                  stage(5.3)
                  for s in range(NS):
                      gs = t * NS + s
                      pm = pmm[s]
                      for kc in range(8):
                          mm(pm.t[:, :], xnT.t[:, kc, s * 128:(s + 1) * 128], win.t[:, kc, 1792:2304], kc == 0, kc == 7,
                             wb(kc, 1792, 2304) + [xnT.b], [pm.b])
                      kvf = wk[1 + s]
                      cp("act", kvf.t[:, 0:512], pm.t[:, :], [pm.b], [kvf.b])
                      r0 = tok0 + s * 128
                      dma("sp", sbk_p[l, r0:r0 + 128, :], kvf.t[:, 0:256], [kvf.b], [Buf()], is_out=True)
                      dma("sp", sbv_p[l, r0:r0 + 128, :], kvf.t[:, 256:512], [kvf.b], [Buf()], is_out=True)
                      cp("dve", Vc.t[:, gs, :], kvf.t[:, 256:512], [kvf.b], [Vc.bs[gs]])
                      stage(5.4)
                      for (c0, dst) in ((2304, sgC), (2816, sgM)):
                          pm2 = pmm[1 - s]
                          for kc in range(8):
                              mm(pm2.t[:, 0:256], xnT.t[:, kc, s * 128:(s + 1) * 128], win.t[:, kc, c0:c0 + 256],
                                 kc == 0, kc == 7, wb(kc, c0, c0 + 256) + [xnT.b], [pm2.b])
                          act(dst.t[:, s, :], pm2.t[:, 0:256], AF.Silu, [pm2.b], [dst.b])

                  stage(6)
                  def merge_fm(br, yv, yb_, sg):
                      sq = wk[7]
                      pm = pmm[0]
                      for hf in range(2):
                          tt("dve", sq.t[:, hf * TT:(hf + 1) * TT], yv(hf), yv(hf), ALU.mult, [yb_], [sq.b])
                      for hf in range(2):
                          mm(pm.t[:, 0:TT], onesf.t[:, :], sq.t[:, hf * TT:(hf + 1) * TT], hf == 0, hf == 1,
                             [onesf.b, sq.b], [pm.b])
                      rs = wk[6]
                      rstd_of(rs.t[:, 0:TT], pm.t[:, 0:TT], 256.0, [pm.b], [rs.b])
                      for hf in range(2):
                          stt(sq.t[:, hf * TT:(hf + 1) * TT], yv(hf), gnAB.t[:, br * 2 + hf:br * 2 + hf + 1], rs.t[:, 0:TT],
                              ALU.mult, ALU.mult, [yb_, gnAB.b, rs.b], [sq.b])
                          tt("dve", mergedT.t[:, br * 2 + hf, :], sq.t[:, hf * TT:(hf + 1) * TT], sg.t[:, hf, :], ALU.mult,
                             [sq.b, sg.b], [mergedT.bs[br * 2 + hf]])

                  yb = wk[3]
                  for hf in range(2):
                      tt("dve", vbuf.t[:, hf, 2:TT + 2], bbx.t[:, 2 + hf, :], bbx.t[:, 4 + hf, :], ALU.mult, [bbx.b], [vbuf.b])
                      acc = yb.t[:, hf * TT:(hf + 1) * TT]
                      ts("dve", acc, vbuf.t[:, hf, 2:TT + 2], cw.t[:, hf, 2:3], ALU.mult, [vbuf.b, cw.b], [yb.b])
                      stt(acc, vbuf.t[:, hf, 1:TT + 1], cw.t[:, hf, 1:2], acc, ALU.mult, ALU.add, [vbuf.b, cw.b, yb.b], [yb.b])
                      stt(acc, vbuf.t[:, hf, 0:TT], cw.t[:, hf, 0:1], acc, ALU.mult, ALU.add, [vbuf.b, cw.b, yb.b], [yb.b])
                      tt("dve", acc, acc, bbx.t[:, 0 + hf, :], ALU.mult, [yb.b, bbx.b], [yb.b])
                  if t == NT - 1:
                      for h_ in range(2):
                          for j_ in range(2):
                              dma("sp", col(conv_p[l, j_, h_ * 128:(h_ + 1) * 128]), vbuf.t[:, h_, TT + j_:TT + j_ + 1], [vbuf.b], [Buf()],
                                  slow=True, is_out=True)
                  for hf in range(2):
                      cp("dve", vbuf.t[:, hf, 0:2], vbuf.t[:, hf, TT:TT + 2], [vbuf.b], [vbuf.b])
                  merge_fm(1, lambda hf: yb.t[:, hf * TT:(hf + 1) * TT], yb.b, sgB)

                  stage(7)
                  ya = wk[3]
                  yab = wkb[2]
                  F4 = 4 * TT

                  def ssm_half_steps(hf):
                      if hf == 0:
                          pre, pim, wa, wb_, g_, hb = pss[0], pss[1], wk[0], wk[1], wk[2], wkb[3]
                      else:
                          pre, pim, wa, wb_, g_, hb = pmm[0], pmm[1], wk[4], wk[5], wk[6], wkb[1]
                      c_ = cosT.t[:, hf * 4:(hf + 1) * 4, :].rearrange("p g t -> p (g t)")
                      s_ = sinT.t[:, hf * 4:(hf + 1) * 4, :].rearrange("p g t -> p (g t)")
                      q1, q3 = wa.t[:, 0:F4], wa.t[:, F4:2 * F4]
                      q2, q4 = wb_.t[:, 0:F4], wb_.t[:, F4:2 * F4]
                      gre, gim = g_.t[:, 0:F4], g_.t[:, F4:2 * F4]
                      hR, hI = hb.t[:, 0:F4], hb.t[:, F4:2 * F4]
                      yreg = po.t[:, 256 + hf * TT:256 + (hf + 1) * TT]

                      def d_bu():
                          for g4 in range(4):
                              gp = hf * 4 + g4
                              mm(pre.t[:, g4 * TT:(g4 + 1) * TT], BtR.t[:, gp, :], au_b.t[:, hf, :], True, True, [BtR.b, au_b.b], [pre.b])
                              mm(pim.t[:, g4 * TT:(g4 + 1) * TT], BtI.t[:, gp, :], au_b.t[:, hf, :], True, True, [BtI.b, au_b.b], [pim.b])

                      def d_f1():
                          tt("dve", q1, pre.t[:, 0:F4], c_, ALU.mult, [pre.b, cosT.b], [wa.b])
                          tt("dve", q2, pim.t[:, 0:F4], s_, ALU.mult, [pim.b, sinT.b], [wb_.b])

                      def d_f2():
                          tt("dve", q3, pim.t[:, 0:F4], c_, ALU.mult, [pim.b, cosT.b], [wa.b])
                          tt("dve", q4, pre.t[:, 0:F4], s_, ALU.mult, [pre.b, sinT.b], [wb_.b])

                      def d_f3():
                          tt("dve", q1, q1, q2, ALU.add, [wa.b, wb_.b], [wa.b])
                          tt("dve", q3, q3, q4, ALU.subtract, [wa.b, wb_.b], [wa.b])

                      def d_scan():
                          for g4 in range(4):
                              gp = hf * 4 + g4
                              rb = rsp.t[:, gp:gp + 1].to_broadcast([128, TT])
                              sl = slice(g4 * TT, (g4 + 1) * TT)
                              scan(gre[:, sl], rb, q1[:, sl], hst.t[:, gp, 0:1], [wa.b, rsp.b, hst.b], [g_.b])
                              scan(gim[:, sl], rb, q3[:, sl], hst.t[:, gp, 1:2], [wa.b, rsp.b, hst.b], [g_.b])

                      def d_b1():
                          tt("dve", q1, gre, c_, ALU.mult, [g_.b, cosT.b], [wa.b])
                          tt("pool", q2, gim, s_, ALU.mult, [g_.b, sinT.b], [wb_.b])
                          tt("dve", q3, gim, c_, ALU.mult, [g_.b, cosT.b], [wa.b])
                          tt("pool", q4, gre, s_, ALU.mult, [g_.b, sinT.b], [wb_.b])

                      def lastc(q):
                          return q.rearrange("p (g t) -> p g t", g=4)[:, :, TT - 1]

                      def d_b2():
                          tt("dve", hR, q1, q2, ALU.subtract, [wa.b, wb_.b], [hb.b])
                          tt("dve", hI, q3, q4, ALU.add, [wa.b, wb_.b], [hb.b])
                          tt("dve", hst.t[:, hf * 4:(hf + 1) * 4, 0], lastc(q1), lastc(q2), ALU.subtract, [wa.b, wb_.b], [hst.b])
                          tt("dve", hst.t[:, hf * 4:(hf + 1) * 4, 1], lastc(q3), lastc(q4), ALU.add, [wa.b, wb_.b], [hst.b])

                      def d_y():
                          for g4 in range(4):
                              gp = hf * 4 + g4
                              sl = slice(g4 * TT, (g4 + 1) * TT)
                              mm(yreg, CtR.t[:, gp, :], hR[:, sl], g4 == 0, False, [CtR.b, hb.b], [po.b])
                              mm(yreg, CtI.t[:, gp, :], hI[:, sl], False, g4 == 3, [CtI.b, hb.b], [po.b])

                      def d_out():
                          yh = ya.t[:, hf * TT:(hf + 1) * TT]
                          stt(yh, au_f.t[:, hf, :], dvec.t[:, hf:hf + 1], yreg, ALU.mult, ALU.add, [au_f.b, dvec.b, po.b], [ya.b])
                          act(yh, yh, AF.Gelu, [ya.b], [ya.b])
                          cp("dve", yab.t[:, hf * TT:(hf + 1) * TT], yh, [ya.b], [yab.b])
                      return [d_bu, d_f1, d_f2, d_f3, d_scan, d_b1, d_b2, d_y, d_out]

                  for fa_, fb_ in zip(ssm_half_steps(0), ssm_half_steps(1)):
                      fa_()
                      fb_()
                  if t == NT - 1:
                      for gi in range(2):
                          dma("sp", ssmre_p[l].rearrange("(gp gi) p -> gi p gp", gi=2)[gi], hst.t[gi * 64:(gi + 1) * 64, :, 0], [hst.b], [Buf()],
                              slow=True, is_out=True)
                          dma("sp", ssmim_p[l].rearrange("(gp gi) p -> gi p gp", gi=2)[gi], hst.t[gi * 64:(gi + 1) * 64, :, 1], [hst.b], [Buf()],
                              slow=True, is_out=True)
                  for oc in range(2):
                      pm = pmm[oc]
                      for k2 in range(2):
                          mm(pm.t[:, 0:TT], wglu.t[:, k2, oc * 128:(oc + 1) * 128], yab.t[:, k2 * TT:(k2 + 1) * TT],
                             k2 == 0, k2 == 1, [wglu.b, yab.b], [pm.b])
                      sg_ = wk[5]
                      act(sg_.t[:, 0:TT], pm.t[:, 0:TT], AF.Sigmoid, [pm.b], [sg_.b])
                      tt("dve", ya.t[:, oc * TT:(oc + 1) * TT], ya.t[:, oc * TT:(oc + 1) * TT], sg_.t[:, 0:TT], ALU.mult,
                         [ya.b, sg_.b], [ya.b])
                  merge_fm(0, lambda hf: ya.t[:, hf * TT:(hf + 1) * TT], ya.b, sgA)

                  stage(8)
                  def merge_tm(idx, yv, yb_, sg, s, mtm):
                      sq = wk[7]
                      ssq = sm[1]
                      tt("dve", sq.t[:, 0:256], yv, yv, ALU.mult, [yb_], [sq.b])
                      p.op("dve", lambda e: e.reduce_sum(out=ssq.t[:, 0:1], in_=sq.t[:, 0:256], axis=AX.X), [sq.b], [ssq.b])
                      rstd_of(ssq.t[:, 0:1], ssq.t[:, 0:1], 256.0, [ssq.b], [ssq.b])
                      stt(sq.t[:, 0:256], yv, ssq.t[:, 0:1], gnCM.t[:, idx, :], ALU.mult, ALU.mult, [yb_, ssq.b, gnCM.b], [sq.b])
                      tt("dve", mtm.t[:, idx * 256:(idx + 1) * 256], sq.t[:, 0:256], sg.t[:, s, :], ALU.mult, [sq.b, sg.b], [mtm.b])

                  for s in range(NS):
                      gq = t * NS + s
                      nkeys = (gq + 1) * 128
                      nblk = (nkeys + 511) // 512
                      mtm = wkb[0]
                      ncars = [sm[2], sm[5]]
                      for nc_ in ncars:
                          mset("dve", nc_.t[:, 0:4], 0.0, [nc_.b])
                      Wbs, WTs = [wkb[1], wkb[3]], [wkb[2], wkb[0]]
                      first = True
                      for kb in range(nblk - 1, -1, -1):
                          ncol = nkeys - kb * 512 if kb == nblk - 1 else 512
                          nc4 = ncol // 128
                          kbufs = [KcT.bs[kb * 4 + c4] for c4 in range(nc4)]

                          def head_steps(h, par, kb=kb, ncol=ncol, nc4=nc4, kbufs=kbufs, first=first):
                              pr, hc = (h % 2) * 64, h // 2
                              S, E, Lb, P_ = pss[par], wk[0 + par], wk[2 + par], wk[4 + par]
                              ncar, Wb, WT = ncars[par], Wbs[par], WTs[par]
                              Eb_ = wk[0].bl if par == 0 else wk[1].b
                              Lbb_ = wk[2].bl if par == 0 else wk[3].b
                              Wbb_, WTb_ = Wb.bl, WT.bl
                              trp = ptr.t[:, 0:512] if par == 0 else po2.t[:, :].bitcast(BF16)[:, 0:512]
                              trb = ptr.bs if par == 0 else [po2.b]

                              def s_S():
                                  mm(S.t[:, 0:ncol], qT.t[pr:pr + 64, hc, s * 128:(s + 1) * 128],
                                     KcT.t[pr:pr + 64, hc, kb * 512:kb * 512 + ncol], True, True, [qT.b] + kbufs, [S.b])

                              def s_E():
                                  act(E.t[:, 0:ncol], S.t[:, 0:ncol], AF.Exp, [S.b, sbb.b], [Eb_], bias=sbb.t[:, h:h + 1], scale=0.125)
                                  if kb == nblk - 1:
                                      p.op("pool", lambda e: e.affine_select(out=E.t[:, ncol - 128:ncol], in_=E.t[:, ncol - 128:ncol],
                                                                             pattern=[[-1, 128]], compare_op=ALU.is_gt, fill=0.0,
                                                                             base=0, channel_multiplier=1), [Eb_], [Eb_])
                                  mset("dve", Lb.t[:, 0:1], 0.0, [Lbb_])

                              def s_L():
                                  act(Lb.t[:, 1:ncol + 1], E.t[:, 0:ncol], AF.Ln, [Eb_], [Lbb_], bias=1.0)

                              def s_scan():
                                  scan(P_.t[:, 0:ncol + 1], onec.t[:, 0:1].to_broadcast([128, ncol + 1]), Lb.t[:, 0:ncol + 1], 0.0,
                                       [onec.b, Lbb_], [P_.b])
                                  tt("dve", ncar.t[:, h:h + 1], ncar.t[:, h:h + 1], P_.t[:, ncol:ncol + 1], ALU.subtract,
                                     [ncar.b, P_.b], [ncar.b])

                              def s_X():
                                  act(P_.t[:, 0:ncol], P_.t[:, 0:ncol], AF.Exp, [P_.b, ncar.b], [P_.b], bias=ncar.t[:, h:h + 1])

                              def s_W():
                                  tt("dve", Wb.t[:, 0:ncol], E.t[:, 0:ncol], P_.t[:, 0:ncol], ALU.mult, [Eb_, P_.b], [Wbb_])

                              def s_tr():
                                  for c4 in range(nc4):
                                      tr(trp[:, c4 * 128:(c4 + 1) * 128], Wb.t[:, c4 * 128:(c4 + 1) * 128], identb.t[:],
                                         [Wbb_, identb.b], trb)

                              def s_ev():
                                  cp("act", WT.t[:, 0:ncol], trp[:, 0:ncol], trb, [WTb_])

                              def s_pv():
                                  for c4 in range(nc4):
                                      mm(po.t[:, h * 64:(h + 1) * 64], WT.t[:, c4 * 128:(c4 + 1) * 128],
                                         Vc.t[:, kb * 4 + c4, h * 64:(h + 1) * 64], first and c4 == 0 and h == 0 and par == 0, kb == 0 and c4 == nc4 - 1,
                                         [WTb_, Vc.bs[kb * 4 + c4]], [po.b])
                              return [s_S, s_E, s_L, s_scan, s_X, s_W, s_tr, s_ev, s_pv]

                          ssteps = next(sgen, None) or []
                          for hp in range(2):
                              sa, sb_ = head_steps(hp, 0), head_steps(hp + 2, 1)
                              third = ssteps if hp == 0 else []
                              for fs in zip_longest(third, sa, sb_):
                                  for f_ in fs:
                                      if f_ is not None:
                                          f_()
                          first = False
                      yc = wk[6]
                      cp("act", yc.t[:, 0:256], po.t[:, 0:256], [po.b], [yc.b])
                      merge_tm(0, yc.t[:, 0:256], yc.b, sgC, s, mtm)
                      stage(9)
                      for h in range(4):
                          pr, hc = (h % 2) * 64, h // 2
                          S = pss[h % 2]
                          mm(S.t[:, 0:256], mqT.t[pr:pr + 64, hc, s * 128:(s + 1) * 128], mkT.t[pr:pr + 64, hc, :], True, True,
                             [mqT.b, mkT.b], [S.b])
                          mx = sm[3]
                          p.op("dve", lambda e: e.reduce_max(out=mx.t[:, h:h + 1], in_=S.t[:, 0:256], axis=AX.X), [S.b], [mx.b])
                          ts("dve", mx.t[:, h:h + 1], mx.t[:, h:h + 1], -0.125, ALU.mult, [mx.b], [mx.b])
                          Pm = wkb[1]
                          act(Pm.t[:, 0:256], S.t[:, 0:256], AF.Exp, [S.b, mx.b], [Pm.b], bias=mx.t[:, h:h + 1], scale=0.125)
                          for mc in range(2):
                              tr(ptr.t[:, mc * 128:(mc + 1) * 128], Pm.t[:, mc * 128:(mc + 1) * 128], identb.t[:],
                                 [Pm.b, identb.b], ptr.bs)
                          PT = wkb[2]
                          cp("act", PT.t[:, 0:256], ptr.t[:, 0:256], ptr.bs, [PT.b])
                          for mc in range(2):
                              mm(po2.t[:, h * 65:(h + 1) * 65], PT.t[:, mc * 128:(mc + 1) * 128], mvA.t[:, mc, h, :],
                                 mc == 0, mc == 1, [PT.b, mvA.b], [po2.b])
                      ym = wk[6]
                      rd = sm[4]
                      for h in range(4):
                          p.op("dve", lambda e: e.reciprocal(out=rd.t[:, h:h + 1], in_=po2.t[:, h * 65 + 64:h * 65 + 65]),
                               [po2.b], [rd.b])
                          ts("dve", ym.t[:, 256 + h * 64:256 + (h + 1) * 64], po2.t[:, h * 65:h * 65 + 64], rd.t[:, h:h + 1],
                             ALU.mult, [po2.b, rd.b], [ym.b])
                      merge_tm(1, ym.t[:, 256:512], ym.b, sgM, s, mtm)
                      stage(10)
                      for c in range(4):
                          tr(ptr.t[:, c * 128:(c + 1) * 128], mtm.t[:, c * 128:(c + 1) * 128], identb.t[:], [mtm.b, identb.b], ptr.bs)
                      for c in range(4):
                          cp("act", mergedT.t[:, 4 + c, s * 128:(s + 1) * 128], ptr.t[:, c * 128:(c + 1) * 128], ptr.bs,
                             [mergedT.bs[4 + c]])

                  stage(11)
                  for s in range(NS):
                      for oc in range(2):
                          pm = pmm[oc]
                          for kc in range(8):
                              mm(pm.t[:, :], mergedT.t[:, kc, s * 128:(s + 1) * 128], wout.t[:, kc, oc * 512:(oc + 1) * 512],
                                 kc == 0, kc == 7, [mergedT.bs[kc], wout.bs[kc * 2 + oc]], [pm.b])
                          tt("dve", xt.t[:, s, oc * 512:(oc + 1) * 512], xt.t[:, s, oc * 512:(oc + 1) * 512], pm.t[:, :], ALU.add,
                             [xt.b, pm.b], [xt.b])
                  if l == 0:
                      dma("sp", x1[tok0:tok0 + TT, :].rearrange("(s p) d -> p s d", p=128), xt.t[:, :, :], [xt.b], [x1bufs[t]])
                  else:
                      fg = wk[5]
                      dma("sp", fg.t[:, :], fin_g.partition_broadcast(128), [], [fg.b])
                      ss = sm[0]
                      for s in range(NS):
                          tt("dve", wk[0].t[:, :], xt.t[:, s, :], xt.t[:, s, :], ALU.mult, [xt.b], [wk[0].b])
                          p.op("dve", lambda e: e.reduce_sum(out=ss.t[:, s:s + 1], in_=wk[0].t[:, :], axis=AX.X),
                               [wk[0].b], [ss.b])
                      rstd_of(ss.t[:, 0:NS], ss.t[:, 0:NS], 1024.0, [ss.b], [ss.b])
                      for s in range(NS):
                          yo = wk[1 + s]
                          stt(yo.t[:, :], xt.t[:, s, :], ss.t[:, s:s + 1], fg.t[:, :], ALU.mult, ALU.mult, [xt.b, ss.b, fg.b], [yo.b])
                          r0 = tok0 + s * 128
                          dma("sp", y_p[r0:r0 + 128, :], yo.t[:, :], [yo.b], [Buf()], is_out=True)
              stage(20)
              for st_ in sgen:
                  for f_ in (st_ or []):
                      f_()


        try:
            body()
        except _Stop:
            pass
        p._wait("sp", set(p.out_tks))
    return nc


def kernel(**inputs):
    f32 = np.float32
    xpr = np.asarray(inputs["x_prompt"], f32)
    B, L, _ = xpr.shape
    NPH = inputs["cache_sb_k"].shape[0]
    NPG = inputs["page_table"].shape[1]
    nc = bass.Bass("TRN2", target_bir_lowering=False)
    build(nc, L, NPG, NPH)
    wnames = ["norm_g", "w_in", "w_out", "group_norm_g", "ssm_lambda_re", "ssm_lambda_im", "ssm_b_re", "ssm_b_im",
              "ssm_c_re", "ssm_c_im", "ssm_log_dt", "ssm_d", "ssm_w_glu", "conv_w", "sb_bias", "w_mem_kv", "final_norm_g"]
    W = {n: np.ascontiguousarray(np.asarray(inputs[n], f32)) for n in wnames}
    ck = np.ascontiguousarray(np.asarray(inputs["cache_sb_k"], f32)).reshape(-1, 256)
    cv = np.ascontiguousarray(np.asarray(inputs["cache_sb_v"], f32)).reshape(-1, 256)
    xsm = np.asarray(inputs["x_sample"], f32)
    in_maps = []
    for c in range(8):
        sl = slice(4 * c, 4 * c + 4)
        m = dict(W)
        m["xp"] = np.ascontiguousarray(xpr[c])
        m["memp"] = np.ascontiguousarray(np.asarray(inputs["mem_prompt"], f32)[c])
        m["xs"] = np.ascontiguousarray(xsm[sl]).reshape(16, D)
        m["ck"] = ck
        m["cv"] = cv
        m["sre"] = np.ascontiguousarray(np.asarray(inputs["state_ssm_re"], f32)[sl])
        m["sim"] = np.ascontiguousarray(np.asarray(inputs["state_ssm_im"], f32)[sl])
        m["sconv"] = np.ascontiguousarray(np.asarray(inputs["state_conv"], f32)[sl])
        m["cmk"] = np.ascontiguousarray(np.asarray(inputs["cache_mem_k"], f32)[sl]).reshape(4, DEPTH, 256, 256)
        m["cmv"] = np.ascontiguousarray(np.asarray(inputs["cache_mem_v"], f32)[sl]).reshape(4, DEPTH, 256, 256)
        m["ptab"] = np.ascontiguousarray(np.asarray(inputs["page_table"], np.int32)[sl]).reshape(-1)
        in_maps.append(m)
    res = run_bass_kernel_spmd(nc, in_maps, core_ids=list(range(8))).results

    def cat(k, shp):
        return np.ascontiguousarray(np.stack([np.asarray(r[k], f32) for r in res])).reshape(shp)

    return (cat("y_p", (8, L, D)), cat("y_s", (32, 4, D)),
            cat("sbk_p", (8, DEPTH, L, 4, 64)), cat("sbv_p", (8, DEPTH, L, 4, 64)),
            cat("ssmre_p", (8, DEPTH, 16, 64)), cat("ssmim_p", (8, DEPTH, 16, 64)), cat("conv_p", (8, DEPTH, 2, 256)),
            cat("memk_p", (8, DEPTH, 256, 4, 64)), cat("memv_p", (8, DEPTH, 256, 4, 64)),
            cat("sbk_s", (32, DEPTH, 4, 4, 64)), cat("sbv_s", (32, DEPTH, 4, 4, 64)),
            cat("ssmre_s", (32, DEPTH, 16, 64)), cat("ssmim_s", (32, DEPTH, 16, 64)), cat("conv_s", (32, DEPTH, 2, 256)))
```
